# Optimizing a Trainium2 kernel written in Bass

```python
import math
import jax
import jax.numpy as jnp
from jax import lax
import numpy as np

D_MODEL = 1024
BATCH = 4
SEQ = 4096
DEPTH = 1

CTX_LEN = 256
GRID_W = 64
HEAD_DIM = 128
N_HEADS_GDN = 4
N_HEADS_RET = 4
W_GDN = N_HEADS_GDN * HEAD_DIM
W_RET = N_HEADS_RET * HEAD_DIM
MIX_WIDTH = W_GDN + W_RET
IN_COLS = 4 * W_GDN + 4 * N_HEADS_GDN + 4 * W_RET
CONV_K = 5
CHUNK = 64
D_FF = ((8 * D_MODEL + 3 * 256 - 1) // (3 * 256)) * 256
ROPE_THETA = 10000.0
ROPE_SEQ_PAIRS = 16
ROPE_ROW_PAIRS = 24
ROPE_COL_PAIRS = 24
HALF = HEAD_DIM // 2
NORM_EPS = 1e-6

kernel_name = "hybrid_gdn_retention_dit_block"


def rms_norm(x, g):
    xf = x.astype(jnp.float32)
    y = xf * lax.rsqrt(jnp.mean(xf * xf, axis=-1, keepdims=True) + NORM_EPS)
    return (y * g.astype(jnp.float32)).astype(x.dtype)


def head_group_norm(t, g):
    mu = jnp.mean(t, axis=-1, keepdims=True)
    var = jnp.mean(jnp.square(t - mu), axis=-1, keepdims=True)
    return (t - mu) * lax.rsqrt(var + NORM_EPS) * g.astype(jnp.float32)


def l2_normalize(t):
    return t * lax.rsqrt(jnp.sum(t * t, axis=-1, keepdims=True) + NORM_EPS)


def modulate(h, shift, scale):
    return h * (1.0 + scale) + shift


def adaln(cond, w, b):
    mod = (jax.nn.silu(cond) @ w + b)[:, None, :]
    return jnp.split(mod, 6, axis=-1)


def swiglu(h, w_in, w_out):
    gate, up = jnp.split(h @ w_in, 2, axis=-1)
    return (jax.nn.silu(gate) * up) @ w_out


def axis_angles(pos, n_pairs):
    inv_freq = ROPE_THETA ** (-jnp.arange(n_pairs, dtype=jnp.float32) / n_pairs)
    return pos[:, None] * inv_freq[None, :]


def rope_tables(rows):
    n_lat = rows * GRID_W
    row = jnp.repeat(jnp.arange(rows, dtype=jnp.float32), GRID_W)
    col = jnp.tile(jnp.arange(GRID_W, dtype=jnp.float32), rows)
    zeros_ctx = jnp.zeros((CTX_LEN,), jnp.float32)
    p_seq = jnp.concatenate([jnp.arange(CTX_LEN, dtype=jnp.float32),
                             jnp.full((n_lat,), float(CTX_LEN), jnp.float32)])
    p_row = jnp.concatenate([zeros_ctx, row])
    p_col = jnp.concatenate([zeros_ctx, col])
    ang = jnp.concatenate([axis_angles(p_seq, ROPE_SEQ_PAIRS),
                           axis_angles(p_row, ROPE_ROW_PAIRS),
                           axis_angles(p_col, ROPE_COL_PAIRS)], axis=-1)
    return jnp.cos(ang), jnp.sin(ang)


def apply_rope(t, cos, sin):
    c = cos[None, :, None, :]
    s = sin[None, :, None, :]
    t1, t2 = t[..., :HALF], t[..., HALF:]
    return jnp.concatenate([t1 * c - t2 * s, t1 * s + t2 * c], axis=-1)


def to_bwd(t):
    return jnp.concatenate([jnp.flip(t[:, :CTX_LEN], axis=1), jnp.flip(t[:, CTX_LEN:], axis=1)], axis=1)


def short_conv(u, w):
    n = u.shape[1]
    pad = (CONV_K - 1) // 2
    up = jnp.pad(u, ((0, 0), (pad, pad), (0, 0)))
    y = up[:, 0:n] * w[0]
    for i in range(1, CONV_K):
        y = y + up[:, i:i + n] * w[i]
    return jax.nn.silu(y)


def to_chunks(t):
    b, tl, h = t.shape[:3]
    t = t.reshape((b, tl // CHUNK, CHUNK, h) + t.shape[3:])
    return t.transpose((1, 0, 3, 2) + tuple(range(4, t.ndim)))


def from_chunks(o):
    nc, b, h, cl, d = o.shape
    return o.transpose(1, 0, 3, 2, 4).reshape(b, nc * cl, h, d)


def gated_delta_rule(q, k, v, g, beta):
    b, tl, h, dk = q.shape
    dv = v.shape[-1]
    qc, kc, vc = to_chunks(q), to_chunks(k), to_chunks(v)
    gc, bc = to_chunks(g), to_chunks(beta)
    G = jnp.cumsum(gc, axis=-1)
    idx = jnp.arange(CHUNK)
    incl = idx[:, None] >= idx[None, :]
    strict = idx[:, None] > idx[None, :]
    diff = G[..., :, None] - G[..., None, :]
    decay = jnp.where(incl, jnp.exp(jnp.where(incl, diff, 0.0)), 0.0)
    kb = kc * bc[..., None]
    A = jnp.where(strict, jnp.einsum('nbhid,nbhjd->nbhij', kb, kc) * decay, 0.0)
    rhs = jnp.concatenate([vc * bc[..., None], kb * jnp.exp(G)[..., None]], axis=-1)
    sol = lax.linalg.triangular_solve(A + jnp.eye(CHUNK, dtype=A.dtype), rhs,
                                      left_side=True, lower=True, unit_diagonal=True)
    u, w = sol[..., :dv], sol[..., dv:]
    qk = jnp.einsum('nbhid,nbhjd->nbhij', qc, kc) * decay
    q_dec = qc * jnp.exp(G)[..., None]
    k_dec = kc * jnp.exp(G[..., -1:] - G)[..., None]
    g_last = jnp.exp(G[..., -1])[..., None, None]

    def step(S, xs):
        u_c, w_c, q_c, qk_c, k_c, gl = xs
        v_new = u_c - jnp.einsum('bhck,bhkv->bhcv', w_c, S)
        o = jnp.einsum('bhck,bhkv->bhcv', q_c, S) + jnp.einsum('bhij,bhjv->bhiv', qk_c, v_new)
        S = gl * S + jnp.einsum('bhck,bhcv->bhkv', k_c, v_new)
        return S, o

    S0 = jnp.zeros((b, h, dk, dv), q.dtype)
    _, o = lax.scan(step, S0, (u, w, q_dec, qk, k_dec, g_last))
    return from_chunks(o)


def retention_chunkwise(q, k, v, log_gamma):
    b, tl, h, dk = q.shape
    dv = v.shape[-1]
    qc, kc, vc = to_chunks(q), to_chunks(k), to_chunks(v)
    pos = jnp.arange(CHUNK, dtype=jnp.float32)
    lg = log_gamma.astype(jnp.float32)[:, None]
    diff = pos[:, None] - pos[None, :]
    decay = jnp.where(diff >= 0, jnp.exp(lg[:, :, None] * jnp.maximum(diff, 0.0)), 0.0)
    xi = jnp.exp(lg * (pos + 1.0))
    zeta = jnp.exp(lg * (CHUNK - 1.0 - pos))
    g_chunk = jnp.exp(lg * CHUNK)[..., None]
    inner = jnp.einsum('nbhij,nbhjv->nbhiv', jnp.einsum('nbhid,nbhjd->nbhij', qc, kc) * decay, vc)

    def step(R, xs):
        q_c, k_c, v_c = xs
        o = jnp.einsum('bhik,bhkv->bhiv', q_c * xi[..., None], R)
        R = g_chunk * R + jnp.einsum('bhjk,bhjv->bhkv', k_c * zeta[..., None], v_c)
        return R, o

    R0 = jnp.zeros((b, h, dk, dv), q.dtype)
    _, cross = lax.scan(step, R0, (qc, kc, vc))
    return from_chunks(inner + cross)


def hybrid_mixer(h_ctx, h_lat, w_in, conv_w, gdn_a_log, gdn_dt_bias, gdn_norm_g,
                 ret_decay_logit, ret_norm_g, cos, sin):
    out_dtype = h_lat.dtype
    f32 = jnp.float32
    h = jnp.concatenate([h_ctx, h_lat], axis=1)
    p = (h @ w_in).astype(f32)
    b, tl, _ = p.shape
    splits = [3 * W_GDN, 4 * W_GDN, 4 * W_GDN + 4 * N_HEADS_GDN,
              4 * W_GDN + 4 * N_HEADS_GDN + W_RET,
              4 * W_GDN + 4 * N_HEADS_GDN + 2 * W_RET,
              4 * W_GDN + 4 * N_HEADS_GDN + 3 * W_RET]
    qkv_a, z, ab, rq, rk, rv, rg = jnp.split(p, splits, axis=-1)

    cw = conv_w.astype(f32)
    qkv_a = jnp.concatenate([short_conv(qkv_a[:, :CTX_LEN], cw), short_conv(qkv_a[:, CTX_LEN:], cw)], axis=1)
    q, k, v = [t.reshape(b, tl, N_HEADS_GDN, HEAD_DIM) for t in jnp.split(qkv_a, 3, axis=-1)]
    q = l2_normalize(q) * (HEAD_DIM ** -0.5)
    k = l2_normalize(k)
    a_f, a_b, b_f, b_b = jnp.split(ab, 4, axis=-1)
    A = jnp.exp(gdn_a_log.astype(f32))
    dtb = gdn_dt_bias.astype(f32)
    g_f = -A[0] * jax.nn.softplus(a_f + dtb[0])
    g_b = -A[1] * jax.nn.softplus(a_b + dtb[1])
    o_f = gated_delta_rule(q, k, v, g_f, jax.nn.sigmoid(b_f))
    o_b = to_bwd(gated_delta_rule(to_bwd(q), to_bwd(k), to_bwd(v), to_bwd(g_b), to_bwd(jax.nn.sigmoid(b_b))))
    o_gdn = rms_norm(o_f + o_b, gdn_norm_g) * jax.nn.silu(z.reshape(b, tl, N_HEADS_GDN, HEAD_DIM))

    rq = apply_rope(rq.reshape(b, tl, N_HEADS_RET, HEAD_DIM), cos, sin)
    rk = apply_rope(rk.reshape(b, tl, N_HEADS_RET, HEAD_DIM), cos, sin) * (HEAD_DIM ** -0.5)
    rv = rv.reshape(b, tl, N_HEADS_RET, HEAD_DIM)
    lg = jax.nn.log_sigmoid(ret_decay_logit.astype(f32))
    r_f = retention_chunkwise(rq, rk, rv, lg[0])
    r_b = to_bwd(retention_chunkwise(to_bwd(rq), to_bwd(rk), to_bwd(rv), lg[1]))
    o_ret = head_group_norm(r_f + r_b, ret_norm_g) * jax.nn.silu(rg.reshape(b, tl, N_HEADS_RET, HEAD_DIM))

    y = jnp.concatenate([o_gdn.reshape(b, tl, W_GDN), o_ret.reshape(b, tl, W_RET)], axis=-1)
    return y.astype(out_dtype)


def setup_inputs(seed: int = 0) -> dict:
    key = jax.random.key(seed)
    ks = jax.random.split(key, 19)
    f32 = jnp.float32

    def nrm(k, shape, scale):
        return scale * jax.random.normal(k, shape, f32)

    x = nrm(ks[0], (BATCH, SEQ, D_MODEL), 1.0)
    c = nrm(ks[1], (BATCH, D_MODEL), 1.0)
    ctx = nrm(ks[2], (BATCH, CTX_LEN, D_MODEL), 1.0)
    c_ctx = nrm(ks[3], (D_MODEL,), 1.0)
    ada_w = nrm(ks[4], (DEPTH, D_MODEL, 6 * D_MODEL), D_MODEL ** -0.5)
    ada_b = nrm(ks[5], (DEPTH, 6 * D_MODEL), 0.02)
    norm_mix_g = 1.0 + nrm(ks[6], (DEPTH, D_MODEL), 0.05)
    norm_ffn_g = 1.0 + nrm(ks[7], (DEPTH, D_MODEL), 0.05)
    w_in = nrm(ks[8], (DEPTH, D_MODEL, IN_COLS), D_MODEL ** -0.5)
    conv_w = nrm(ks[9], (DEPTH, CONV_K, 3 * W_GDN), CONV_K ** -0.5)
    gdn_a_log = jnp.log(jax.random.uniform(ks[10], (DEPTH, 2, N_HEADS_GDN), f32, 1.0, 16.0))
    dt = jnp.exp(jax.random.uniform(ks[11], (DEPTH, 2, N_HEADS_GDN), f32, math.log(1e-3), math.log(1e-1)))
    gdn_dt_bias = dt + jnp.log(-jnp.expm1(-dt))
    gdn_norm_g = 1.0 + nrm(ks[12], (DEPTH, HEAD_DIM), 0.05)
    heads = jnp.arange(N_HEADS_RET, dtype=f32)
    ret_decay_logit = jnp.log(2.0 ** (5.0 + heads) - 1.0) + nrm(ks[13], (DEPTH, 2, N_HEADS_RET), 0.01)
    ret_norm_g = 1.0 + nrm(ks[14], (DEPTH, HEAD_DIM), 0.05)
    w_out = nrm(ks[15], (DEPTH, MIX_WIDTH, D_MODEL), MIX_WIDTH ** -0.5)
    w_ffn_in = nrm(ks[16], (DEPTH, D_MODEL, 2 * D_FF), D_MODEL ** -0.5)
    w_ffn_out = nrm(ks[17], (DEPTH, D_FF, D_MODEL), D_FF ** -0.5)
    final_g = 1.0 + nrm(ks[18], (D_MODEL,), 0.05)
    return {"x": x, "c": c, "ctx": ctx, "c_ctx": c_ctx, "ada_w": ada_w, "ada_b": ada_b,
            "norm_mix_g": norm_mix_g, "norm_ffn_g": norm_ffn_g, "w_in": w_in, "conv_w": conv_w,
            "gdn_a_log": gdn_a_log, "gdn_dt_bias": gdn_dt_bias, "gdn_norm_g": gdn_norm_g,
            "ret_decay_logit": ret_decay_logit, "ret_norm_g": ret_norm_g, "w_out": w_out,
            "w_ffn_in": w_ffn_in, "w_ffn_out": w_ffn_out, "final_g": final_g}


def reference(x, c, ctx, c_ctx, ada_w, ada_b, norm_mix_g, norm_ffn_g, w_in, conv_w,
              gdn_a_log, gdn_dt_bias, gdn_norm_g, ret_decay_logit, ret_norm_g, w_out,
              w_ffn_in, w_ffn_out, final_g):
    n_lat = x.shape[1]
    rows = n_lat // GRID_W
    cos, sin = rope_tables(rows)
    x_ctx = ctx
    for layer in range(DEPTH):
        sh1, sc1, g1, sh2, sc2, g2 = adaln(c, ada_w[layer], ada_b[layer])
        csh1, csc1, cg1, csh2, csc2, cg2 = adaln(c_ctx[None, :], ada_w[layer], ada_b[layer])
        h_lat = modulate(rms_norm(x, norm_mix_g[layer]), sh1, sc1)
        h_ctx = modulate(rms_norm(x_ctx, norm_mix_g[layer]), csh1, csc1)
        y = hybrid_mixer(h_ctx, h_lat, w_in[layer], conv_w[layer], gdn_a_log[layer],
                         gdn_dt_bias[layer], gdn_norm_g[layer], ret_decay_logit[layer],
                         ret_norm_g[layer], cos, sin)
        x = x + g1 * (y[:, CTX_LEN:] @ w_out[layer])
        x = x + g2 * swiglu(modulate(rms_norm(x, norm_ffn_g[layer]), sh2, sc2),
                            w_ffn_in[layer], w_ffn_out[layer])
        if layer < DEPTH - 1:
            x_ctx = x_ctx + cg1 * (y[:, :CTX_LEN] @ w_out[layer])
            x_ctx = x_ctx + cg2 * swiglu(modulate(rms_norm(x_ctx, norm_ffn_g[layer]), csh2, csc2),
                                         w_ffn_in[layer], w_ffn_out[layer])
    return rms_norm(x, final_g)
```

```python
import contextlib
import numpy as np
import concourse.bass as bass
import concourse.mybir as mybir
from concourse.bass_utils import run_bass_kernel_spmd

F32 = mybir.dt.float32
BF16 = mybir.dt.bfloat16
AF = mybir.ActivationFunctionType
ALU = mybir.AluOpType
AX = mybir.AxisListType

NDMA_SEM = 8
ALLSIG = False
STRICT = True


class Res:
    __slots__ = ("name", "w", "r", "rd")

    def __init__(self, name=""):
        self.name = name
        self.w = None
        self.r = {}
        self.rd = []


class Op:
    __slots__ = ("eng", "fn", "deps", "sig", "sidx", "dma", "didx", "pos", "pseudo")

    def __init__(self, eng, fn, dma):
        self.eng = eng
        self.fn = fn
        self.dma = dma
        self.deps = []
        self.sig = False
        self.sidx = 0
        self.didx = 0
        self.pos = 0
        self.pseudo = False


class Prog:
    ENGS = ("pe", "act", "dve", "pool", "sp")
    BLK = {"pe": "tensor", "act": "scalar", "dve": "vector", "pool": "gpsimd", "sp": "sync"}

    def __init__(self, nc):
        self.nc = nc
        self.ops = {e: [] for e in self.ENGS}
        self.stack = contextlib.ExitStack()
        self.nres = 0
        self.out_res = []
        self._bar_pos = {}

    def __enter__(self):
        self.stack.__enter__()
        self.esem = {e: self.stack.enter_context(self.nc.semaphore("s_" + e)) for e in self.ENGS}
        self.dsem = {e: [self.stack.enter_context(self.nc.semaphore("d_%s%d" % (e, i))) for i in range(NDMA_SEM)]
                     for e in ("sp", "pool", "act")}
        return self

    def __exit__(self, *a):
        return self.stack.__exit__(*a)

    def sb(self, name, shape, dt):
        return self.stack.enter_context(self.nc.sbuf_tensor(name, list(shape), dt))

    def ps(self, name, shape=(128, 512), dt=F32):
        return self.stack.enter_context(self.nc.psum_tensor(name, list(shape), dt))

    def res(self, name=""):
        self.nres += 1
        return Res(name)

    def op(self, eng, fn, R=(), W=(), dma=False):
        o = Op(eng, fn, dma)
        lst = self.ops[eng]
        o.pos = len(lst)
        deps = {}

        def add(d, raw):
            if d is None or d is o:
                return
            if d.dma or o.dma or d.eng != eng:
                deps[id(d)] = d
            elif eng != "pe" and (raw or STRICT):
                deps[id(d)] = d

        for r in R:
            add(r.w, True)
        for w in W:
            add(w.w, False)
            for rr in w.r.values():
                add(rr, False)
            for rr in w.rd:
                add(rr, False)
        best = {}
        out = []
        for d in deps.values():
            if d.dma:
                out.append(d)
            else:
                b = best.get(d.eng)
                if b is None or d.pos > b.pos:
                    best[d.eng] = d
        out.extend(best.values())
        o.deps = out
        for d in out:
            d.sig = True
        for r in R:
            if dma:
                r.rd.append(o)
            else:
                r.r[eng] = o
        for w in W:
            w.w = o
            w.r = {}
            w.rd = []
        lst.append(o)
        return o

    def dma(self, eng, out, in_, R=(), W=()):
        return self.op(eng, lambda e: e.dma_start(out=out, in_=in_), R, W, dma=True)

    def mm(self, out, lhsT, rhs, start, stop, R=(), W=()):
        return self.op("pe", lambda e: e.matmul(out, lhsT, rhs, start=start, stop=stop), R, W)

    def tr(self, out, in_, ident, R=(), W=()):
        return self.op("pe", lambda e: e.transpose(out, in_, ident), R, W)

    def act(self, out, in_, func, bias=None, scale=None, accum=None, R=(), W=()):
        kw = {}
        if bias is not None:
            kw["bias"] = bias
        if scale is not None:
            kw["scale"] = scale
        if accum is not None:
            kw["accum_out"] = accum
        return self.op("act", lambda e: e.activation(out, in_, func, **kw), R, W)

    def tt(self, eng, out, a, b, op, R=(), W=()):
        return self.op(eng, lambda e: e.tensor_tensor(out, a, b, op), R, W)

    def ts(self, eng, out, a, s1, s2, op0, op1=None, R=(), W=()):
        if op1 is None:
            return self.op(eng, lambda e: e.tensor_scalar(out, a, s1, None, op0), R, W)
        return self.op(eng, lambda e: e.tensor_scalar(out, a, s1, s2, op0, op1), R, W)

    def stt(self, eng, out, in0, scalar, in1, op0, op1, R=(), W=()):
        return self.op(eng, lambda e: e.scalar_tensor_tensor(out, in0, scalar, in1, op0, op1), R, W)

    def cp(self, eng, out, in_, R=(), W=()):
        if eng == "act":
            return self.op(eng, lambda e: e.copy(out, in_), R, W)
        return self.op(eng, lambda e: e.tensor_copy(out, in_), R, W)

    def out_dma(self, eng, out, in_, R=()):
        r = self.res("out")
        self.out_res.append(r)
        return self.dma(eng, out, in_, R=R, W=[r])

    def barrier(self):
        lasts = []
        for e in self.ENGS:
            for o_ in reversed(self.ops[e]):
                if not o_.pseudo and not o_.dma:
                    lasts.append(o_)
                    break
        dmas = [o for e in self.ENGS for o in self.ops[e][self._bar_pos.get(e, 0):] if o.dma]
        self._bar_pos = {e: len(self.ops[e]) for e in self.ENGS}
        for e in self.ENGS:
            o = Op(e, lambda eng: None, False)
            o.pseudo = True
            o.pos = len(self.ops[e])
            deps = [d for d in lasts if d.eng != e and not d.dma] + dmas
            o.deps = deps
            for d in deps:
                d.sig = True
            self.ops[e].append(o)

    def finish(self):
        fo = self.op("sp", lambda e: None, R=self.out_res)
        fo.pseudo = True
        for e in self.ENGS:
            cnt = 0
            nd = 0
            for o in self.ops[e]:
                if ALLSIG and e in ("dve", "act", "pool") and not o.pseudo and not o.dma:
                    o.sig = True
                if o.dma:
                    o.didx = nd
                    nd += 1
                elif o.sig:
                    assert not o.pseudo
                    cnt += 1
                    o.sidx = cnt
        with self.nc.Block() as block:
            for e in self.ENGS:
                getattr(block, self.BLK[e])(lambda eng, e=e: self._emit(e, eng))

    def _emit(self, e, eng):
        known = {}
        K = NDMA_SEM
        for o in self.ops[e]:
            waits = {}
            for d in o.deps:
                if d.dma:
                    sem = self.dsem[d.eng][d.didx % K]
                    val = 16 * (d.didx // K + 1)
                else:
                    sem = self.esem[d.eng]
                    val = d.sidx
                k = sem.num
                if waits.get(k, (None, 0))[1] < val:
                    waits[k] = (sem, val)
            if o.dma and o.didx >= K:
                sem = self.dsem[e][o.didx % K]
                val = 16 * (o.didx // K)
                if waits.get(sem.num, (None, 0))[1] < val:
                    waits[sem.num] = (sem, val)
            for k, (sem, val) in waits.items():
                if known.get(k, 0) < val:
                    eng.wait_ge(sem, val)
                    known[k] = val
            inst = o.fn(eng)
            if inst is None:
                continue
            if o.dma:
                inst.then_inc(self.dsem[e][o.didx % K], 16)
            elif o.sig:
                inst.then_inc(self.esem[e], 1)


CTX, LAT, DM, HD, C = 256, 4096, 1024, 128, 128
T = CTX + LAT
NCH = T // C
OWN0, OWN1 = 2, 18
NOWN = 16
DFF = 2816
EPS = 1e-6
U8 = mybir.dt.uint8

CB_I, CB_MI, CB_MS, CB_LM, CB_DIST, CB_ONES, NCB = 0, 1, 3, 5, 19, 21, 22
RO_GN, RO_RN, RO_AL, RO_DTB, RO_RL, NRS = 0, 128, 256, 264, 272, 280
RO_FG, RO_B1, RO_B2, NRO = 280, 1304, 2328, 3352


def interleave(gens):
    gens = list(gens)
    while gens:
        nxt = []
        for g in gens:
            try:
                next(g)
                nxt.append(g)
            except StopIteration:
                pass
        gens = nxt


def chain(gs):
    for g in gs:
        yield from g


def delayed(g, n):
    for _ in range(n):
        yield
    yield from g


MARKS = []


def build(dbg=False):
    nc = bass.Bass("TRN2", target_bir_lowering=False)

    def IN(name, shape):
        return nc.dram_tensor(name, list(shape), F32, kind="ExternalInput").ap()

    xs = IN("xs", [T, DM]); xo = IN("xo", [2048, DM])
    cvec = IN("cvec", [128, 16]); ada_w = IN("ada_w", [DM, 6 * DM]); adabT = IN("adabT", [128, 48])
    gmixT = IN("gmixT", [128, 8]); gffnT = IN("gffnT", [128, 8])
    w_in = IN("w_in", [DM, 4112]); w_ab = IN("w_ab", [DM, 16]); convT = IN("convT", [128, 12, 5])
    rowc = IN("rowc", [128, NRO]); cs = IN("cs", [128, NCH, 128]); cst = IN("cst", [128, NCB * 128]); csm = IN("csm", [128, 24])
    w_out = IN("w_out", [DM, DM]); w_fi = IN("w_fi", [DM, 2 * DFF]); w_fo = IN("w_fo", [DFF, DM])
    out = nc.dram_tensor("out", [2048, DM], F32, kind="ExternalOutput").ap()
    dbg_out = {}

    P = Prog(nc)
    MARKS.clear()
    def MK(name):
        MARKS.append((name, len(P.ops['pe'])))
    with P:
        R_ = P.res
        cstf = P.sb("cstf", [128, 8 * 128], F32); r_cst = R_()
        P.dma("sp", cstf[:, 0:640], cst[:, 0:640], W=[r_cst])
        P.dma("sp", cstf[:, 640:1024], cst[:, CB_DIST * 128:NCB * 128], W=[r_cst])
        csmf = P.sb("csmf", [128, 24], F32)
        P.dma("sp", csmf[:], csm[:, :], W=[r_cst])
        rowf = P.sb("rowf", [128, NRS], F32)
        P.dma("sp", rowf[:], rowc[:, 0:NRS], W=[r_cst])
        convf = P.sb("convf", [128, 12, 5], F32)
        P.dma("sp", convf[:], convT[:, :, :], W=[r_cst])
        lm8 = P.sb("lm8", [128, 14, 2, 128], U8); idb = P.sb("idb", [128, 128], BF16); r_c2 = R_()
        with nc.sbuf_tensor("lmf", [128, 14 * 128], F32) as lmf:
            r_lmf = R_()
            P.dma("sp", lmf[:], cst[:, CB_LM * 128:(CB_LM + 14) * 128], W=[r_lmf])
            for g_ in range(2):
                P.cp("dve", lm8[:, :, g_, :], lmf[:].rearrange("p (a x) -> p a x", x=128), R=[r_lmf], W=[r_c2])
            P.barrier()
        P.cp("dve", idb[:], cstf[:, 0:128], R=[r_cst], W=[r_c2])
        RC = [r_cst, r_c2]

        def cb(i):
            if i >= CB_DIST:
                i -= 14
            return cstf[:, i * 128:(i + 1) * 128]
        I_f = cb(CB_I); ones_f = cb(CB_ONES)
        MI = [cb(CB_MI), cb(CB_MI + 1)]; MS = [cb(CB_MS), cb(CB_MS + 1)]; DIST = [cb(CB_DIST), cb(CB_DIST + 1)]

        def LM8(d, l):
            return lm8[:, d * 7 + (l - 1), :, :]

        banks = [P.ps("bank%d" % i) for i in range(8)]
        bres = [R_() for _ in range(8)]
        bctr = [0]

        def PS():
            i = bctr[0] % 8
            bctr[0] += 1
            return banks[i], bres[i]

        modT = P.sb("modT", [128, 48, 2], F32); r_mod = R_()
        g1row = P.sb("g1row", [128, DM], BF16); g2row = P.sb("g2row", [128, DM], BF16); r_grow = R_()
        sc1 = P.sb("sc1", [128, 8, 4], F32)
        sc2 = P.sb("sc2", [128, 8, 2], F32); r_sc = R_()
        with contextlib.ExitStack() as es:
            adw = [es.enter_context(nc.sbuf_tensor("adw%d" % i, [128, 8, 512], F32)) for i in range(2)]
            r_adw = [R_(), R_()]
            cv = es.enter_context(nc.sbuf_tensor("cv", [128, 16], F32)); r_cv = R_()
            scv = es.enter_context(nc.sbuf_tensor("scv", [128, 8, 2], F32))
            mrow = es.enter_context(nc.sbuf_tensor("mrow", [2, 6 * DM], F32)); r_mrow = R_()
            P.dma("sp", cv[:], cvec[:, :], W=[r_cv])
            brow = es.enter_context(nc.sbuf_tensor("brow", [128, 2048], F32))
            P.dma("sp", brow[:], rowc[:, RO_B1:RO_B1 + 2048], W=[r_cv])
            P.act(scv[:].rearrange("p k t -> p t k"), cv[:].rearrange("p (t k) -> p t k", t=2), AF.Silu, R=[r_cv], W=[r_cv])
            adv = ada_w.rearrange("(k p) n -> p k n", p=128)
            for blk in range(12):
                bi = blk % 2
                P.dma("sp", adw[bi][:], adv[:, :, blk * 512:(blk + 1) * 512], W=[r_adw[bi]])
                pr, r_pr = banks[1 + blk % 2], bres[1 + blk % 2]
                for k in range(8):
                    P.mm(pr[0:2, :], scv[:, k, :], adw[bi][:, k, :], k == 0, k == 7, R=[r_adw[bi], r_cv], W=[r_pr])
                P.cp("act", mrow[0:2, blk * 512:(blk + 1) * 512], pr[0:2, :], R=[r_pr], W=[r_mrow])
            pm, r_pm = banks[0], bres[0]
            for n in range(48):
                P.tr(pm[:, n * 2:n * 2 + 2], mrow[0:2, n * 128:(n + 1) * 128], I_f[0:2, 0:2], R=[r_mrow] + RC, W=[r_pm])
            for gi_, (dstrow, col0) in enumerate(((g1row, 2 * DM), (g2row, 5 * DM))):
                for hf in range(2):
                    pg, r_pg = banks[3 + hf], bres[3 + hf]
                    P.mm(pg[:, :], ones_f[0:1, 0:128], mrow[0:1, col0 + hf * 512:col0 + (hf + 1) * 512], True, True, R=[r_mrow] + RC, W=[r_pg])
                    P.tt("dve", dstrow[:, hf * 512:(hf + 1) * 512], pg[:, :], brow[:, gi_ * 1024 + hf * 512:gi_ * 1024 + (hf + 1) * 512], ALU.add,
                         R=[r_pg, r_cv], W=[r_grow])
            abT = es.enter_context(nc.sbuf_tensor("abT", [128, 48], F32))
            gm = es.enter_context(nc.sbuf_tensor("gm", [128, 16], F32))
            P.dma("sp", abT[:], adabT[:, :], W=[r_cv])
            P.dma("sp", gm[:, 0:8], gmixT[:, :], W=[r_cv])
            P.dma("sp", gm[:, 8:16], gffnT[:, :], W=[r_cv])
            P.tt("dve", modT[:], pm[:, 0:96].rearrange("p (n t) -> p n t", t=2), abT[:].unsqueeze(2).to_broadcast([128, 48, 2]),
                 ALU.add, R=[r_pm, r_cv], W=[r_mod])
            P.stt("dve", sc1[:, :, 0], modT[:, 8:16, 0], 1.0, gm[:, 0:8], ALU.add, ALU.mult, R=[r_mod, r_cv], W=[r_sc])
            P.cp("dve", sc1[:, :, 1], modT[:, 0:8, 0], R=[r_mod], W=[r_sc])
            P.stt("dve", sc1[:, :, 2], modT[:, 8:16, 1], 1.0, gm[:, 0:8], ALU.add, ALU.mult, R=[r_mod, r_cv], W=[r_sc])
            P.cp("dve", sc1[:, :, 3], modT[:, 0:8, 1], R=[r_mod], W=[r_sc])
            P.stt("dve", sc2[:, :, 0], modT[:, 32:40, 0], 1.0, gm[:, 8:16], ALU.add, ALU.mult, R=[r_mod, r_cv], W=[r_sc])
            P.cp("dve", sc2[:, :, 1], modT[:, 24:32, 0], R=[r_mod], W=[r_sc])
            P.barrier()

        def norm_block(src, r_src, dstT, r_dst, col0, scl, shf, tmp):
            sq, ss, xn, r_t = tmp
            P.act(sq[:], src, AF.Square, accum=ss[:, 0:1], R=[r_src], W=[r_t])
            P.act(ss[:, 1:2], ss[:, 0:1], AF.Ln, bias=EPS, scale=1.0 / DM, R=[r_t], W=[r_t])
            P.act(ss[:, 2:3], ss[:, 1:2], AF.Exp, scale=-0.5, R=[r_t], W=[r_t])
            P.ts("dve", xn[:], src, ss[:, 2:3], None, ALU.mult, R=[r_src, r_t], W=[r_t])
            pt, r_pt = PS()
            ptb = pt[:, :].bitcast(BF16)
            for k in range(8):
                P.tr(ptb[:, k * 128:(k + 1) * 128], xn[:, k * 128:(k + 1) * 128], idb[:], R=[r_t, r_c2], W=[r_pt])
            for k in range(8):
                if k % 2 == 0:
                    P.act(dstT[:, k, col0:col0 + 128], ptb[:, k * 128:(k + 1) * 128], AF.Identity, bias=shf(k), scale=scl(k),
                          R=[r_pt, r_sc], W=[r_dst])
                else:
                    P.ts("dve", dstT[:, k, col0:col0 + 128], ptb[:, k * 128:(k + 1) * 128], scl(k), shf(k), ALU.mult, ALU.add,
                         R=[r_pt, r_sc], W=[r_dst])

        yT = P.sb("yT", [128, 8, 2048], BF16); r_yT = [R_() for _ in range(8)]
        wv = w_in.rearrange("(k p) n -> p k n", p=128)

        with contextlib.ExitStack() as es:
            def SB(name, shape, dt):
                return es.enter_context(nc.sbuf_tensor(name, list(shape), dt))
            hT = SB("hT", [128, 8, T], BF16); r_hT = [R_() for _ in range(NCH)]
            MK('B')
            esB = contextlib.ExitStack()
            def SBB(name, shape, dt):
                return esB.enter_context(nc.sbuf_tensor(name, list(shape), dt))
            xts = [SBB("xt%d" % i, [128, DM], F32) for i in range(2)]; r_xt = [R_(), R_()]
            ntmp = [(SBB("nsq%d" % i, [128, DM], BF16), SBB("nss%d" % i, [128, 4], F32), SBB("nxn%d" % i, [128, DM], BF16), R_()) for i in range(2)]
            for c in range(NCH):
                i = c % 2
                P.dma("sp", xts[i][:], xs[c * 128:(c + 1) * 128, :], W=[r_xt[i]])
                o_ = 2 if c < 2 else 0
                norm_block(xts[i][:], r_xt[i], hT, r_hT[c], c * 128,
                           lambda k, o_=o_: sc1[:, k, o_:o_ + 1], lambda k, o_=o_: sc1[:, k, o_ + 1:o_ + 2], ntmp[i])
            if dbg:
                dbg_out["hT"] = nc.dram_tensor("d_hT", [128, 8, T], BF16, kind="ExternalOutput").ap()
                P.out_dma("sp", dbg_out["hT"][:, :, :], hT[:], R=r_hT)

            P.barrier()
            esB.close()
            MK('C')
            wab = SB("wab", [128, 8, 16], BF16); r_wab = R_()
            P.dma("pool", wab[:], w_ab.rearrange("(k p) n -> p k n", p=128), W=[r_wab])
            abs_ = SB("abs", [128, 16, NCH], F32); r_ab = R_()
            pa, r_pa = PS(); pb, r_pb = PS()
            for c in range(NCH):
                tgt = pa[:, c * 16:(c + 1) * 16] if c < 32 else pb[:, (c - 32) * 16:(c - 31) * 16]
                for k in range(8):
                    P.mm(tgt, hT[:, k, c * 128:(c + 1) * 128], wab[:, k, :], k == 0, k == 7, R=[r_hT[c], r_wab],
                         W=[r_pa if c < 32 else r_pb])
            P.cp("dve", abs_[:, :, 0:32].rearrange("p j c -> p c j"), pa[:, 0:512].rearrange("p (c j) -> p c j", j=16), R=[r_pa], W=[r_ab])
            P.cp("dve", abs_[:, :, 32:34].rearrange("p j c -> p c j"), pb[:, 0:32].rearrange("p (c j) -> p c j", j=16), R=[r_pb], W=[r_ab])
            gal = SB("gal", [128, 8, NCH], F32); sal = SB("sal", [128, 8, NCH], F32); Gal = SB("Gal", [128, 8, NCH], F32)
            eG = SB("eG", [128, 8, NCH], F32); neG = SB("neG", [128, 8, NCH], F32); eGl = SB("eGl", [128, 8, NCH], F32)
            ekd = SB("ekd", [128, 8, NCH], F32); nA8 = SB("nA8", [128, 8], F32); r_g = R_()
            P.tt("dve", gal[:], abs_[:, 0:8, :], rowf[:, RO_DTB:RO_DTB + 8].unsqueeze(2).to_broadcast([128, 8, NCH]), ALU.add, R=[r_ab] + RC, W=[r_g])
            P.act(gal[:], gal[:], AF.Exp, R=[r_g], W=[r_g])
            P.act(gal[:], gal[:], AF.Ln, bias=1.0, R=[r_g], W=[r_g])
            P.act(nA8[:], rowf[:, RO_AL:RO_AL + 8], AF.Exp, R=RC, W=[r_g])
            P.stt("dve", gal[:], gal[:], -1.0, nA8[:].unsqueeze(2).to_broadcast([128, 8, NCH]), ALU.mult, ALU.mult, R=[r_g], W=[r_g])
            P.act(sal[:], abs_[:, 8:16, :], AF.Sigmoid, R=[r_ab], W=[r_g])
            P.act(sal[:], sal[:], AF.Sqrt, R=[r_g], W=[r_g])
            galf = gal[:].rearrange("p a c -> p (a c)")
            for d in range(2):
                pg, r_pg = PS()
                P.mm(pg[:, 0:4 * NCH], MI[d], galf[:, d * 4 * NCH:(d + 1) * 4 * NCH], True, True, R=[r_g] + RC, W=[r_pg])
                P.cp("dve", Gal[:, d * 4:(d + 1) * 4, :].rearrange("p a c -> p (a c)"), pg[:, 0:4 * NCH], R=[r_pg], W=[r_g])
            pg, r_pg = PS()
            P.mm(pg[:, 0:8 * NCH], ones_f, galf, True, True, R=[r_g] + RC, W=[r_pg])
            P.act(eGl[:].rearrange("p a c -> p (a c)"), pg[:, 0:8 * NCH], AF.Exp, R=[r_pg], W=[r_g])
            P.tt("dve", ekd[:].rearrange("p a c -> p (a c)"), pg[:, 0:8 * NCH], Gal[:].rearrange("p a c -> p (a c)"), ALU.subtract, R=[r_pg, r_g], W=[r_g])
            P.act(ekd[:], ekd[:], AF.Exp, R=[r_g], W=[r_g])
            P.act(eG[:], Gal[:], AF.Exp, R=[r_g], W=[r_g])
            P.ts("dve", neG[:], eG[:], -1.0, None, ALU.mult, R=[r_g], W=[r_g])

            junk = SB("junk", [128, 384], F32); r_junk = R_(); ysc = SB("ysc", [128, 128], BF16); r_ysc = R_()
            kbar = SB("kbar", [128, NCH, 128], BF16); r_kbar = [R_() for _ in range(NCH)]
            vtok = SB("vtok", [128, NCH, 128], BF16); r_vtok = [R_() for _ in range(NCH)]
            qT = SB("qT", [128, 2048], BF16); r_qT = R_()
            zs = SB("zs", [128, NOWN, 128], BF16); r_zs = R_()
            oacc = SB("oacc", [128, NOWN, 128], F32); r_oacc = [R_() for _ in range(NOWN)]
            sm = SB("sm", [128, 8], F32); r_sm = R_()
            smk = SB("smk", [128, 3, 4], F32); r_smk = [R_(), R_(), R_()]
            Sf = [SB("Sf%d" % d, [128, 128], F32) for d in range(2)]; Sb = [SB("Sb%d" % d, [128, 128], BF16) for d in range(2)]
            r_S = [R_(), R_()]
            esD = contextlib.ExitStack()
            def SBD(name, shape, dt):
                return esD.enter_context(nc.sbuf_tensor(name, list(shape), dt))
            u = SBD("u", [128, T], BF16); yc = SBD("yc", [128, T], BF16); kact = yc; r_u = R_(); r_yc = R_(); r_ka = r_yc
            wq = SBD("wq", [128, 8, 128], BF16); r_wq = R_()
            oss = SBD("oss", [128, 2, NOWN], F32); r_oss = R_(); ysc2 = SBD("ysc2", [128, 128], BF16); r_ysc2 = R_(); r_jk = [R_(), R_()]
            NR = 4
            GB = 2

            def bfv(base, off, n=GB * 128):
                return base[:, off:off + n].rearrange("p (g x) -> p g x", x=128)

            def f32v(base, off):
                return base[:, off:off + 2 * GB * 128].bitcast(F32).rearrange("p (g x) -> p g x", x=128)

            def mk_tset(i):
                return dict(kp=SBD("pkp%d" % i, [128, GB, 128], BF16), nA=SBD("pnA%d" % i, [128, GB, 128], BF16),
                            nAT=SBD("pnAT%d" % i, [128, GB, 128], BF16), D=SBD("pD%d" % i, [128, GB, 128], BF16),
                            X=SBD("pX%d" % i, [128, GB, 128], BF16), rg=SBD("prg%d" % i, [128, GB, 128], F32),
                            ex=SBD("pex%d" % i, [128, GB, 128], F32), exT=SBD("pexT%d" % i, [128, GB, 128], F32), res=R_())

            def mk_gslot(i):
                return dict(E=SBD("gE%d" % i, [128, GB, 128], BF16), exTI=SBD("gX%d" % i, [128, GB, 128], BF16),
                            kT=SBD("gkT%d" % i, [128, GB, 128], BF16), kpp=SBD("gkpp%d" % i, [128, GB, 128], BF16),
                            vp=SBD("gvp%d" % i, [128, GB, 128], BF16), res=R_())
            W_ = GB * 128
            tsets = [mk_tset(0), mk_tset(1),
                     dict(kp=bfv(u, 0), nA=bfv(u, W_), nAT=bfv(u, 2 * W_), D=bfv(u, 3 * W_), X=bfv(u, 4 * W_),
                          rg=f32v(u, 5 * W_), ex=f32v(u, 7 * W_), exT=f32v(u, 9 * W_), res=r_u),
                     dict(kp=bfv(yc, 5 * W_), nA=bfv(yc, 6 * W_), nAT=bfv(yc, 7 * W_), D=bfv(yc, 8 * W_), X=bfv(yc, 9 * W_),
                          rg=f32v(yc, 10 * W_), ex=f32v(yc, 12 * W_), exT=f32v(yc, 14 * W_), res=r_yc)]
            NGS = 8
            gring = [mk_gslot(i) for i in range(NGS - 2)]
            gring.append(dict(E=bfv(u, 11 * W_), exTI=bfv(u, 12 * W_), kT=bfv(u, 13 * W_), kpp=bfv(u, 14 * W_), vp=bfv(u, 15 * W_), res=r_u))
            gring.append(dict(E=bfv(yc, 0), exTI=bfv(yc, W_), kT=bfv(yc, 2 * W_), kpp=bfv(yc, 3 * W_), vp=bfv(yc, 4 * W_), res=r_yc))
            assert 16 * W_ <= T
            NSS = 2
            sring = [dict(Rr=SBD("sR%d" % i, [128, 128], BF16), vn=SBD("svn%d" % i, [128, 128], BF16), qkm=SBD("sqkm%d" % i, [128, 128], BF16),
                          o1=SBD("so1%d" % i, [128, 128], F32), res2=R_()) for i in range(NSS)]

            def conv_silu(ci, lo, hi, parts, dst, r_dst):
                first = True
                for tap in (2, 0, 1, 3, 4):
                    s = tap - 2
                    for (a, e) in parts:
                        l2, h2 = max(a, a - s, lo), min(e, e - s, hi)
                        if tap == 2:
                            P.ts("dve", yc[:, l2:h2], u[:, l2:h2], convf[:, ci, tap:tap + 1], None, ALU.mult, R=[r_u] + RC, W=[r_yc])
                        else:
                            eng = "dve"
                            P.stt(eng, yc[:, l2:h2], u[:, l2 + s:h2 + s], convf[:, ci, tap:tap + 1], yc[:, l2:h2], ALU.mult, ALU.add,
                                  R=[r_u, r_yc] + RC, W=[r_yc])
                P.act(dst, yc[:, lo:hi], AF.Silu, R=[r_yc], W=[r_dst])

            def inproj_fm(col, t0, t1, wt, r_wt):
                P.dma("pool", wt[:], wv[:, :, col:col + 128], W=[r_wt])
                t = t0
                i = 0
                while t < t1:
                    n = min(512, t1 - t)
                    pp, r_pp = PS()
                    for k in range(8):
                        P.mm(pp[:, 0:n], wt[:, k, :], hT[:, k, t:t + n], k == 0, k == 7, R=[r_wt] + r_hT, W=[r_pp])
                    if i % 2 == 0:
                        P.cp("act", u[:, t:t + n], pp[:, 0:n], R=[r_pp], W=[r_u])
                    else:
                        P.cp("dve", u[:, t:t + n], pp[:, 0:n], R=[r_pp], W=[r_u])
                    t += n
                    i += 1

            def fl(ap):
                return ap.rearrange("p g x -> p (g x)")

            def bc(ap2d, G):
                return ap2d.unsqueeze(1).to_broadcast([128, G, 128])

            def gdn_prep(h, d, c0, G, gs, full, ti):
                tm = tsets[ti]; r1 = tm["res"]; gr = gring[gs]; rg_ = gr["res"]
                j = d * 4 + h
                scb = sal[:, j, c0:c0 + G].unsqueeze(2).to_broadcast([128, G, 128])
                P.tt("pool", tm["kp"][:, 0:G, :], kbar[:, c0:c0 + G, :], scb, ALU.mult, R=r_kbar[c0:c0 + G] + [r_g], W=[r1])
                P.tt("pool", gr["kpp"][:, 0:G, :], tm["kp"][:, 0:G, :], ekd[:, j, c0:c0 + G].unsqueeze(2).to_broadcast([128, G, 128]), ALU.mult,
                     R=[r1, r_g], W=[rg_])
                P.tt("pool", gr["vp"][:, 0:G, :], vtok[:, c0:c0 + G, :], scb, ALU.mult, R=r_vtok[c0:c0 + G] + [r_g], W=[rg_])
                pt, r_pt = PS(); ptb = pt[:, 0:G * 64].bitcast(BF16)
                for g in range(G):
                    P.tr(ptb[:, g * 128:(g + 1) * 128], tm["kp"][:, g, :], idb[:], R=[r1, r_c2], W=[r_pt])
                P.cp("act", fl(gr["kT"][:, 0:G, :]), ptb, R=[r_pt], W=[rg_])
                yield
                P.tt("pool", tm["rg"][:, 0:G, :], bc(MS[d], G), gal[:, j, c0:c0 + G].unsqueeze(2).to_broadcast([128, G, 128]), ALU.mult,
                     R=[r_g] + RC, W=[r1])
                p1, r_p1 = PS(); p2, r_p2 = PS()
                for g in range(G):
                    P.mm(p1[:, g * 128:(g + 1) * 128], MI[d], tm["rg"][:, g, :], True, True, R=[r1] + RC, W=[r_p1])
                for g in range(G):
                    P.mm(p2[:, g * 128:(g + 1) * 128], tm["rg"][:, g, :], MI[d], True, True, R=[r1] + RC, W=[r_p2])
                P.act(fl(tm["ex"][:, 0:G, :]), p1[:, 0:G * 128], AF.Exp, R=[r_p1], W=[r1])
                P.act(fl(tm["exT"][:, 0:G, :]), p2[:, 0:G * 128], AF.Exp, R=[r_p2], W=[r1])
                yield
                P.tt("pool", tm["ex"][:, 0:G, :], tm["ex"][:, 0:G, :], bc(MS[d], G), ALU.mult, R=[r1] + RC, W=[r1])
                if full:
                    P.tt("pool", gr["exTI"][:, 0:G, :], tm["exT"][:, 0:G, :], bc(MI[d], G), ALU.mult, R=[r1] + RC, W=[rg_])
                P.tt("pool", tm["exT"][:, 0:G, :], tm["exT"][:, 0:G, :], bc(MS[1 - d], G), ALU.mult, R=[r1] + RC, W=[r1])
                pk, r_pk = PS()
                for g in range(G):
                    P.mm(pk[:, g * 128:(g + 1) * 128], gr["kT"][:, g, :], gr["kT"][:, g, :], True, True, R=[rg_], W=[r_pk])
                P.stt("dve", fl(tm["nA"][:, 0:G, :]), pk[:, 0:G * 128], -1.0, fl(tm["ex"][:, 0:G, :]), ALU.mult, ALU.mult, R=[r_pk, r1], W=[r1])
                P.stt("dve", fl(tm["nAT"][:, 0:G, :]), pk[:, 0:G * 128], -1.0, fl(tm["exT"][:, 0:G, :]), ALU.mult, ALU.mult, R=[r_pk, r1], W=[r1])
                yield
                P.cp("pool", tm["D"][:, 0:G, :], bc(idb[:], G), R=[r_c2], W=[r1])
                P.cp("pool", gr["E"][:, 0:G, :], bc(idb[:], G), R=[r_c2], W=[rg_])
                P.op("dve", lambda e: e.copy_predicated(tm["D"][:, 0:G, :], LM8(d, 1), tm["nA"][:, 0:G, :]), R=[r1, r_c2], W=[r1])
                P.op("dve", lambda e: e.copy_predicated(gr["E"][:, 0:G, :], LM8(1 - d, 1), tm["nAT"][:, 0:G, :]), R=[r1, rg_, r_c2], W=[rg_])
                yield
                for l in range(2, 8):
                    px, r_px = PS()
                    for g in range(G):
                        P.mm(px[:, g * 128:(g + 1) * 128], tm["nA"][:, g, :], gr["E"][:, g, :], True, True, R=[r1, rg_], W=[r_px])
                    P.cp("act", fl(tm["X"][:, 0:G, :]), px[:, 0:G * 128], R=[r_px], W=[r1])
                    yield
                    pe_, r_pe = PS()
                    for g in range(G):
                        P.mm(pe_[:, g * 128:(g + 1) * 128], tm["D"][:, g, :], tm["X"][:, g, :], True, True, R=[r1], W=[r_pe])
                    if l < 7:
                        pd, r_pd = PS()
                        for g in range(G):
                            P.mm(pd[:, g * 128:(g + 1) * 128], tm["X"][:, g, :], tm["D"][:, g, :], True, True, R=[r1], W=[r_pd])
                    P.op("dve", lambda e, pe_=pe_, l=l: e.copy_predicated(gr["E"][:, 0:G, :], LM8(1 - d, l),
                                                                        pe_[:, 0:G * 128].rearrange("p (g x) -> p g x", x=128)),
                         R=[r_pe, rg_, r_c2], W=[rg_])
                    if l < 7:
                        P.op("dve", lambda e, pd=pd, l=l: e.copy_predicated(tm["D"][:, 0:G, :], LM8(d, l),
                                                                          pd[:, 0:G * 128].rearrange("p (g x) -> p g x", x=128)),
                             R=[r_pd, r1, r_c2], W=[r1])
                    yield

            sctr = [0]

            def gdn_scan(h, d, c, gs, g, full):
                gr = gring[gs]; r1 = gr["res"]
                rs = sring[sctr[0] % NSS]; sctr[0] += 1
                r2 = rs["res2"]
                j = d * 4 + h
                kT_ = gr["kT"][:, g, :]
                oc = c - OWN0
                pk, r_pk = PS()
                P.mm(pk[:, 0:128], kT_, Sb[d][:], True, True, R=[r1, r_S[d]], W=[r_pk])
                if full:
                    po1, r_po1 = PS()
                    P.mm(po1[:, 0:128], qT[:, oc * 128:(oc + 1) * 128], Sb[d][:], True, True, R=[r_qT, r_S[d]], W=[r_po1])
                P.stt("dve", rs["Rr"][:], pk[:, 0:128], neG[:, j, c:c + 1], gr["vp"][:, g, :], ALU.mult, ALU.add, R=[r_pk, r1, r_g], W=[r2])
                if full:
                    P.act(rs["o1"][:], po1[:, 0:128], AF.Identity, scale=eG[:, j, c:c + 1], R=[r_po1, r_g], W=[r2])
                    pq, r_pq = PS()
                    P.mm(pq[:, 0:128], kT_, qT[:, oc * 128:(oc + 1) * 128], True, True, R=[r1, r_qT], W=[r_pq])
                    P.tt("dve", rs["qkm"][:], pq[:, 0:128], gr["exTI"][:, g, :], ALU.mult, R=[r_pq, r1], W=[r2])
                yield
                pv, r_pv = PS()
                P.mm(pv[:, 0:128], gr["E"][:, g, :], rs["Rr"][:], True, True, R=[r1, r2], W=[r_pv])
                P.cp("act", rs["vn"][:], pv[:, 0:128], R=[r_pv], W=[r2])
                yield
                ps_, r_ps = PS()
                P.mm(ps_[:, 0:128], gr["kpp"][:, g, :], rs["vn"][:], True, True, R=[r1, r2], W=[r_ps])
                P.stt("dve", Sf[d][:], Sf[d][:], eGl[:, j, c:c + 1], ps_[:, 0:128], ALU.mult, ALU.add, R=[r_S[d], r_ps, r_g], W=[r_S[d]])
                P.cp("act", Sb[d][:], Sf[d][:], R=[r_S[d]], W=[r_S[d]])
                yield
                if full:
                    po2, r_po2 = PS()
                    P.mm(po2[:, 0:128], rs["qkm"][:], rs["vn"][:], True, True, R=[r2], W=[r_po2])
                    P.tt("dve", rs["o1"][:], rs["o1"][:], po2[:, 0:128], ALU.add, R=[r2, r_po2], W=[r2])
                    P.tt("pool", oacc[:, oc, :], oacc[:, oc, :], rs["o1"][:], ALU.add, R=[r2, r_oacc[oc]], W=[r_oacc[oc]])
                    yield

            chF = [0, 1] + list(range(OWN0, OWN1))
            chR = [1, 0] + list(range(NCH - 1, OWN0 - 1, -1))

            def sched_tasks():
                tasks = []
                iF = iR = 0
                while iF < len(chF) or iR < len(chR):
                    if iR < len(chR):
                        tasks.append((1, chR[iR])); iR += 1
                    if iF < len(chF) and iR * len(chF) >= iF * len(chR):
                        tasks.append((0, chF[iF])); iF += 1
                return tasks

            for h in range(4):
                MK('gdn%d_prep' % h)
                inproj_fm(512 + h * 128, 0, T, wq, r_wq)
                conv_silu(4 + h, 0, T, [(0, CTX), (CTX, T)], kact[:, :], r_ka)
                for c in range(NCH):
                    pt, r_pt = PS(); ptb = pt[:, 0:64].bitcast(BF16)
                    P.tr(ptb, kact[:, c * 128:(c + 1) * 128], idb[:], R=[r_ka, r_c2], W=[r_pt])
                    q_ = c % 3
                    P.act(junk[:, q_ * 128:(q_ + 1) * 128], ptb, AF.Square, accum=smk[:, q_, 0:1], R=[r_pt], W=[r_smk[q_]])
                    P.act(smk[:, q_, 1:2], smk[:, q_, 0:1], AF.Ln, bias=EPS, R=[r_smk[q_]], W=[r_smk[q_]])
                    P.act(smk[:, q_, 2:3], smk[:, q_, 1:2], AF.Exp, scale=-0.5, R=[r_smk[q_]], W=[r_smk[q_]])
                    P.ts("dve", kbar[:, c, :], ptb, smk[:, q_, 2:3], None, ALU.mult, R=[r_pt, r_smk[q_]], W=[r_kbar[c]])
                inproj_fm(1024 + h * 128, 0, T, wq, r_wq)
                conv_silu(8 + h, 0, T, [(0, CTX), (CTX, T)], kact[:, :], r_ka)
                for c in range(NCH):
                    pt, r_pt = PS(); ptb = pt[:, 0:64].bitcast(BF16)
                    P.tr(ptb, kact[:, c * 128:(c + 1) * 128], idb[:], R=[r_ka, r_c2], W=[r_pt])
                    P.cp("act" if c % 2 else "dve", vtok[:, c, :], ptb, R=[r_pt], W=[r_vtok[c]])
                inproj_fm(h * 128, CTX, CTX + 2048 + 128, wq, r_wq)
                conv_silu(h, CTX, CTX + 2048, [(CTX, T)], kact[:, CTX:CTX + 2048], r_ka)
                qsq = oacc[:].rearrange("p a b -> p (a b)"); r_qsq = r_oacc[0]
                P.act(qsq, kact[:, CTX:CTX + 2048], AF.Square, R=[r_ka] + r_oacc, W=r_oacc)
                for t in range(4):
                    pp, r_pp = PS()
                    P.mm(pp[:, :], ones_f, qsq[:, t * 512:(t + 1) * 512], True, True, R=[r_qsq] + RC, W=[r_pp])
                    P.act(qsq[:, t * 512:(t + 1) * 512], pp[:, :], AF.Ln, bias=EPS, R=[r_pp, r_qsq], W=[r_qsq])
                    P.act(qsq[:, t * 512:(t + 1) * 512], qsq[:, t * 512:(t + 1) * 512], AF.Exp, scale=-0.5, R=[r_qsq], W=[r_qsq])
                P.stt("dve", qT[:], kact[:, CTX:CTX + 2048], float(HD) ** -0.5, qsq, ALU.mult, ALU.mult, R=[r_ka, r_qsq], W=[r_qT])
                P.dma("pool", wq[:], wv[:, :, 1536 + h * 128:1536 + (h + 1) * 128], W=[r_wq])
                for oc in range(NOWN):
                    c = OWN0 + oc
                    pz, r_pz = PS()
                    for k in range(8):
                        P.mm(pz[:, 0:128], hT[:, k, c * 128:(c + 1) * 128], wq[:, k, :], k == 0, k == 7, R=[r_hT[c], r_wq], W=[r_pz])
                    P.act(zs[:, oc, :], pz[:, 0:128], AF.Silu, R=[r_pz], W=[r_zs])
                MK('gdn%d_scan' % h)
                for d in range(2):
                    P.op("pool", lambda e, d=d: e.memset(Sf[d][:], 0.0), W=[r_S[d]])
                    P.op("pool", lambda e, d=d: e.memset(Sb[d][:], 0.0), W=[r_S[d]])
                P.op("pool", lambda e: e.memset(oacc[:], 0.0), W=r_oacc)
                Fg = [(0, c0, 2, [0, 1]) for c0 in [0] + list(range(OWN0, OWN1, 2))]
                Rg = [(1, c0, 2, [1, 0]) for c0 in [0] + list(range(NCH - 2, OWN0 - 1, -2))]
                groups = []
                iF = iR = 0
                while iF < len(Fg) or iR < len(Rg):
                    if iR < len(Rg):
                        groups.append(Rg[iR]); iR += 1
                    if iF < len(Fg) and iR * len(Fg) >= iF * len(Rg):
                        groups.append(Fg[iF]); iF += 1
                WV = 4
                prevw = []
                for w0 in range(0, len(groups) + WV, WV):
                    wave = groups[w0:w0 + WV]
                    gens = []
                    cur = []
                    for i, (d, c0, G, order) in enumerate(wave):
                        gsl = (w0 + i) % NGS
                        full = OWN0 <= c0 < OWN1
                        gens.append(delayed(gdn_prep(h, d, c0, G, gsl, full, i), i % 2))
                        cur.append((d, c0, G, order, gsl, full))
                    for dd in range(2):
                        sc_ = []
                        for (pd_, pc0, pG, porder, pgs, pfull) in prevw:
                            if pd_ == dd:
                                sc_ += [gdn_scan(h, pd_, pc0 + g, pgs, g, pfull) for g in porder]
                        if sc_:
                            gens.append(delayed(chain(sc_), dd))
                    interleave(gens)
                    prevw = cur
                MK('gdn%d_out' % h)
                for oc in range(NOWN):
                    P.act(junk[:, 0:128], oacc[:, oc, :], AF.Square, accum=oss[:, 0, oc:oc + 1], R=[r_oacc[oc]], W=[r_junk, r_oss])
                P.act(oss[:, 0, :], oss[:, 0, :], AF.Ln, bias=EPS, scale=1.0 / HD, R=[r_oss], W=[r_oss])
                P.act(oss[:, 0, :], oss[:, 0, :], AF.Exp, scale=-0.5, R=[r_oss], W=[r_oss])
                for oc in range(NOWN):
                    i_ = oc % 2
                    jb = junk[:, 128 + i_ * 128:256 + i_ * 128]
                    ys_, r_ys_ = (ysc, r_ysc) if i_ == 0 else (ysc2, r_ysc2)
                    P.stt("dve", jb, oacc[:, oc, :], oss[:, 0, oc:oc + 1], rowf[:, RO_GN:RO_GN + 128], ALU.mult, ALU.mult,
                          R=[r_oacc[oc], r_oss] + RC, W=[r_jk[i_]])
                    P.tt("pool", ys_[:], jb, zs[:, oc, :], ALU.mult, R=[r_jk[i_], r_zs], W=[r_ys_])
                    pt, r_pt = PS(); ptb = pt[:, 0:64].bitcast(BF16)
                    P.tr(ptb, ys_[:], idb[:], R=[r_ys_, r_c2], W=[r_pt])
                    P.cp("act", yT[:, h, oc * 128:(oc + 1) * 128], ptb, R=[r_pt], W=[r_yT[h]])

            P.barrier()
            esD.close()
            cst_ = SB("cs_sb", [128, NCH, 128], BF16)
            r_csg = {}
            for (c0_, G_) in [(0, 2)] + [(c0_, 4) for c0_ in range(2, NCH, 4)]:
                r_csg[c0_] = R_()
                P.dma("pool", cst_[:, c0_:c0_ + G_, :], cs[:, c0_:c0_ + G_, :], W=[r_csg[c0_]])
            ktok = kbar; r_ktok = r_kbar
            kTr = SB("kTr", [128, NCH, 128], BF16); r_kTr = [R_() for _ in range(NCH)]
            lgc = SB("lgc", [128, 8], F32); r_lg = R_()
            ring = [dict(qkm=SB("eqkm%d" % i, [128, 128], BF16), o1=SB("eo1%d" % i, [128, 128], F32), kpp=SB("ekpp%d" % i, [128, 128], BF16), res2=R_()) for i in range(NR)]
            P.act(lgc[:], rowf[:, RO_RL:RO_RL + 8], AF.Exp, scale=-1.0, R=RC, W=[r_lg])
            P.act(lgc[:], lgc[:], AF.Ln, bias=1.0, R=[r_lg], W=[r_lg])
            P.ts("dve", lgc[:], lgc[:], -1.0, None, ALU.mult, R=[r_lg], W=[r_lg])
            DrT = [SB("DrT%d" % d, [128, 128], F32) for d in range(2)]
            rsc = SB("rsc", [128, 8], F32); r_rc = R_()
            rw3 = [SB("rw%d" % i, [128, 8, 128], BF16) for i in range(4)]; r_rw3 = [R_() for _ in range(4)]

            rts = [SB("rts%d" % i, [128, 4, 4, 64], F32) for i in range(2)]; r_rts = [R_(), R_()]
            qtmp = SB("qtmp", [128, 4, 128], BF16); r_qtmp = R_()
            ross = SB("ross", [128, 4, NOWN], F32); r_ross = R_(); junkE = SB("junkE", [128, 4, 128], F32); r_jkE = [R_(), R_()]
            ysc2e = SB("ysc2e", [128, 128], BF16); r_ysc2e = R_()
            kzb = SB("kzb", [128, 16, 128], BF16); z2 = SB("z2", [128, 18], F32); r_kzb = R_()
            rctr = [0]

            def rope_group(ps, r_ps, c0, G, dst, r_dst):
                i = rctr[0] % 2
                rctr[0] += 1
                rt_, r_rt_ = rts[i], r_rts[i]
                pv3 = ps[:, 0:G * 128].rearrange("p (g x) -> p g x", x=128)
                t1 = pv3[:, :, 0:64]; t2 = pv3[:, :, 64:128]
                cosv = cst_[:, c0:c0 + G, 0:64]; sinv = cst_[:, c0:c0 + G, 64:128]
                r_cs = r_csg[c0]
                P.tt("dve", rt_[:, 0, 0:G, :], t1, cosv, ALU.mult, R=[r_ps, r_cs], W=[r_rt_])
                P.tt("dve", rt_[:, 1, 0:G, :], t2, sinv, ALU.mult, R=[r_ps, r_cs], W=[r_rt_])
                P.tt("dve", rt_[:, 2, 0:G, :], t1, sinv, ALU.mult, R=[r_ps, r_cs], W=[r_rt_])
                P.tt("dve", rt_[:, 3, 0:G, :], t2, cosv, ALU.mult, R=[r_ps, r_cs], W=[r_rt_])
                P.tt("pool", dst[:, :, 0:64], rt_[:, 0, 0:G, :], rt_[:, 1, 0:G, :], ALU.subtract, R=[r_rt_], W=r_dst)
                P.tt("pool", dst[:, :, 64:128], rt_[:, 2, 0:G, :], rt_[:, 3, 0:G, :], ALU.add, R=[r_rt_], W=r_dst)

            kgroups = [(0, 2)] + [(c0, 4) for c0 in range(2, NCH, 4)]
            for h in range(4):
                MK('ret%d_prep' % h)
                cols = [2064 + h * 128, 2576 + h * 128, 3088 + h * 128, 3600 + h * 128]
                for i in range(4):
                    P.dma("pool", rw3[i][:], wv[:, :, cols[i]:cols[i] + 128], W=[r_rw3[i]])
                for (c0, G) in kgroups:
                    pk, r_pk = PS()
                    for g in range(G):
                        c = c0 + g
                        for k in range(8):
                            P.mm(pk[:, g * 128:(g + 1) * 128], hT[:, k, c * 128:(c + 1) * 128], rw3[1][:, k, :], k == 0, k == 7,
                                 R=[r_hT[c], r_rw3[1]], W=[r_pk])
                    rope_group(pk, r_pk, c0, G, ktok[:, c0:c0 + G, :], r_ktok[c0:c0 + G])
                    pt, r_pt = PS(); ptb = pt[:, 0:G * 64].bitcast(BF16)
                    for g in range(G):
                        P.tr(ptb[:, g * 128:(g + 1) * 128], ktok[:, c0 + g, :], idb[:], R=[r_ktok[c0 + g], r_c2], W=[r_pt])
                    P.cp("act", kTr[:, c0:c0 + G, :].rearrange("p g x -> p (g x)"), ptb, R=[r_pt], W=r_kTr[c0:c0 + G])
                    pv, r_pv = PS()
                    for g in range(G):
                        c = c0 + g
                        for k in range(8):
                            P.mm(pv[:, g * 128:(g + 1) * 128], hT[:, k, c * 128:(c + 1) * 128], rw3[2][:, k, :], k == 0, k == 7,
                                 R=[r_hT[c], r_rw3[2]], W=[r_pv])
                    P.cp("act", vtok[:, c0:c0 + G, :].rearrange("p g x -> p (g x)"), pv[:, 0:G * 128], R=[r_pv], W=r_vtok[c0:c0 + G])
                    if OWN0 <= c0 < OWN1:
                        oc0 = c0 - OWN0
                        pq, r_pq = PS()
                        for g in range(G):
                            c = c0 + g
                            for k in range(8):
                                P.mm(pq[:, g * 128:(g + 1) * 128], hT[:, k, c * 128:(c + 1) * 128], rw3[0][:, k, :], k == 0, k == 7,
                                     R=[r_hT[c], r_rw3[0]], W=[r_pq])
                        rope_group(pq, r_pq, c0, G, qtmp[:, 0:G, :], [r_qtmp])
                        pt, r_pt = PS(); ptb = pt[:, 0:G * 64].bitcast(BF16)
                        for g in range(G):
                            P.tr(ptb[:, g * 128:(g + 1) * 128], qtmp[:, g, :], idb[:], R=[r_qtmp, r_c2], W=[r_pt])
                        P.cp("act", qT[:, oc0 * 128:(oc0 + G) * 128], ptb, R=[r_pt], W=[r_qT])
                        pz, r_pz = PS()
                        for g in range(G):
                            c = c0 + g
                            for k in range(8):
                                P.mm(pz[:, g * 128:(g + 1) * 128], hT[:, k, c * 128:(c + 1) * 128], rw3[3][:, k, :], k == 0, k == 7,
                                     R=[r_hT[c], r_rw3[3]], W=[r_pz])
                        P.act(zs[:, oc0:oc0 + G, :].rearrange("p g x -> p (g x)"), pz[:, 0:G * 128], AF.Silu, R=[r_pz], W=[r_zs])
                for d in range(2):
                    j = d * 4 + h
                    lg = lgc[:, j:j + 1]
                    P.act(DrT[d][:], DIST[d], AF.Exp, scale=lg, R=RC + [r_lg], W=[r_rc])
                    P.stt("dve", DrT[d][:], DrT[d][:], float(HD) ** -0.5, MI[d], ALU.mult, ALU.mult, R=[r_rc] + RC, W=[r_rc])
                    P.act(rsc[:, d * 4:d * 4 + 1], csmf[:, d:d + 1], AF.Exp, scale=lg, R=RC + [r_lg], W=[r_rc])
                    P.ts("dve", rsc[:, d * 4:d * 4 + 1], rsc[:, d * 4:d * 4 + 1], float(HD) ** -0.5, None, ALU.mult, R=[r_rc], W=[r_rc])
                    P.act(rsc[:, d * 4 + 1:d * 4 + 2], csmf[:, 2 + d:3 + d], AF.Exp, scale=lg, R=RC + [r_lg], W=[r_rc])
                    P.act(rsc[:, d * 4 + 2:d * 4 + 3], csmf[:, 4:5], AF.Exp, scale=lg, R=RC + [r_lg], W=[r_rc])
                    P.op("pool", lambda e, d=d: e.memset(Sf[d][:], 0.0), W=[r_S[d]])
                    P.op("pool", lambda e, d=d: e.memset(Sb[d][:], 0.0), W=[r_S[d]])
                P.op("pool", lambda e: e.memset(oacc[:], 0.0), W=r_oacc)

                def ret_step(d, c, slot):
                    rs = ring[slot]; r2 = rs["res2"]
                    full = OWN0 <= c < OWN1
                    oc = c - OWN0
                    if full:
                        po1, r_po1 = PS()
                        P.mm(po1[:, 0:128], qT[:, oc * 128:(oc + 1) * 128], Sb[d][:], True, True, R=[r_qT, r_S[d]], W=[r_po1])
                        P.act(rs["o1"][:], po1[:, 0:128], AF.Identity, scale=rsc[:, d * 4:d * 4 + 1], R=[r_po1, r_rc], W=[r2])
                    P.ts("dve", rs["kpp"][:], ktok[:, c, :], rsc[:, d * 4 + 1:d * 4 + 2], None, ALU.mult, R=[r_ktok[c], r_rc], W=[r2])
                    ps_, r_ps = PS()
                    P.mm(ps_[:, 0:128], rs["kpp"][:], vtok[:, c, :], True, True, R=[r2, r_vtok[c]], W=[r_ps])
                    P.stt("dve", Sf[d][:], Sf[d][:], rsc[:, d * 4 + 2:d * 4 + 3], ps_[:, 0:128], ALU.mult, ALU.add, R=[r_S[d], r_ps, r_rc], W=[r_S[d]])
                    P.cp("act", Sb[d][:], Sf[d][:], R=[r_S[d]], W=[r_S[d]])
                    yield
                    if full:
                        pp, r_pp = PS()
                        P.mm(pp[:, 0:128], kTr[:, c, :], qT[:, oc * 128:(oc + 1) * 128], True, True, R=[r_kTr[c], r_qT], W=[r_pp])
                        P.tt("dve", rs["qkm"][:], pp[:, 0:128], DrT[d][:], ALU.mult, R=[r_pp, r_rc], W=[r2])
                        po2, r_po2 = PS()
                        P.mm(po2[:, 0:128], rs["qkm"][:], vtok[:, c, :], True, True, R=[r2, r_vtok[c]], W=[r_po2])
                        P.tt("dve", rs["o1"][:], rs["o1"][:], po2[:, 0:128], ALU.add, R=[r2, r_po2], W=[r2])
                        P.tt("pool", oacc[:, oc, :], oacc[:, oc, :], rs["o1"][:], ALU.add, R=[r2, r_oacc[oc]], W=[r_oacc[oc]])
                        yield

                def ret_batch():
                    lgR = lgc[:, 4 + h:5 + h]
                    P.act(z2[:, 0:16], csmf[:, 8:24], AF.Exp, scale=lgR, R=RC + [r_lg], W=[r_kzb])
                    P.act(z2[:, 16:17], csmf[:, 5:6], AF.Exp, scale=lgR, R=RC + [r_lg], W=[r_kzb])
                    P.tt("pool", kzb[:], ktok[:, OWN1:NCH, :], z2[:, 0:16].unsqueeze(2).to_broadcast([128, 16, 128]), ALU.mult,
                         R=r_ktok[OWN1:NCH] + [r_kzb], W=[r_kzb])
                    yield
                    pb_, r_pb_ = PS()
                    for m in range(16):
                        P.mm(pb_[:, 0:128], kzb[:, m, :], vtok[:, OWN1 + m, :], m == 0, m == 15, R=[r_kzb, r_vtok[OWN1 + m]], W=[r_pb_])
                    P.stt("dve", Sf[1][:], Sf[1][:], z2[:, 16:17], pb_[:, 0:128], ALU.mult, ALU.add, R=[r_S[1], r_pb_, r_kzb], W=[r_S[1]])
                    P.cp("act", Sb[1][:], Sf[1][:], R=[r_S[1]], W=[r_S[1]])
                    yield

                gF = chain([ret_step(0, c, (2 * i) % NR) for i, c in enumerate(chF)])
                gR = chain([ret_step(1, c, 1) for c in (1, 0)] + [ret_batch()]
                           + [ret_step(1, c, (2 * i + 1) % NR) for i, c in enumerate(range(OWN1 - 1, OWN0 - 1, -1))])
                interleave([gR, delayed(gF, 1)])
                for oc in range(NOWN):
                    P.act(junk[:, 0:128], oacc[:, oc, :], AF.Square, accum=ross[:, 1, oc:oc + 1], R=[r_oacc[oc]], W=[r_junk, r_ross])
                    P.act(junk[:, 128:256], oacc[:, oc, :], AF.Identity, accum=ross[:, 0, oc:oc + 1], R=[r_oacc[oc]], W=[r_junk, r_ross])
                P.ts("dve", ross[:, 0, :], ross[:, 0, :], 1.0 / HD, None, ALU.mult, R=[r_ross], W=[r_ross])
                P.tt("dve", ross[:, 2, :], ross[:, 0, :], ross[:, 0, :], ALU.mult, R=[r_ross], W=[r_ross])
                P.stt("dve", ross[:, 1, :], ross[:, 1, :], 1.0 / HD, ross[:, 2, :], ALU.mult, ALU.subtract, R=[r_ross], W=[r_ross])
                P.act(ross[:, 3, :], ross[:, 1, :], AF.Ln, bias=EPS, R=[r_ross], W=[r_ross])
                P.act(ross[:, 3, :], ross[:, 3, :], AF.Exp, scale=-0.5, R=[r_ross], W=[r_ross])
                for oc in range(NOWN):
                    i_ = oc % 2
                    ys_, r_ys_ = (ysc, r_ysc) if i_ == 0 else (ysc2e, r_ysc2e)
                    P.ts("dve", junkE[:, 2 * i_, :], oacc[:, oc, :], ross[:, 0, oc:oc + 1], ross[:, 3, oc:oc + 1], ALU.subtract, ALU.mult,
                         R=[r_oacc[oc], r_ross], W=[r_jkE[i_]])
                    P.tt("pool", junkE[:, 2 * i_ + 1, :], junkE[:, 2 * i_, :], rowf[:, RO_RN:RO_RN + 128], ALU.mult, R=[r_jkE[i_]] + RC, W=[r_jkE[i_]])
                    P.tt("pool", ys_[:], junkE[:, 2 * i_ + 1, :], zs[:, oc, :], ALU.mult, R=[r_jkE[i_], r_zs], W=[r_ys_])
                    pt, r_pt = PS(); ptb = pt[:, 0:64].bitcast(BF16)
                    P.tr(ptb, ys_[:], idb[:], R=[r_ys_, r_c2], W=[r_pt])
                    P.cp("act", yT[:, 4 + h, oc * 128:(oc + 1) * 128], ptb, R=[r_pt], W=[r_yT[4 + h]])
            if dbg:
                dbg_out["yT"] = nc.dram_tensor("d_yT", [128, 8, 2048], BF16, kind="ExternalOutput").ap()
                P.out_dma("sp", dbg_out["yT"][:, :, :], yT[:], R=r_yT)
            P.barrier()

        MK('F')
        with contextlib.ExitStack() as es:
            def SB(name, shape, dt):
                return es.enter_context(nc.sbuf_tensor(name, list(shape), dt))
            wo = SB("wo", [128, 8, DM], BF16); r_wo = R_()
            fgrow = SB("fgrow", [128, DM], F32); r_fg = R_()
            P.dma("sp", fgrow[:], rowc[:, RO_FG:RO_FG + DM], W=[r_fg])
            wfo = SB("wfo", [128, 22, DM], BF16); r_wfo = R_()
            wov = w_out.rearrange("(k p) n -> p k n", p=128); wfov = w_fo.rearrange("(k p) n -> p k n", p=128)
            wfiv = w_fi.rearrange("(k p) n -> p k n", p=128)
            wst = [SB("wst%d" % i, [128, 8, 512], BF16) for i in range(2)]; r_wst = [R_(), R_()]
            x1_ = SB("x1_", [128, 4, DM], F32); x1 = [x1_, x1_]; r_x1_ = R_(); r_x1 = [r_x1_, r_x1_]
            h2T_ = SB("h2T", [128, 8, 512], BF16); h2T = [h2T_, h2T_]; r_h2_ = R_(); r_h2 = [r_h2_, r_h2_]
            aT = SB("aT", [128, 22, 512], BF16); r_aT = R_()
            xob = [SB("xob%d" % i, [128, DM], F32) for i in range(2)]; r_xob = [R_(), R_()]
            ntmp = [(SB("fsq%d" % i, [128, DM], BF16), SB("fss%d" % i, [128, 4], F32), SB("fxn%d" % i, [128, DM], BF16), R_()) for i in range(2)]
            sg = [SB("sg%d" % i, [128, 4, 512], BF16) for i in range(2)]; r_sg = [R_(), R_()]
            ob = [SB("ob%d" % i, [128, DM], F32) for i in range(2)]; r_ob = [R_(), R_()]
            for k in range(8):
                i = k % 2
                P.dma("sp", ob[i][:], wov[:, k, :], W=[r_ob[i]])
                P.tt("dve" if i else "pool", wo[:, k, :], ob[i][:], g1row[:], ALU.mult, R=[r_ob[i], r_grow], W=[r_wo])
            nst = 0
            for grp in range(4):
                gi = grp % 2
                for bb in range(4):
                    blk = grp * 4 + bb
                    i = blk % 2
                    P.dma("sp", xob[i][:], xo[blk * 128:(blk + 1) * 128, :], W=[r_xob[i]])
                    for hf in range(2):
                        pp, r_pp = PS()
                        for k in range(8):
                            P.mm(pp[:, :], yT[:, k, blk * 128:(blk + 1) * 128], wo[:, k, hf * 512:(hf + 1) * 512], k == 0, k == 7,
                                 R=[r_yT[k], r_wo], W=[r_pp])
                        P.tt("dve", x1[gi][:, bb, hf * 512:(hf + 1) * 512], pp[:, :], xob[i][:, hf * 512:(hf + 1) * 512], ALU.add,
                             R=[r_pp, r_xob[i]], W=[r_x1[gi]])
                    norm_block(x1[gi][:, bb, :], r_x1[gi], h2T[gi], r_h2[gi], bb * 128,
                               lambda k: sc2[:, k, 0:1], lambda k: sc2[:, k, 1:2], ntmp[i])
                def wfo_chunk(k):
                    i = k % 2
                    P.dma("sp", ob[i][:], wfov[:, k, :], W=[r_ob[i]])
                    P.tt("dve", wfo[:, k, :], ob[i][:], g2row[:], ALU.mult, R=[r_ob[i], r_grow], W=[r_wfo])
                for j0 in range(0, 22, 4):
                    nj = min(4, 22 - j0)
                    si = (j0 // 4) % 2
                    wi = nst % 2
                    nst += 1
                    P.dma("pool", wst[wi][:, :, 0:nj * 128], wfiv[:, :, j0 * 128:(j0 + nj) * 128], W=[r_wst[wi]])
                    for jj in range(nj):
                        pg_, r_pg_ = PS()
                        for k in range(8):
                            P.mm(pg_[:, :], wst[wi][:, k, jj * 128:(jj + 1) * 128], h2T[gi][:, k, :], k == 0, k == 7, R=[r_wst[wi], r_h2[gi]], W=[r_pg_])
                        P.act(sg[si][:, jj, :], pg_[:, :], AF.Silu, R=[r_pg_], W=[r_sg[si]])
                    wi = nst % 2
                    nst += 1
                    P.dma("pool", wst[wi][:, :, 0:nj * 128], wfiv[:, :, DFF + j0 * 128:DFF + (j0 + nj) * 128], W=[r_wst[wi]])
                    for jj in range(nj):
                        pu, r_pu = PS()
                        for k in range(8):
                            P.mm(pu[:, :], wst[wi][:, k, jj * 128:(jj + 1) * 128], h2T[gi][:, k, :], k == 0, k == 7, R=[r_wst[wi], r_h2[gi]], W=[r_pu])
                        P.tt("dve", aT[:, j0 + jj, :], pu[:, :], sg[si][:, jj, :], ALU.mult, R=[r_pu, r_sg[si]], W=[r_aT])
                        if grp == 0:
                            wfo_chunk(j0 + jj)
                for bb in range(4):
                    blk = grp * 4 + bb
                    oi = blk % 2
                    for hf in range(2):
                        pp, r_pp = PS()
                        for j in range(22):
                            P.mm(pp[:, :], aT[:, j, bb * 128:(bb + 1) * 128], wfo[:, j, hf * 512:(hf + 1) * 512], j == 0, j == 21,
                                 R=[r_aT, r_wfo], W=[r_pp])
                        P.tt("dve", x1[gi][:, bb, hf * 512:(hf + 1) * 512], pp[:, :], x1[gi][:, bb, hf * 512:(hf + 1) * 512], ALU.add,
                             R=[r_pp, r_x1[gi]], W=[r_x1[gi]])
                    sq, ss, xn, r_t = ntmp[oi]
                    P.act(sq[:], x1[gi][:, bb, :], AF.Square, accum=ss[:, 0:1], R=[r_x1[gi]], W=[r_t])
                    P.act(ss[:, 1:2], ss[:, 0:1], AF.Ln, bias=EPS, scale=1.0 / DM, R=[r_t], W=[r_t])
                    P.act(ss[:, 2:3], ss[:, 1:2], AF.Exp, scale=-0.5, R=[r_t], W=[r_t])
                    P.stt("dve", ob[oi][:], x1[gi][:, bb, :], ss[:, 2:3], fgrow[:], ALU.mult, ALU.mult,
                          R=[r_x1[gi], r_t, r_fg], W=[r_ob[oi]])
                    P.out_dma("sp", out[blk * 128:(blk + 1) * 128, :], ob[oi][:], R=[r_ob[oi]])
            MK('end')
            P.finish()
    return nc, dbg_out


def _flipseq(a):
    return np.concatenate([a[:CTX][::-1], a[CTX:][::-1]], axis=0)


def _rope_tables():
    rows = LAT // 64
    row = np.repeat(np.arange(rows, dtype=np.float32), 64)
    col = np.tile(np.arange(64, dtype=np.float32), rows)
    z = np.zeros(CTX, np.float32)
    p_seq = np.concatenate([np.arange(CTX, dtype=np.float32), np.full(LAT, float(CTX), np.float32)])
    p_row = np.concatenate([z, row]); p_col = np.concatenate([z, col])

    def aa(pos, n):
        inv = (np.float32(10000.0) ** (-np.arange(n, dtype=np.float32) / np.float32(n))).astype(np.float32)
        return pos[:, None] * inv[None, :]
    ang = np.concatenate([aa(p_seq, 16), aa(p_row, 24), aa(p_col, 24)], axis=-1).astype(np.float32)
    return np.cos(ang).astype(np.float32), np.sin(ang).astype(np.float32)


def _consts():
    i = np.arange(C)
    blocks = [np.eye(C, dtype=np.float32)]
    MIs, MSs, LMs = [], [], []
    for d in range(2):
        incl = (i[:, None] <= i[None, :]) if d == 0 else (i[:, None] >= i[None, :])
        MIs.append(incl.astype(np.float32))
    for d in range(2):
        st = (i[None, :] < i[:, None]) if d == 0 else (i[None, :] > i[:, None])
        MSs.append(st.astype(np.float32))
    for d in range(2):
        for l in range(1, 8):
            b = 2 ** (l - 1)
            blk = (i[:, None] // (2 * b)) == (i[None, :] // (2 * b))
            ih = (i // b) % 2
            m = blk & (ih[:, None] == 1) & (ih[None, :] == 0)
            if d == 1:
                m = m.T
            LMs.append(m.astype(np.float32))
    dist = np.abs(i[:, None] - i[None, :]).astype(np.float32)
    blocks += MIs + MSs + LMs + [dist, dist, np.ones((C, C), np.float32)]
    cst = np.concatenate(blocks, axis=1)
    pos = i.astype(np.float32)
    csm = np.zeros((128, 24), np.float32)
    csm[:, 0] = pos + 1.0
    csm[:, 1] = (C - 1 - pos) + 1.0
    csm[:, 2] = C - 1.0 - pos
    csm[:, 3] = pos
    csm[:, 4] = float(C)
    csm[:, 5] = float(C * 16)
    for m in range(16):
        csm[:, 8 + m] = pos + float(C * m)
    return np.ascontiguousarray(cst), csm


def _core_inputs(inp, b, p, shared):
    f = np.float32
    xs = np.concatenate([inp["ctx"][b], inp["x"][b]], axis=0)
    if p == 1:
        xs = _flipseq(xs)
    xs = np.ascontiguousarray(xs, dtype=f)
    pi = [p, 1 - p]
    w_in = inp["w_in"][0]
    ab0 = 2048
    a_cols = [w_in[:, ab0 + 4 * k: ab0 + 4 * k + 4] for k in range(4)]
    w_ab = np.concatenate([a_cols[pi[0]], a_cols[pi[1]], a_cols[2 + pi[0]], a_cols[2 + pi[1]]], axis=1)
    cw = inp["conv_w"][0]
    if p == 1:
        cw = cw[::-1]
    convT = np.ascontiguousarray(cw.reshape(5, 12, 128).transpose(2, 1, 0), dtype=f)
    cvec = np.concatenate([inp["c"][b].reshape(8, 128).T, inp["c_ctx"].reshape(8, 128).T], axis=1)
    rowc = np.zeros((NRO,), f)
    rowc[RO_GN:RO_GN + 128] = inp["gdn_norm_g"][0]
    rowc[RO_RN:RO_RN + 128] = inp["ret_norm_g"][0]
    rowc[RO_FG:RO_FG + DM] = inp["final_g"]
    for d in range(2):
        rowc[RO_AL + d * 4:RO_AL + d * 4 + 4] = inp["gdn_a_log"][0][pi[d]]
        rowc[RO_DTB + d * 4:RO_DTB + d * 4 + 4] = inp["gdn_dt_bias"][0][pi[d]]
        rowc[RO_RL + d * 4:RO_RL + d * 4 + 4] = inp["ret_decay_logit"][0][pi[d]]
    rowc[RO_B1:RO_B1 + DM] = inp["ada_b"][0][2 * DM:3 * DM]
    rowc[RO_B2:RO_B2 + DM] = inp["ada_b"][0][5 * DM:6 * DM]
    rowc = np.ascontiguousarray(np.broadcast_to(rowc[None, :], (128, NRO)))
    cos, sin = shared["rope"]
    if p == 1:
        cos, sin = _flipseq(cos), _flipseq(sin)
    cs = np.concatenate([cos, sin], axis=1).reshape(NCH, 128, 128).transpose(1, 0, 2)
    return {
        "xs": xs, "xo": np.ascontiguousarray(xs[CTX:CTX + 2048]),
        "cvec": np.ascontiguousarray(cvec, dtype=f), "ada_w": shared["ada_w"], "adabT": shared["adabT"],
        "gmixT": shared["gmixT"], "gffnT": shared["gffnT"], "w_in": shared["w_in"],
        "w_ab": np.ascontiguousarray(w_ab, dtype=f), "convT": convT, "rowc": rowc,
        "cs": np.ascontiguousarray(cs, dtype=f), "cst": shared["cst"], "csm": shared["csm"],
        "w_out": shared["w_out"], "w_fi": shared["w_fi"], "w_fo": shared["w_fo"],
    }


def _shared(inp):
    f = np.float32
    cst, csm = _consts()
    return {
        "rope": _rope_tables(), "cst": cst, "csm": csm,
        "ada_w": np.ascontiguousarray(inp["ada_w"][0], dtype=f),
        "adabT": np.ascontiguousarray(inp["ada_b"][0].reshape(48, 128).T, dtype=f),
        "gmixT": np.ascontiguousarray(inp["norm_mix_g"][0].reshape(8, 128).T, dtype=f),
        "gffnT": np.ascontiguousarray(inp["norm_ffn_g"][0].reshape(8, 128).T, dtype=f),
        "w_in": np.ascontiguousarray(inp["w_in"][0], dtype=f), "w_out": np.ascontiguousarray(inp["w_out"][0], dtype=f),
        "w_fi": np.ascontiguousarray(inp["w_ffn_in"][0], dtype=f), "w_fo": np.ascontiguousarray(inp["w_ffn_out"][0], dtype=f),
    }


def kernel(**inputs):
    inp = {k: np.asarray(v) for k, v in inputs.items()}
    sh = _shared(inp)
    in_maps = [_core_inputs(inp, core // 2, core % 2, sh) for core in range(8)]
    nc, _ = build(False)
    res = run_bass_kernel_spmd(nc, in_maps, core_ids=list(range(8)))
    outp = np.zeros((4, LAT, DM), np.float32)
    for core in range(8):
        b, p = core // 2, core % 2
        o = np.asarray(res.results[core]["out"], dtype=np.float32)
        if p == 0:
            outp[b, :2048] = o
        else:
            outp[b, 2048:] = o[::-1]
    return outp
```

```python
import contextlib
import numpy as np
import concourse.bass as bass
import concourse.mybir as mybir
from concourse.bass_utils import run_bass_kernel_spmd

F32 = mybir.dt.float32
BF16 = mybir.dt.bfloat16
AF = mybir.ActivationFunctionType
ALU = mybir.AluOpType
AX = mybir.AxisListType

NDMA_SEM = 8
ALLSIG = False
STRICT = True


class Res:
    __slots__ = ("name", "w", "r", "rd")

    def __init__(self, name=""):
        self.name = name
        self.w = None
        self.r = {}
        self.rd = []


class Op:
    __slots__ = ("eng", "fn", "deps", "sig", "sidx", "dma", "didx", "pos", "pseudo")

    def __init__(self, eng, fn, dma):
        self.eng = eng
        self.fn = fn
        self.dma = dma
        self.deps = []
        self.sig = False
        self.sidx = 0
        self.didx = 0
        self.pos = 0
        self.pseudo = False


class Prog:
    ENGS = ("pe", "act", "dve", "pool", "sp")
    BLK = {"pe": "tensor", "act": "scalar", "dve": "vector", "pool": "gpsimd", "sp": "sync"}

    def __init__(self, nc):
        self.nc = nc
        self.ops = {e: [] for e in self.ENGS}
        self.stack = contextlib.ExitStack()
        self.nres = 0
        self.out_res = []
        self._bar_pos = {}

    def __enter__(self):
        self.stack.__enter__()
        self.esem = {e: self.stack.enter_context(self.nc.semaphore("s_" + e)) for e in self.ENGS}
        self.dsem = {e: [self.stack.enter_context(self.nc.semaphore("d_%s%d" % (e, i))) for i in range(NDMA_SEM)]
                     for e in ("sp", "pool", "act")}
        return self

    def __exit__(self, *a):
        return self.stack.__exit__(*a)

    def sb(self, name, shape, dt):
        return self.stack.enter_context(self.nc.sbuf_tensor(name, list(shape), dt))

    def ps(self, name, shape=(128, 512), dt=F32):
        return self.stack.enter_context(self.nc.psum_tensor(name, list(shape), dt))

    def res(self, name=""):
        self.nres += 1
        return Res(name)

    def op(self, eng, fn, R=(), W=(), dma=False):
        o = Op(eng, fn, dma)
        lst = self.ops[eng]
        o.pos = len(lst)
        deps = {}

        def add(d, raw):
            if d is None or d is o:
                return
            if d.dma or o.dma or d.eng != eng:
                deps[id(d)] = d
            elif eng != "pe" and (raw or STRICT):
                deps[id(d)] = d

        for r in R:
            add(r.w, True)
        for w in W:
            add(w.w, False)
            for rr in w.r.values():
                add(rr, False)
            for rr in w.rd:
                add(rr, False)
        best = {}
        out = []
        for d in deps.values():
            if d.dma:
                out.append(d)
            else:
                b = best.get(d.eng)
                if b is None or d.pos > b.pos:
                    best[d.eng] = d
        out.extend(best.values())
        o.deps = out
        for d in out:
            d.sig = True
        for r in R:
            if dma:
                r.rd.append(o)
            else:
                r.r[eng] = o
        for w in W:
            w.w = o
            w.r = {}
            w.rd = []
        lst.append(o)
        return o

    def dma(self, eng, out, in_, R=(), W=()):
        return self.op(eng, lambda e: e.dma_start(out=out, in_=in_), R, W, dma=True)

    def mm(self, out, lhsT, rhs, start, stop, R=(), W=()):
        return self.op("pe", lambda e: e.matmul(out, lhsT, rhs, start=start, stop=stop), R, W)

    def tr(self, out, in_, ident, R=(), W=()):
        return self.op("pe", lambda e: e.transpose(out, in_, ident), R, W)

    def act(self, out, in_, func, bias=None, scale=None, accum=None, R=(), W=()):
        kw = {}
        if bias is not None:
            kw["bias"] = bias
        if scale is not None:
            kw["scale"] = scale
        if accum is not None:
            kw["accum_out"] = accum
        return self.op("act", lambda e: e.activation(out, in_, func, **kw), R, W)

    def tt(self, eng, out, a, b, op, R=(), W=()):
        return self.op(eng, lambda e: e.tensor_tensor(out, a, b, op), R, W)

    def ts(self, eng, out, a, s1, s2, op0, op1=None, R=(), W=()):
        if op1 is None:
            return self.op(eng, lambda e: e.tensor_scalar(out, a, s1, None, op0), R, W)
        return self.op(eng, lambda e: e.tensor_scalar(out, a, s1, s2, op0, op1), R, W)

    def stt(self, eng, out, in0, scalar, in1, op0, op1, R=(), W=()):
        return self.op(eng, lambda e: e.scalar_tensor_tensor(out, in0, scalar, in1, op0, op1), R, W)

    def cp(self, eng, out, in_, R=(), W=()):
        if eng == "act":
            return self.op(eng, lambda e: e.copy(out, in_), R, W)
        return self.op(eng, lambda e: e.tensor_copy(out, in_), R, W)

    def out_dma(self, eng, out, in_, R=()):
        r = self.res("out")
        self.out_res.append(r)
        return self.dma(eng, out, in_, R=R, W=[r])

    def barrier(self):
        lasts = []
        for e in self.ENGS:
            for o_ in reversed(self.ops[e]):
                if not o_.pseudo and not o_.dma:
                    lasts.append(o_)
                    break
        dmas = [o for e in self.ENGS for o in self.ops[e][self._bar_pos.get(e, 0):] if o.dma]
        self._bar_pos = {e: len(self.ops[e]) for e in self.ENGS}
        for e in self.ENGS:
            o = Op(e, lambda eng: None, False)
            o.pseudo = True
            o.pos = len(self.ops[e])
            deps = [d for d in lasts if d.eng != e and not d.dma] + dmas
            o.deps = deps
            for d in deps:
                d.sig = True
            self.ops[e].append(o)

    def finish(self):
        fo = self.op("sp", lambda e: None, R=self.out_res)
        fo.pseudo = True
        for e in self.ENGS:
            cnt = 0
            nd = 0
            for o in self.ops[e]:
                if ALLSIG and e in ("dve", "act", "pool") and not o.pseudo and not o.dma:
                    o.sig = True
                if o.dma:
                    o.didx = nd
                    nd += 1
                elif o.sig:
                    assert not o.pseudo
                    cnt += 1
                    o.sidx = cnt
        with self.nc.Block() as block:
            for e in self.ENGS:
                getattr(block, self.BLK[e])(lambda eng, e=e: self._emit(e, eng))

    def _emit(self, e, eng):
        known = {}
        K = NDMA_SEM
        for o in self.ops[e]:
            waits = {}
            for d in o.deps:
                if d.dma:
                    sem = self.dsem[d.eng][d.didx % K]
                    val = 16 * (d.didx // K + 1)
                else:
                    sem = self.esem[d.eng]
                    val = d.sidx
                k = sem.num
                if waits.get(k, (None, 0))[1] < val:
                    waits[k] = (sem, val)
            if o.dma and o.didx >= K:
                sem = self.dsem[e][o.didx % K]
                val = 16 * (o.didx // K)
                if waits.get(sem.num, (None, 0))[1] < val:
                    waits[sem.num] = (sem, val)
            for k, (sem, val) in waits.items():
                if known.get(k, 0) < val:
                    eng.wait_ge(sem, val)
                    known[k] = val
            inst = o.fn(eng)
            if inst is None:
                continue
            if o.dma:
                inst.then_inc(self.dsem[e][o.didx % K], 16)
            elif o.sig:
                inst.then_inc(self.esem[e], 1)


CTX, LAT, DM, HD, C = 256, 4096, 1024, 128, 128
T = CTX + LAT
NCH = T // C
OWN0, OWN1 = 2, 18
NOWN = 16
DFF = 2816
EPS = 1e-6
U8 = mybir.dt.uint8

CB_I, CB_MI, CB_MS, CB_LM, CB_DIST, CB_ONES, NCB = 0, 1, 3, 5, 19, 21, 22
RO_GN, RO_RN, RO_AL, RO_DTB, RO_RL, NRS = 0, 128, 256, 264, 272, 280
RO_FG, RO_B1, RO_B2, NRO = 280, 1304, 2328, 3352


def interleave(gens):
    gens = list(gens)
    while gens:
        nxt = []
        for g in gens:
            try:
                next(g)
                nxt.append(g)
            except StopIteration:
                pass
        gens = nxt


def chain(gs):
    for g in gs:
        yield from g


def delayed(g, n):
    for _ in range(n):
        yield
    yield from g


MARKS = []


def build(dbg=False):
    nc = bass.Bass("TRN2", target_bir_lowering=False)

    def IN(name, shape):
        return nc.dram_tensor(name, list(shape), F32, kind="ExternalInput").ap()

    xs = IN("xs", [T, DM]); xo = IN("xo", [2048, DM])
    cvec = IN("cvec", [128, 16]); ada_w = IN("ada_w", [DM, 6 * DM]); adabT = IN("adabT", [128, 48])
    gmixT = IN("gmixT", [128, 8]); gffnT = IN("gffnT", [128, 8])
    w_in = IN("w_in", [DM, 4112]); w_ab = IN("w_ab", [DM, 16]); convT = IN("convT", [128, 12, 5])
    rowc = IN("rowc", [128, NRO]); cs = IN("cs", [128, NCH, 128]); cst = IN("cst", [128, NCB * 128]); csm = IN("csm", [128, 24])
    w_out = IN("w_out", [DM, DM]); w_fi = IN("w_fi", [DM, 2 * DFF]); w_fo = IN("w_fo", [DFF, DM])
    out = nc.dram_tensor("out", [2048, DM], F32, kind="ExternalOutput").ap()
    dbg_out = {}

    P = Prog(nc)
    MARKS.clear()
    def MK(name):
        MARKS.append((name, len(P.ops['pe'])))
    with P:
        R_ = P.res
        cstf = P.sb("cstf", [128, 8 * 128], F32); r_cst = R_()
        P.dma("sp", cstf[:, 0:640], cst[:, 0:640], W=[r_cst])
        P.dma("sp", cstf[:, 640:1024], cst[:, CB_DIST * 128:NCB * 128], W=[r_cst])
        csmf = P.sb("csmf", [128, 24], F32)
        P.dma("sp", csmf[:], csm[:, :], W=[r_cst])
        rowf = P.sb("rowf", [128, NRS], F32)
        P.dma("sp", rowf[:], rowc[:, 0:NRS], W=[r_cst])
        convf = P.sb("convf", [128, 12, 5], F32)
        P.dma("sp", convf[:], convT[:, :, :], W=[r_cst])
        lm8 = P.sb("lm8", [128, 14, 2, 128], U8); idb = P.sb("idb", [128, 128], BF16); r_c2 = R_()
        with nc.sbuf_tensor("lmf", [128, 14 * 128], F32) as lmf:
            r_lmf = R_()
            P.dma("sp", lmf[:], cst[:, CB_LM * 128:(CB_LM + 14) * 128], W=[r_lmf])
            for g_ in range(2):
                P.cp("dve", lm8[:, :, g_, :], lmf[:].rearrange("p (a x) -> p a x", x=128), R=[r_lmf], W=[r_c2])
            P.barrier()
        P.cp("dve", idb[:], cstf[:, 0:128], R=[r_cst], W=[r_c2])
        RC = [r_cst, r_c2]

        def cb(i):
            if i >= CB_DIST:
                i -= 14
            return cstf[:, i * 128:(i + 1) * 128]
        I_f = cb(CB_I); ones_f = cb(CB_ONES)
        MI = [cb(CB_MI), cb(CB_MI + 1)]; MS = [cb(CB_MS), cb(CB_MS + 1)]; DIST = [cb(CB_DIST), cb(CB_DIST + 1)]

        def LM8(d, l):
            return lm8[:, d * 7 + (l - 1), :, :]

        banks = [P.ps("bank%d" % i) for i in range(8)]
        bres = [R_() for _ in range(8)]
        bctr = [0]

        def PS():
            i = bctr[0] % 8
            bctr[0] += 1
            return banks[i], bres[i]

        modT = P.sb("modT", [128, 48, 2], F32); r_mod = R_()
        g1row = P.sb("g1row", [128, DM], BF16); g2row = P.sb("g2row", [128, DM], BF16); r_grow = R_()
        sc1 = P.sb("sc1", [128, 8, 4], F32)
        sc2 = P.sb("sc2", [128, 8, 2], F32); r_sc = R_()
        with contextlib.ExitStack() as es:
            adw = [es.enter_context(nc.sbuf_tensor("adw%d" % i, [128, 8, 512], F32)) for i in range(2)]
            r_adw = [R_(), R_()]
            cv = es.enter_context(nc.sbuf_tensor("cv", [128, 16], F32)); r_cv = R_()
            scv = es.enter_context(nc.sbuf_tensor("scv", [128, 8, 2], F32))
            mrow = es.enter_context(nc.sbuf_tensor("mrow", [2, 6 * DM], F32)); r_mrow = R_()
            P.dma("sp", cv[:], cvec[:, :], W=[r_cv])
            brow = es.enter_context(nc.sbuf_tensor("brow", [128, 2048], F32))
            P.dma("sp", brow[:], rowc[:, RO_B1:RO_B1 + 2048], W=[r_cv])
            P.act(scv[:].rearrange("p k t -> p t k"), cv[:].rearrange("p (t k) -> p t k", t=2), AF.Silu, R=[r_cv], W=[r_cv])
            adv = ada_w.rearrange("(k p) n -> p k n", p=128)
            for blk in range(12):
                bi = blk % 2
                P.dma("sp", adw[bi][:], adv[:, :, blk * 512:(blk + 1) * 512], W=[r_adw[bi]])
                pr, r_pr = banks[1 + blk % 2], bres[1 + blk % 2]
                for k in range(8):
                    P.mm(pr[0:2, :], scv[:, k, :], adw[bi][:, k, :], k == 0, k == 7, R=[r_adw[bi], r_cv], W=[r_pr])
                P.cp("act", mrow[0:2, blk * 512:(blk + 1) * 512], pr[0:2, :], R=[r_pr], W=[r_mrow])
            pm, r_pm = banks[0], bres[0]
            for n in range(48):
                P.tr(pm[:, n * 2:n * 2 + 2], mrow[0:2, n * 128:(n + 1) * 128], I_f[0:2, 0:2], R=[r_mrow] + RC, W=[r_pm])
            for gi_, (dstrow, col0) in enumerate(((g1row, 2 * DM), (g2row, 5 * DM))):
                for hf in range(2):
                    pg, r_pg = banks[3 + hf], bres[3 + hf]
                    P.mm(pg[:, :], ones_f[0:1, 0:128], mrow[0:1, col0 + hf * 512:col0 + (hf + 1) * 512], True, True, R=[r_mrow] + RC, W=[r_pg])
                    P.tt("dve", dstrow[:, hf * 512:(hf + 1) * 512], pg[:, :], brow[:, gi_ * 1024 + hf * 512:gi_ * 1024 + (hf + 1) * 512], ALU.add,
                         R=[r_pg, r_cv], W=[r_grow])
            abT = es.enter_context(nc.sbuf_tensor("abT", [128, 48], F32))
            gm = es.enter_context(nc.sbuf_tensor("gm", [128, 16], F32))
            P.dma("sp", abT[:], adabT[:, :], W=[r_cv])
            P.dma("sp", gm[:, 0:8], gmixT[:, :], W=[r_cv])
            P.dma("sp", gm[:, 8:16], gffnT[:, :], W=[r_cv])
            P.tt("dve", modT[:], pm[:, 0:96].rearrange("p (n t) -> p n t", t=2), abT[:].unsqueeze(2).to_broadcast([128, 48, 2]),
                 ALU.add, R=[r_pm, r_cv], W=[r_mod])
            P.stt("dve", sc1[:, :, 0], modT[:, 8:16, 0], 1.0, gm[:, 0:8], ALU.add, ALU.mult, R=[r_mod, r_cv], W=[r_sc])
            P.cp("dve", sc1[:, :, 1], modT[:, 0:8, 0], R=[r_mod], W=[r_sc])
            P.stt("dve", sc1[:, :, 2], modT[:, 8:16, 1], 1.0, gm[:, 0:8], ALU.add, ALU.mult, R=[r_mod, r_cv], W=[r_sc])
            P.cp("dve", sc1[:, :, 3], modT[:, 0:8, 1], R=[r_mod], W=[r_sc])
            P.stt("dve", sc2[:, :, 0], modT[:, 32:40, 0], 1.0, gm[:, 8:16], ALU.add, ALU.mult, R=[r_mod, r_cv], W=[r_sc])
            P.cp("dve", sc2[:, :, 1], modT[:, 24:32, 0], R=[r_mod], W=[r_sc])
            P.barrier()

        def norm_block(src, r_src, dstT, r_dst, col0, scl, shf, tmp):
            sq, ss, xn, r_t = tmp
            P.act(sq[:], src, AF.Square, accum=ss[:, 0:1], R=[r_src], W=[r_t])
            P.act(ss[:, 1:2], ss[:, 0:1], AF.Ln, bias=EPS, scale=1.0 / DM, R=[r_t], W=[r_t])
            P.act(ss[:, 2:3], ss[:, 1:2], AF.Exp, scale=-0.5, R=[r_t], W=[r_t])
            P.ts("dve", xn[:], src, ss[:, 2:3], None, ALU.mult, R=[r_src, r_t], W=[r_t])
            pt, r_pt = PS()
            ptb = pt[:, :].bitcast(BF16)
            for k in range(8):
                P.tr(ptb[:, k * 128:(k + 1) * 128], xn[:, k * 128:(k + 1) * 128], idb[:], R=[r_t, r_c2], W=[r_pt])
            for k in range(8):
                if k % 2 == 0:
                    P.act(dstT[:, k, col0:col0 + 128], ptb[:, k * 128:(k + 1) * 128], AF.Identity, bias=shf(k), scale=scl(k),
                          R=[r_pt, r_sc], W=[r_dst])
                else:
                    P.ts("dve", dstT[:, k, col0:col0 + 128], ptb[:, k * 128:(k + 1) * 128], scl(k), shf(k), ALU.mult, ALU.add,
                         R=[r_pt, r_sc], W=[r_dst])

        yT = P.sb("yT", [128, 8, 2048], BF16); r_yT = [R_() for _ in range(8)]
        wv = w_in.rearrange("(k p) n -> p k n", p=128)

        with contextlib.ExitStack() as es:
            def SB(name, shape, dt):
                return es.enter_context(nc.sbuf_tensor(name, list(shape), dt))
            hT = SB("hT", [128, 8, T], BF16); r_hT = [R_() for _ in range(NCH)]
            MK('B')
            esB = contextlib.ExitStack()
            def SBB(name, shape, dt):
                return esB.enter_context(nc.sbuf_tensor(name, list(shape), dt))
            xts = [SBB("xt%d" % i, [128, DM], F32) for i in range(2)]; r_xt = [R_(), R_()]
            ntmp = [(SBB("nsq%d" % i, [128, DM], BF16), SBB("nss%d" % i, [128, 4], F32), SBB("nxn%d" % i, [128, DM], BF16), R_()) for i in range(2)]
            for c in range(NCH):
                i = c % 2
                P.dma("sp", xts[i][:], xs[c * 128:(c + 1) * 128, :], W=[r_xt[i]])
                o_ = 2 if c < 2 else 0
                norm_block(xts[i][:], r_xt[i], hT, r_hT[c], c * 128,
                           lambda k, o_=o_: sc1[:, k, o_:o_ + 1], lambda k, o_=o_: sc1[:, k, o_ + 1:o_ + 2], ntmp[i])
            if dbg:
                dbg_out["hT"] = nc.dram_tensor("d_hT", [128, 8, T], BF16, kind="ExternalOutput").ap()
                P.out_dma("sp", dbg_out["hT"][:, :, :], hT[:], R=r_hT)

            P.barrier()
            esB.close()
            MK('C')
            wab = SB("wab", [128, 8, 16], BF16); r_wab = R_()
            P.dma("pool", wab[:], w_ab.rearrange("(k p) n -> p k n", p=128), W=[r_wab])
            abs_ = SB("abs", [128, 16, NCH], F32); r_ab = R_()
            pa, r_pa = PS(); pb, r_pb = PS()
            for c in range(NCH):
                tgt = pa[:, c * 16:(c + 1) * 16] if c < 32 else pb[:, (c - 32) * 16:(c - 31) * 16]
                for k in range(8):
                    P.mm(tgt, hT[:, k, c * 128:(c + 1) * 128], wab[:, k, :], k == 0, k == 7, R=[r_hT[c], r_wab],
                         W=[r_pa if c < 32 else r_pb])
            P.cp("dve", abs_[:, :, 0:32].rearrange("p j c -> p c j"), pa[:, 0:512].rearrange("p (c j) -> p c j", j=16), R=[r_pa], W=[r_ab])
            P.cp("dve", abs_[:, :, 32:34].rearrange("p j c -> p c j"), pb[:, 0:32].rearrange("p (c j) -> p c j", j=16), R=[r_pb], W=[r_ab])
            gal = SB("gal", [128, 8, NCH], F32); sal = SB("sal", [128, 8, NCH], F32); Gal = SB("Gal", [128, 8, NCH], F32)
            eG = SB("eG", [128, 8, NCH], F32); neG = SB("neG", [128, 8, NCH], F32); eGl = SB("eGl", [128, 8, NCH], F32)
            ekd = SB("ekd", [128, 8, NCH], F32); nA8 = SB("nA8", [128, 8], F32); r_g = R_()
            P.tt("dve", gal[:], abs_[:, 0:8, :], rowf[:, RO_DTB:RO_DTB + 8].unsqueeze(2).to_broadcast([128, 8, NCH]), ALU.add, R=[r_ab] + RC, W=[r_g])
            P.act(gal[:], gal[:], AF.Exp, R=[r_g], W=[r_g])
            P.act(gal[:], gal[:], AF.Ln, bias=1.0, R=[r_g], W=[r_g])
            P.act(nA8[:], rowf[:, RO_AL:RO_AL + 8], AF.Exp, R=RC, W=[r_g])
            P.stt("dve", gal[:], gal[:], -1.0, nA8[:].unsqueeze(2).to_broadcast([128, 8, NCH]), ALU.mult, ALU.mult, R=[r_g], W=[r_g])
            P.act(sal[:], abs_[:, 8:16, :], AF.Sigmoid, R=[r_ab], W=[r_g])
            P.act(sal[:], sal[:], AF.Sqrt, R=[r_g], W=[r_g])
            galf = gal[:].rearrange("p a c -> p (a c)")
            for d in range(2):
                pg, r_pg = PS()
                P.mm(pg[:, 0:4 * NCH], MI[d], galf[:, d * 4 * NCH:(d + 1) * 4 * NCH], True, True, R=[r_g] + RC, W=[r_pg])
                P.cp("dve", Gal[:, d * 4:(d + 1) * 4, :].rearrange("p a c -> p (a c)"), pg[:, 0:4 * NCH], R=[r_pg], W=[r_g])
            pg, r_pg = PS()
            P.mm(pg[:, 0:8 * NCH], ones_f, galf, True, True, R=[r_g] + RC, W=[r_pg])
            P.act(eGl[:].rearrange("p a c -> p (a c)"), pg[:, 0:8 * NCH], AF.Exp, R=[r_pg], W=[r_g])
            P.tt("dve", ekd[:].rearrange("p a c -> p (a c)"), pg[:, 0:8 * NCH], Gal[:].rearrange("p a c -> p (a c)"), ALU.subtract, R=[r_pg, r_g], W=[r_g])
            P.act(ekd[:], ekd[:], AF.Exp, R=[r_g], W=[r_g])
            P.act(eG[:], Gal[:], AF.Exp, R=[r_g], W=[r_g])
            P.ts("dve", neG[:], eG[:], -1.0, None, ALU.mult, R=[r_g], W=[r_g])

            junk = SB("junk", [128, 384], F32); r_junk = R_(); ysc = SB("ysc", [128, 128], BF16); r_ysc = R_()
            kbar = SB("kbar", [128, NCH, 128], BF16); r_kbar = [R_() for _ in range(NCH)]
            vtok = SB("vtok", [128, NCH, 128], BF16); r_vtok = [R_() for _ in range(NCH)]
            qT = SB("qT", [128, 2048], BF16); r_qT = R_()
            zs = SB("zs", [128, NOWN, 128], BF16); r_zs = R_()
            oacc = SB("oacc", [128, NOWN, 128], F32); r_oacc = [R_() for _ in range(NOWN)]
            sm = SB("sm", [128, 8], F32); r_sm = R_()
            smk = SB("smk", [128, 3, 4], F32); r_smk = [R_(), R_(), R_()]
            Sf = [SB("Sf%d" % d, [128, 128], F32) for d in range(2)]; Sb = [SB("Sb%d" % d, [128, 128], BF16) for d in range(2)]
            r_S = [R_(), R_()]
            esD = contextlib.ExitStack()
            def SBD(name, shape, dt):
                return esD.enter_context(nc.sbuf_tensor(name, list(shape), dt))
            u = SBD("u", [128, T], BF16); yc = SBD("yc", [128, T], BF16); kact = yc; r_u = R_(); r_yc = R_(); r_ka = r_yc
            wq = SBD("wq", [128, 8, 128], BF16); r_wq = R_()
            oss = SBD("oss", [128, 2, NOWN], F32); r_oss = R_(); ysc2 = SBD("ysc2", [128, 128], BF16); r_ysc2 = R_(); r_jk = [R_(), R_()]
            NR = 4
            GB = 2

            def bfv(base, off, n=GB * 128):
                return base[:, off:off + n].rearrange("p (g x) -> p g x", x=128)

            def f32v(base, off):
                return base[:, off:off + 2 * GB * 128].bitcast(F32).rearrange("p (g x) -> p g x", x=128)

            def mk_tset(i):
                return dict(kp=SBD("pkp%d" % i, [128, GB, 128], BF16), nA=SBD("pnA%d" % i, [128, GB, 128], BF16),
                            nAT=SBD("pnAT%d" % i, [128, GB, 128], BF16), D=SBD("pD%d" % i, [128, GB, 128], BF16),
                            X=SBD("pX%d" % i, [128, GB, 128], BF16), rg=SBD("prg%d" % i, [128, GB, 128], F32),
                            ex=SBD("pex%d" % i, [128, GB, 128], F32), exT=SBD("pexT%d" % i, [128, GB, 128], F32), res=R_())

            def mk_gslot(i):
                return dict(E=SBD("gE%d" % i, [128, GB, 128], BF16), exTI=SBD("gX%d" % i, [128, GB, 128], BF16),
                            kT=SBD("gkT%d" % i, [128, GB, 128], BF16), kpp=SBD("gkpp%d" % i, [128, GB, 128], BF16),
                            vp=SBD("gvp%d" % i, [128, GB, 128], BF16), res=R_())
            W_ = GB * 128
            tsets = [mk_tset(0), mk_tset(1),
                     dict(kp=bfv(u, 0), nA=bfv(u, W_), nAT=bfv(u, 2 * W_), D=bfv(u, 3 * W_), X=bfv(u, 4 * W_),
                          rg=f32v(u, 5 * W_), ex=f32v(u, 7 * W_), exT=f32v(u, 9 * W_), res=r_u),
                     dict(kp=bfv(yc, 5 * W_), nA=bfv(yc, 6 * W_), nAT=bfv(yc, 7 * W_), D=bfv(yc, 8 * W_), X=bfv(yc, 9 * W_),
                          rg=f32v(yc, 10 * W_), ex=f32v(yc, 12 * W_), exT=f32v(yc, 14 * W_), res=r_yc)]
            NGS = 8
            gring = [mk_gslot(i) for i in range(NGS - 2)]
            gring.append(dict(E=bfv(u, 11 * W_), exTI=bfv(u, 12 * W_), kT=bfv(u, 13 * W_), kpp=bfv(u, 14 * W_), vp=bfv(u, 15 * W_), res=r_u))
            gring.append(dict(E=bfv(yc, 0), exTI=bfv(yc, W_), kT=bfv(yc, 2 * W_), kpp=bfv(yc, 3 * W_), vp=bfv(yc, 4 * W_), res=r_yc))
            assert 16 * W_ <= T
            NSS = 2
            sring = [dict(Rr=SBD("sR%d" % i, [128, 128], BF16), vn=SBD("svn%d" % i, [128, 128], BF16), qkm=SBD("sqkm%d" % i, [128, 128], BF16),
                          o1=SBD("so1%d" % i, [128, 128], F32), res2=R_()) for i in range(NSS)]

            def conv_silu(ci, lo, hi, parts, dst, r_dst):
                first = True
                for tap in (2, 0, 1, 3, 4):
                    s = tap - 2
                    for (a, e) in parts:
                        l2, h2 = max(a, a - s, lo), min(e, e - s, hi)
                        if tap == 2:
                            P.ts("dve", yc[:, l2:h2], u[:, l2:h2], convf[:, ci, tap:tap + 1], None, ALU.mult, R=[r_u] + RC, W=[r_yc])
                        else:
                            eng = "dve"
                            P.stt(eng, yc[:, l2:h2], u[:, l2 + s:h2 + s], convf[:, ci, tap:tap + 1], yc[:, l2:h2], ALU.mult, ALU.add,
                                  R=[r_u, r_yc] + RC, W=[r_yc])
                P.act(dst, yc[:, lo:hi], AF.Silu, R=[r_yc], W=[r_dst])

            def inproj_fm(col, t0, t1, wt, r_wt):
                P.dma("pool", wt[:], wv[:, :, col:col + 128], W=[r_wt])
                t = t0
                i = 0
                while t < t1:
                    n = min(512, t1 - t)
                    pp, r_pp = PS()
                    for k in range(8):
                        P.mm(pp[:, 0:n], wt[:, k, :], hT[:, k, t:t + n], k == 0, k == 7, R=[r_wt] + r_hT, W=[r_pp])
                    if i % 2 == 0:
                        P.cp("act", u[:, t:t + n], pp[:, 0:n], R=[r_pp], W=[r_u])
                    else:
                        P.cp("dve", u[:, t:t + n], pp[:, 0:n], R=[r_pp], W=[r_u])
                    t += n
                    i += 1

            def fl(ap):
                return ap.rearrange("p g x -> p (g x)")

            def bc(ap2d, G):
                return ap2d.unsqueeze(1).to_broadcast([128, G, 128])

            def gdn_prep(h, d, c0, G, gs, full, ti):
                tm = tsets[ti]; r1 = tm["res"]; gr = gring[gs]; rg_ = gr["res"]
                j = d * 4 + h
                scb = sal[:, j, c0:c0 + G].unsqueeze(2).to_broadcast([128, G, 128])
                P.tt("pool", tm["kp"][:, 0:G, :], kbar[:, c0:c0 + G, :], scb, ALU.mult, R=r_kbar[c0:c0 + G] + [r_g], W=[r1])
                P.tt("pool", gr["kpp"][:, 0:G, :], tm["kp"][:, 0:G, :], ekd[:, j, c0:c0 + G].unsqueeze(2).to_broadcast([128, G, 128]), ALU.mult,
                     R=[r1, r_g], W=[rg_])
                P.tt("pool", gr["vp"][:, 0:G, :], vtok[:, c0:c0 + G, :], scb, ALU.mult, R=r_vtok[c0:c0 + G] + [r_g], W=[rg_])
                pt, r_pt = PS(); ptb = pt[:, 0:G * 64].bitcast(BF16)
                for g in range(G):
                    P.tr(ptb[:, g * 128:(g + 1) * 128], tm["kp"][:, g, :], idb[:], R=[r1, r_c2], W=[r_pt])
                P.cp("act", fl(gr["kT"][:, 0:G, :]), ptb, R=[r_pt], W=[rg_])
                yield
                P.tt("pool", tm["rg"][:, 0:G, :], bc(MS[d], G), gal[:, j, c0:c0 + G].unsqueeze(2).to_broadcast([128, G, 128]), ALU.mult,
                     R=[r_g] + RC, W=[r1])
                p1, r_p1 = PS(); p2, r_p2 = PS()
                for g in range(G):
                    P.mm(p1[:, g * 128:(g + 1) * 128], MI[d], tm["rg"][:, g, :], True, True, R=[r1] + RC, W=[r_p1])
                for g in range(G):
                    P.mm(p2[:, g * 128:(g + 1) * 128], tm["rg"][:, g, :], MI[d], True, True, R=[r1] + RC, W=[r_p2])
                P.act(fl(tm["ex"][:, 0:G, :]), p1[:, 0:G * 128], AF.Exp, R=[r_p1], W=[r1])
                P.act(fl(tm["exT"][:, 0:G, :]), p2[:, 0:G * 128], AF.Exp, R=[r_p2], W=[r1])
                yield
                P.tt("pool", tm["ex"][:, 0:G, :], tm["ex"][:, 0:G, :], bc(MS[d], G), ALU.mult, R=[r1] + RC, W=[r1])
                if full:
                    P.tt("pool", gr["exTI"][:, 0:G, :], tm["exT"][:, 0:G, :], bc(MI[d], G), ALU.mult, R=[r1] + RC, W=[rg_])
                P.tt("pool", tm["exT"][:, 0:G, :], tm["exT"][:, 0:G, :], bc(MS[1 - d], G), ALU.mult, R=[r1] + RC, W=[r1])
                pk, r_pk = PS()
                for g in range(G):
                    P.mm(pk[:, g * 128:(g + 1) * 128], gr["kT"][:, g, :], gr["kT"][:, g, :], True, True, R=[rg_], W=[r_pk])
                P.stt("dve", fl(tm["nA"][:, 0:G, :]), pk[:, 0:G * 128], -1.0, fl(tm["ex"][:, 0:G, :]), ALU.mult, ALU.mult, R=[r_pk, r1], W=[r1])
                P.stt("dve", fl(tm["nAT"][:, 0:G, :]), pk[:, 0:G * 128], -1.0, fl(tm["exT"][:, 0:G, :]), ALU.mult, ALU.mult, R=[r_pk, r1], W=[r1])
                yield
                P.cp("pool", tm["D"][:, 0:G, :], bc(idb[:], G), R=[r_c2], W=[r1])
                P.cp("pool", gr["E"][:, 0:G, :], bc(idb[:], G), R=[r_c2], W=[rg_])
                P.op("dve", lambda e: e.copy_predicated(tm["D"][:, 0:G, :], LM8(d, 1), tm["nA"][:, 0:G, :]), R=[r1, r_c2], W=[r1])
                P.op("dve", lambda e: e.copy_predicated(gr["E"][:, 0:G, :], LM8(1 - d, 1), tm["nAT"][:, 0:G, :]), R=[r1, rg_, r_c2], W=[rg_])
                yield
                for l in range(2, 8):
                    px, r_px = PS()
                    for g in range(G):
                        P.mm(px[:, g * 128:(g + 1) * 128], tm["nA"][:, g, :], gr["E"][:, g, :], True, True, R=[r1, rg_], W=[r_px])
                    P.cp("act", fl(tm["X"][:, 0:G, :]), px[:, 0:G * 128], R=[r_px], W=[r1])
                    yield
                    pe_, r_pe = PS()
                    for g in range(G):
                        P.mm(pe_[:, g * 128:(g + 1) * 128], tm["D"][:, g, :], tm["X"][:, g, :], True, True, R=[r1], W=[r_pe])
                    if l < 7:
                        pd, r_pd = PS()
                        for g in range(G):
                            P.mm(pd[:, g * 128:(g + 1) * 128], tm["X"][:, g, :], tm["D"][:, g, :], True, True, R=[r1], W=[r_pd])
                    P.op("dve", lambda e, pe_=pe_, l=l: e.copy_predicated(gr["E"][:, 0:G, :], LM8(1 - d, l),
                                                                        pe_[:, 0:G * 128].rearrange("p (g x) -> p g x", x=128)),
                         R=[r_pe, rg_, r_c2], W=[rg_])
                    if l < 7:
                        P.op("dve", lambda e, pd=pd, l=l: e.copy_predicated(tm["D"][:, 0:G, :], LM8(d, l),
                                                                          pd[:, 0:G * 128].rearrange("p (g x) -> p g x", x=128)),
                             R=[r_pd, r1, r_c2], W=[r1])
                    yield

            sctr = [0]

            def gdn_scan(h, d, c, gs, g, full):
                gr = gring[gs]; r1 = gr["res"]
                rs = sring[sctr[0] % NSS]; sctr[0] += 1
                r2 = rs["res2"]
                j = d * 4 + h
                kT_ = gr["kT"][:, g, :]
                oc = c - OWN0
                pk, r_pk = PS()
                P.mm(pk[:, 0:128], kT_, Sb[d][:], True, True, R=[r1, r_S[d]], W=[r_pk])
                if full:
                    po1, r_po1 = PS()
                    P.mm(po1[:, 0:128], qT[:, oc * 128:(oc + 1) * 128], Sb[d][:], True, True, R=[r_qT, r_S[d]], W=[r_po1])
                P.stt("dve", rs["Rr"][:], pk[:, 0:128], neG[:, j, c:c + 1], gr["vp"][:, g, :], ALU.mult, ALU.add, R=[r_pk, r1, r_g], W=[r2])
                if full:
                    P.act(rs["o1"][:], po1[:, 0:128], AF.Identity, scale=eG[:, j, c:c + 1], R=[r_po1, r_g], W=[r2])
                yield
                pv, r_pv = PS()
                P.mm(pv[:, 0:128], gr["E"][:, g, :], rs["Rr"][:], True, True, R=[r1, r2], W=[r_pv])
                P.cp("act", rs["vn"][:], pv[:, 0:128], R=[r_pv], W=[r2])
                yield
                ps_, r_ps = PS()
                P.mm(ps_[:, 0:128], gr["kpp"][:, g, :], rs["vn"][:], True, True, R=[r1, r2], W=[r_ps])
                P.stt("dve", Sf[d][:], Sf[d][:], eGl[:, j, c:c + 1], ps_[:, 0:128], ALU.mult, ALU.add, R=[r_S[d], r_ps, r_g], W=[r_S[d]])
                P.cp("act", Sb[d][:], Sf[d][:], R=[r_S[d]], W=[r_S[d]])
                yield
                if full:
                    pq, r_pq = PS()
                    P.mm(pq[:, 0:128], kT_, qT[:, oc * 128:(oc + 1) * 128], True, True, R=[r1, r_qT], W=[r_pq])
                    P.tt("dve", rs["qkm"][:], pq[:, 0:128], gr["exTI"][:, g, :], ALU.mult, R=[r_pq, r1], W=[r2])
                    yield
                    po2, r_po2 = PS()
                    P.mm(po2[:, 0:128], rs["qkm"][:], rs["vn"][:], True, True, R=[r2], W=[r_po2])
                    P.tt("dve", rs["o1"][:], rs["o1"][:], po2[:, 0:128], ALU.add, R=[r2, r_po2], W=[r2])
                    P.tt("pool", oacc[:, oc, :], oacc[:, oc, :], rs["o1"][:], ALU.add, R=[r2, r_oacc[oc]], W=[r_oacc[oc]])
                    yield

            chF = [0, 1] + list(range(OWN0, OWN1))
            chR = [1, 0] + list(range(NCH - 1, OWN0 - 1, -1))

            def sched_tasks():
                tasks = []
                iF = iR = 0
                while iF < len(chF) or iR < len(chR):
                    if iR < len(chR):
                        tasks.append((1, chR[iR])); iR += 1
                    if iF < len(chF) and iR * len(chF) >= iF * len(chR):
                        tasks.append((0, chF[iF])); iF += 1
                return tasks

            for h in range(4):
                MK('gdn%d_prep' % h)
                inproj_fm(512 + h * 128, 0, T, wq, r_wq)
                conv_silu(4 + h, 0, T, [(0, CTX), (CTX, T)], kact[:, :], r_ka)
                for c in range(NCH):
                    pt, r_pt = PS(); ptb = pt[:, 0:64].bitcast(BF16)
                    P.tr(ptb, kact[:, c * 128:(c + 1) * 128], idb[:], R=[r_ka, r_c2], W=[r_pt])
                    q_ = c % 3
                    P.act(junk[:, q_ * 128:(q_ + 1) * 128], ptb, AF.Square, accum=smk[:, q_, 0:1], R=[r_pt], W=[r_smk[q_]])
                    P.act(smk[:, q_, 1:2], smk[:, q_, 0:1], AF.Ln, bias=EPS, R=[r_smk[q_]], W=[r_smk[q_]])
                    P.act(smk[:, q_, 2:3], smk[:, q_, 1:2], AF.Exp, scale=-0.5, R=[r_smk[q_]], W=[r_smk[q_]])
                    P.ts("dve", kbar[:, c, :], ptb, smk[:, q_, 2:3], None, ALU.mult, R=[r_pt, r_smk[q_]], W=[r_kbar[c]])
                inproj_fm(1024 + h * 128, 0, T, wq, r_wq)
                conv_silu(8 + h, 0, T, [(0, CTX), (CTX, T)], kact[:, :], r_ka)
                for c in range(NCH):
                    pt, r_pt = PS(); ptb = pt[:, 0:64].bitcast(BF16)
                    P.tr(ptb, kact[:, c * 128:(c + 1) * 128], idb[:], R=[r_ka, r_c2], W=[r_pt])
                    P.cp("act" if c % 2 else "dve", vtok[:, c, :], ptb, R=[r_pt], W=[r_vtok[c]])
                inproj_fm(h * 128, CTX, CTX + 2048 + 128, wq, r_wq)
                conv_silu(h, CTX, CTX + 2048, [(CTX, T)], kact[:, CTX:CTX + 2048], r_ka)
                qsq = oacc[:].rearrange("p a b -> p (a b)"); r_qsq = r_oacc[0]
                P.act(qsq, kact[:, CTX:CTX + 2048], AF.Square, R=[r_ka] + r_oacc, W=r_oacc)
                for t in range(4):
                    pp, r_pp = PS()
                    P.mm(pp[:, :], ones_f, qsq[:, t * 512:(t + 1) * 512], True, True, R=[r_qsq] + RC, W=[r_pp])
                    P.act(qsq[:, t * 512:(t + 1) * 512], pp[:, :], AF.Ln, bias=EPS, R=[r_pp, r_qsq], W=[r_qsq])
                    P.act(qsq[:, t * 512:(t + 1) * 512], qsq[:, t * 512:(t + 1) * 512], AF.Exp, scale=-0.5, R=[r_qsq], W=[r_qsq])
                P.stt("dve", qT[:], kact[:, CTX:CTX + 2048], float(HD) ** -0.5, qsq, ALU.mult, ALU.mult, R=[r_ka, r_qsq], W=[r_qT])
                P.dma("pool", wq[:], wv[:, :, 1536 + h * 128:1536 + (h + 1) * 128], W=[r_wq])
                for oc in range(NOWN):
                    c = OWN0 + oc
                    pz, r_pz = PS()
                    for k in range(8):
                        P.mm(pz[:, 0:128], hT[:, k, c * 128:(c + 1) * 128], wq[:, k, :], k == 0, k == 7, R=[r_hT[c], r_wq], W=[r_pz])
                    P.act(zs[:, oc, :], pz[:, 0:128], AF.Silu, R=[r_pz], W=[r_zs])
                MK('gdn%d_scan' % h)
                for d in range(2):
                    P.op("pool", lambda e, d=d: e.memset(Sf[d][:], 0.0), W=[r_S[d]])
                    P.op("pool", lambda e, d=d: e.memset(Sb[d][:], 0.0), W=[r_S[d]])
                P.op("pool", lambda e: e.memset(oacc[:], 0.0), W=r_oacc)
                Fg = [(0, c0, 2, [0, 1]) for c0 in [0] + list(range(OWN0, OWN1, 2))]
                Rg = [(1, c0, 2, [1, 0]) for c0 in [0] + list(range(NCH - 2, OWN0 - 1, -2))]
                groups = []
                iF = iR = 0
                while iF < len(Fg) or iR < len(Rg):
                    if iR < len(Rg):
                        groups.append(Rg[iR]); iR += 1
                    if iF < len(Fg) and iR * len(Fg) >= iF * len(Rg):
                        groups.append(Fg[iF]); iF += 1
                WV = 4
                prevw = []
                for w0 in range(0, len(groups) + WV, WV):
                    wave = groups[w0:w0 + WV]
                    gens = []
                    cur = []
                    for i, (d, c0, G, order) in enumerate(wave):
                        gsl = (w0 + i) % NGS
                        full = OWN0 <= c0 < OWN1
                        gens.append(delayed(gdn_prep(h, d, c0, G, gsl, full, i), i % 2))
                        cur.append((d, c0, G, order, gsl, full))
                    for dd in range(2):
                        sc_ = []
                        for (pd_, pc0, pG, porder, pgs, pfull) in prevw:
                            if pd_ == dd:
                                sc_ += [gdn_scan(h, pd_, pc0 + g, pgs, g, pfull) for g in porder]
                        if sc_:
                            gens.append(delayed(chain(sc_), dd))
                    interleave(gens)
                    prevw = cur
                MK('gdn%d_out' % h)
                for oc in range(NOWN):
                    P.act(junk[:, 0:128], oacc[:, oc, :], AF.Square, accum=oss[:, 0, oc:oc + 1], R=[r_oacc[oc]], W=[r_junk, r_oss])
                P.act(oss[:, 0, :], oss[:, 0, :], AF.Ln, bias=EPS, scale=1.0 / HD, R=[r_oss], W=[r_oss])
                P.act(oss[:, 0, :], oss[:, 0, :], AF.Exp, scale=-0.5, R=[r_oss], W=[r_oss])
                for oc in range(NOWN):
                    i_ = oc % 2
                    jb = junk[:, 128 + i_ * 128:256 + i_ * 128]
                    ys_, r_ys_ = (ysc, r_ysc) if i_ == 0 else (ysc2, r_ysc2)
                    P.stt("dve", jb, oacc[:, oc, :], oss[:, 0, oc:oc + 1], rowf[:, RO_GN:RO_GN + 128], ALU.mult, ALU.mult,
                          R=[r_oacc[oc], r_oss] + RC, W=[r_jk[i_]])
                    P.tt("pool", ys_[:], jb, zs[:, oc, :], ALU.mult, R=[r_jk[i_], r_zs], W=[r_ys_])
                    pt, r_pt = PS(); ptb = pt[:, 0:64].bitcast(BF16)
                    P.tr(ptb, ys_[:], idb[:], R=[r_ys_, r_c2], W=[r_pt])
                    P.cp("act", yT[:, h, oc * 128:(oc + 1) * 128], ptb, R=[r_pt], W=[r_yT[h]])

            P.barrier()
            esD.close()
            cst_ = SB("cs_sb", [128, NCH, 128], BF16)
            r_csg = {}
            for (c0_, G_) in [(0, 2)] + [(c0_, 4) for c0_ in range(2, NCH, 4)]:
                r_csg[c0_] = R_()
                P.dma("pool", cst_[:, c0_:c0_ + G_, :], cs[:, c0_:c0_ + G_, :], W=[r_csg[c0_]])
            ktok = kbar; r_ktok = r_kbar
            kTr = SB("kTr", [128, NCH, 128], BF16); r_kTr = [R_() for _ in range(NCH)]
            lgc = SB("lgc", [128, 8], F32); r_lg = R_()
            ring = [dict(qkm=SB("eqkm%d" % i, [128, 128], BF16), o1=SB("eo1%d" % i, [128, 128], F32), kpp=SB("ekpp%d" % i, [128, 128], BF16), res2=R_()) for i in range(NR)]
            P.act(lgc[:], rowf[:, RO_RL:RO_RL + 8], AF.Exp, scale=-1.0, R=RC, W=[r_lg])
            P.act(lgc[:], lgc[:], AF.Ln, bias=1.0, R=[r_lg], W=[r_lg])
            P.ts("dve", lgc[:], lgc[:], -1.0, None, ALU.mult, R=[r_lg], W=[r_lg])
            DrT = [SB("DrT%d" % d, [128, 128], F32) for d in range(2)]
            rsc = SB("rsc", [128, 8], F32); r_rc = R_()
            rw3 = [SB("rw%d" % i, [128, 8, 128], BF16) for i in range(4)]; r_rw3 = [R_() for _ in range(4)]

            rts = [SB("rts%d" % i, [128, 4, 4, 64], F32) for i in range(2)]; r_rts = [R_(), R_()]
            qtmp = SB("qtmp", [128, 4, 128], BF16); r_qtmp = R_()
            ross = SB("ross", [128, 4, NOWN], F32); r_ross = R_(); junkE = SB("junkE", [128, 4, 128], F32); r_jkE = [R_(), R_()]
            ysc2e = SB("ysc2e", [128, 128], BF16); r_ysc2e = R_()
            kzb = SB("kzb", [128, 16, 128], BF16); z2 = SB("z2", [128, 18], F32); r_kzb = R_()
            rctr = [0]

            def rope_group(ps, r_ps, c0, G, dst, r_dst):
                i = rctr[0] % 2
                rctr[0] += 1
                rt_, r_rt_ = rts[i], r_rts[i]
                pv3 = ps[:, 0:G * 128].rearrange("p (g x) -> p g x", x=128)
                t1 = pv3[:, :, 0:64]; t2 = pv3[:, :, 64:128]
                cosv = cst_[:, c0:c0 + G, 0:64]; sinv = cst_[:, c0:c0 + G, 64:128]
                r_cs = r_csg[c0]
                P.tt("dve", rt_[:, 0, 0:G, :], t1, cosv, ALU.mult, R=[r_ps, r_cs], W=[r_rt_])
                P.tt("dve", rt_[:, 1, 0:G, :], t2, sinv, ALU.mult, R=[r_ps, r_cs], W=[r_rt_])
                P.tt("dve", rt_[:, 2, 0:G, :], t1, sinv, ALU.mult, R=[r_ps, r_cs], W=[r_rt_])
                P.tt("dve", rt_[:, 3, 0:G, :], t2, cosv, ALU.mult, R=[r_ps, r_cs], W=[r_rt_])
                P.tt("pool", dst[:, :, 0:64], rt_[:, 0, 0:G, :], rt_[:, 1, 0:G, :], ALU.subtract, R=[r_rt_], W=r_dst)
                P.tt("pool", dst[:, :, 64:128], rt_[:, 2, 0:G, :], rt_[:, 3, 0:G, :], ALU.add, R=[r_rt_], W=r_dst)

            kgroups = [(0, 2)] + [(c0, 4) for c0 in range(2, NCH, 4)]
            for h in range(4):
                MK('ret%d_prep' % h)
                cols = [2064 + h * 128, 2576 + h * 128, 3088 + h * 128, 3600 + h * 128]
                for i in range(4):
                    P.dma("pool", rw3[i][:], wv[:, :, cols[i]:cols[i] + 128], W=[r_rw3[i]])
                for (c0, G) in kgroups:
                    pk, r_pk = PS()
                    for g in range(G):
                        c = c0 + g
                        for k in range(8):
                            P.mm(pk[:, g * 128:(g + 1) * 128], hT[:, k, c * 128:(c + 1) * 128], rw3[1][:, k, :], k == 0, k == 7,
                                 R=[r_hT[c], r_rw3[1]], W=[r_pk])
                    rope_group(pk, r_pk, c0, G, ktok[:, c0:c0 + G, :], r_ktok[c0:c0 + G])
                    pt, r_pt = PS(); ptb = pt[:, 0:G * 64].bitcast(BF16)
                    for g in range(G):
                        P.tr(ptb[:, g * 128:(g + 1) * 128], ktok[:, c0 + g, :], idb[:], R=[r_ktok[c0 + g], r_c2], W=[r_pt])
                    P.cp("act", kTr[:, c0:c0 + G, :].rearrange("p g x -> p (g x)"), ptb, R=[r_pt], W=r_kTr[c0:c0 + G])
                    pv, r_pv = PS()
                    for g in range(G):
                        c = c0 + g
                        for k in range(8):
                            P.mm(pv[:, g * 128:(g + 1) * 128], hT[:, k, c * 128:(c + 1) * 128], rw3[2][:, k, :], k == 0, k == 7,
                                 R=[r_hT[c], r_rw3[2]], W=[r_pv])
                    P.cp("act", vtok[:, c0:c0 + G, :].rearrange("p g x -> p (g x)"), pv[:, 0:G * 128], R=[r_pv], W=r_vtok[c0:c0 + G])
                    if OWN0 <= c0 < OWN1:
                        oc0 = c0 - OWN0
                        pq, r_pq = PS()
                        for g in range(G):
                            c = c0 + g
                            for k in range(8):
                                P.mm(pq[:, g * 128:(g + 1) * 128], hT[:, k, c * 128:(c + 1) * 128], rw3[0][:, k, :], k == 0, k == 7,
                                     R=[r_hT[c], r_rw3[0]], W=[r_pq])
                        rope_group(pq, r_pq, c0, G, qtmp[:, 0:G, :], [r_qtmp])
                        pt, r_pt = PS(); ptb = pt[:, 0:G * 64].bitcast(BF16)
                        for g in range(G):
                            P.tr(ptb[:, g * 128:(g + 1) * 128], qtmp[:, g, :], idb[:], R=[r_qtmp, r_c2], W=[r_pt])
                        P.cp("act", qT[:, oc0 * 128:(oc0 + G) * 128], ptb, R=[r_pt], W=[r_qT])
                        pz, r_pz = PS()
                        for g in range(G):
                            c = c0 + g
                            for k in range(8):
                                P.mm(pz[:, g * 128:(g + 1) * 128], hT[:, k, c * 128:(c + 1) * 128], rw3[3][:, k, :], k == 0, k == 7,
                                     R=[r_hT[c], r_rw3[3]], W=[r_pz])
                        P.act(zs[:, oc0:oc0 + G, :].rearrange("p g x -> p (g x)"), pz[:, 0:G * 128], AF.Silu, R=[r_pz], W=[r_zs])
                for d in range(2):
                    j = d * 4 + h
                    lg = lgc[:, j:j + 1]
                    P.act(DrT[d][:], DIST[d], AF.Exp, scale=lg, R=RC + [r_lg], W=[r_rc])
                    P.stt("dve", DrT[d][:], DrT[d][:], float(HD) ** -0.5, MI[d], ALU.mult, ALU.mult, R=[r_rc] + RC, W=[r_rc])
                    P.act(rsc[:, d * 4:d * 4 + 1], csmf[:, d:d + 1], AF.Exp, scale=lg, R=RC + [r_lg], W=[r_rc])
                    P.ts("dve", rsc[:, d * 4:d * 4 + 1], rsc[:, d * 4:d * 4 + 1], float(HD) ** -0.5, None, ALU.mult, R=[r_rc], W=[r_rc])
                    P.act(rsc[:, d * 4 + 1:d * 4 + 2], csmf[:, 2 + d:3 + d], AF.Exp, scale=lg, R=RC + [r_lg], W=[r_rc])
                    P.act(rsc[:, d * 4 + 2:d * 4 + 3], csmf[:, 4:5], AF.Exp, scale=lg, R=RC + [r_lg], W=[r_rc])
                    P.op("pool", lambda e, d=d: e.memset(Sf[d][:], 0.0), W=[r_S[d]])
                    P.op("pool", lambda e, d=d: e.memset(Sb[d][:], 0.0), W=[r_S[d]])
                P.op("pool", lambda e: e.memset(oacc[:], 0.0), W=r_oacc)

                def ret_step(d, c, slot):
                    rs = ring[slot]; r2 = rs["res2"]
                    full = OWN0 <= c < OWN1
                    oc = c - OWN0
                    if full:
                        po1, r_po1 = PS()
                        P.mm(po1[:, 0:128], qT[:, oc * 128:(oc + 1) * 128], Sb[d][:], True, True, R=[r_qT, r_S[d]], W=[r_po1])
                        P.act(rs["o1"][:], po1[:, 0:128], AF.Identity, scale=rsc[:, d * 4:d * 4 + 1], R=[r_po1, r_rc], W=[r2])
                    P.ts("dve", rs["kpp"][:], ktok[:, c, :], rsc[:, d * 4 + 1:d * 4 + 2], None, ALU.mult, R=[r_ktok[c], r_rc], W=[r2])
                    ps_, r_ps = PS()
                    P.mm(ps_[:, 0:128], rs["kpp"][:], vtok[:, c, :], True, True, R=[r2, r_vtok[c]], W=[r_ps])
                    P.stt("dve", Sf[d][:], Sf[d][:], rsc[:, d * 4 + 2:d * 4 + 3], ps_[:, 0:128], ALU.mult, ALU.add, R=[r_S[d], r_ps, r_rc], W=[r_S[d]])
                    P.cp("act", Sb[d][:], Sf[d][:], R=[r_S[d]], W=[r_S[d]])
                    yield
                    if full:
                        pp, r_pp = PS()
                        P.mm(pp[:, 0:128], kTr[:, c, :], qT[:, oc * 128:(oc + 1) * 128], True, True, R=[r_kTr[c], r_qT], W=[r_pp])
                        P.tt("dve", rs["qkm"][:], pp[:, 0:128], DrT[d][:], ALU.mult, R=[r_pp, r_rc], W=[r2])
                        po2, r_po2 = PS()
                        P.mm(po2[:, 0:128], rs["qkm"][:], vtok[:, c, :], True, True, R=[r2, r_vtok[c]], W=[r_po2])
                        P.tt("dve", rs["o1"][:], rs["o1"][:], po2[:, 0:128], ALU.add, R=[r2, r_po2], W=[r2])
                        P.tt("pool", oacc[:, oc, :], oacc[:, oc, :], rs["o1"][:], ALU.add, R=[r2, r_oacc[oc]], W=[r_oacc[oc]])
                        yield

                def ret_batch_pre():
                    lgR = lgc[:, 4 + h:5 + h]
                    P.act(z2[:, 0:16], csmf[:, 8:24], AF.Exp, scale=lgR, R=RC + [r_lg], W=[r_kzb])
                    P.act(z2[:, 16:17], csmf[:, 5:6], AF.Exp, scale=lgR, R=RC + [r_lg], W=[r_kzb])
                    P.tt("pool", kzb[:], ktok[:, OWN1:NCH, :], z2[:, 0:16].unsqueeze(2).to_broadcast([128, 16, 128]), ALU.mult,
                         R=r_ktok[OWN1:NCH] + [r_kzb], W=[r_kzb])

                def ret_batch():
                    pb_, r_pb_ = PS()
                    for m in range(16):
                        P.mm(pb_[:, 0:128], kzb[:, m, :], vtok[:, OWN1 + m, :], m == 0, m == 15, R=[r_kzb, r_vtok[OWN1 + m]], W=[r_pb_])
                    P.stt("dve", Sf[1][:], Sf[1][:], z2[:, 16:17], pb_[:, 0:128], ALU.mult, ALU.add, R=[r_S[1], r_pb_, r_kzb], W=[r_S[1]])
                    P.cp("act", Sb[1][:], Sf[1][:], R=[r_S[1]], W=[r_S[1]])
                    yield

                ret_batch_pre()
                gF = chain([ret_step(0, c, (2 * i) % NR) for i, c in enumerate(chF)])
                gR = chain([ret_step(1, c, 1) for c in (1, 0)] + [ret_batch()]
                           + [ret_step(1, c, (2 * i + 1) % NR) for i, c in enumerate(range(OWN1 - 1, OWN0 - 1, -1))])
                interleave([gR, delayed(gF, 1)])
                for oc in range(NOWN):
                    P.act(junk[:, 0:128], oacc[:, oc, :], AF.Square, accum=ross[:, 1, oc:oc + 1], R=[r_oacc[oc]], W=[r_junk, r_ross])
                    P.act(junk[:, 128:256], oacc[:, oc, :], AF.Identity, accum=ross[:, 0, oc:oc + 1], R=[r_oacc[oc]], W=[r_junk, r_ross])
                P.ts("dve", ross[:, 0, :], ross[:, 0, :], 1.0 / HD, None, ALU.mult, R=[r_ross], W=[r_ross])
                P.tt("dve", ross[:, 2, :], ross[:, 0, :], ross[:, 0, :], ALU.mult, R=[r_ross], W=[r_ross])
                P.stt("dve", ross[:, 1, :], ross[:, 1, :], 1.0 / HD, ross[:, 2, :], ALU.mult, ALU.subtract, R=[r_ross], W=[r_ross])
                P.act(ross[:, 3, :], ross[:, 1, :], AF.Ln, bias=EPS, R=[r_ross], W=[r_ross])
                P.act(ross[:, 3, :], ross[:, 3, :], AF.Exp, scale=-0.5, R=[r_ross], W=[r_ross])
                for oc in range(NOWN):
                    i_ = oc % 2
                    ys_, r_ys_ = (ysc, r_ysc) if i_ == 0 else (ysc2e, r_ysc2e)
                    P.ts("dve", junkE[:, 2 * i_, :], oacc[:, oc, :], ross[:, 0, oc:oc + 1], ross[:, 3, oc:oc + 1], ALU.subtract, ALU.mult,
                         R=[r_oacc[oc], r_ross], W=[r_jkE[i_]])
                    P.tt("pool", junkE[:, 2 * i_ + 1, :], junkE[:, 2 * i_, :], rowf[:, RO_RN:RO_RN + 128], ALU.mult, R=[r_jkE[i_]] + RC, W=[r_jkE[i_]])
                    P.tt("pool", ys_[:], junkE[:, 2 * i_ + 1, :], zs[:, oc, :], ALU.mult, R=[r_jkE[i_], r_zs], W=[r_ys_])
                    pt, r_pt = PS(); ptb = pt[:, 0:64].bitcast(BF16)
                    P.tr(ptb, ys_[:], idb[:], R=[r_ys_, r_c2], W=[r_pt])
                    P.cp("act", yT[:, 4 + h, oc * 128:(oc + 1) * 128], ptb, R=[r_pt], W=[r_yT[4 + h]])
            if dbg:
                dbg_out["yT"] = nc.dram_tensor("d_yT", [128, 8, 2048], BF16, kind="ExternalOutput").ap()
                P.out_dma("sp", dbg_out["yT"][:, :, :], yT[:], R=r_yT)
            P.barrier()

        MK('F')
        with contextlib.ExitStack() as es:
            def SB(name, shape, dt):
                return es.enter_context(nc.sbuf_tensor(name, list(shape), dt))
            wo = SB("wo", [128, 8, DM], BF16); r_wo = R_()
            fgrow = SB("fgrow", [128, DM], F32); r_fg = R_()
            P.dma("sp", fgrow[:], rowc[:, RO_FG:RO_FG + DM], W=[r_fg])
            wfo = SB("wfo", [128, 22, DM], BF16); r_wfo = R_()
            wov = w_out.rearrange("(k p) n -> p k n", p=128); wfov = w_fo.rearrange("(k p) n -> p k n", p=128)
            wfiv = w_fi.rearrange("(k p) n -> p k n", p=128)
            wst = [SB("wst%d" % i, [128, 8, 512], BF16) for i in range(2)]; r_wst = [R_(), R_()]
            x1_ = SB("x1_", [128, 4, DM], F32); x1 = [x1_, x1_]; r_x1_ = R_(); r_x1 = [r_x1_, r_x1_]
            h2T_ = SB("h2T", [128, 8, 512], BF16); h2T = [h2T_, h2T_]; r_h2_ = R_(); r_h2 = [r_h2_, r_h2_]
            aT = SB("aT", [128, 22, 512], BF16); r_aT = R_()
            xob = [SB("xob%d" % i, [128, DM], F32) for i in range(2)]; r_xob = [R_(), R_()]
            ntmp = [(SB("fsq%d" % i, [128, DM], BF16), SB("fss%d" % i, [128, 4], F32), SB("fxn%d" % i, [128, DM], BF16), R_()) for i in range(2)]
            sg = [SB("sg%d" % i, [128, 4, 512], BF16) for i in range(2)]; r_sg = [R_(), R_()]
            ob = [SB("ob%d" % i, [128, DM], F32) for i in range(2)]; r_ob = [R_(), R_()]
            for k in range(8):
                i = k % 2
                P.dma("sp", ob[i][:], wov[:, k, :], W=[r_ob[i]])
                P.tt("dve" if i else "pool", wo[:, k, :], ob[i][:], g1row[:], ALU.mult, R=[r_ob[i], r_grow], W=[r_wo])
            nst = 0
            for grp in range(4):
                gi = grp % 2
                for bb in range(4):
                    blk = grp * 4 + bb
                    i = blk % 2
                    P.dma("sp", xob[i][:], xo[blk * 128:(blk + 1) * 128, :], W=[r_xob[i]])
                    for hf in range(2):
                        pp, r_pp = PS()
                        for k in range(8):
                            P.mm(pp[:, :], yT[:, k, blk * 128:(blk + 1) * 128], wo[:, k, hf * 512:(hf + 1) * 512], k == 0, k == 7,
                                 R=[r_yT[k], r_wo], W=[r_pp])
                        P.tt("dve", x1[gi][:, bb, hf * 512:(hf + 1) * 512], pp[:, :], xob[i][:, hf * 512:(hf + 1) * 512], ALU.add,
                             R=[r_pp, r_xob[i]], W=[r_x1[gi]])
                    norm_block(x1[gi][:, bb, :], r_x1[gi], h2T[gi], r_h2[gi], bb * 128,
                               lambda k: sc2[:, k, 0:1], lambda k: sc2[:, k, 1:2], ntmp[i])
                def wfo_chunk(k):
                    i = k % 2
                    P.dma("sp", ob[i][:], wfov[:, k, :], W=[r_ob[i]])
                    P.tt("dve", wfo[:, k, :], ob[i][:], g2row[:], ALU.mult, R=[r_ob[i], r_grow], W=[r_wfo])
                for j0 in range(0, 22, 4):
                    nj = min(4, 22 - j0)
                    si = (j0 // 4) % 2
                    wi = nst % 2
                    nst += 1
                    P.dma("pool", wst[wi][:, :, 0:nj * 128], wfiv[:, :, j0 * 128:(j0 + nj) * 128], W=[r_wst[wi]])
                    for jj in range(nj):
                        pg_, r_pg_ = PS()
                        for k in range(8):
                            P.mm(pg_[:, :], wst[wi][:, k, jj * 128:(jj + 1) * 128], h2T[gi][:, k, :], k == 0, k == 7, R=[r_wst[wi], r_h2[gi]], W=[r_pg_])
                        P.act(sg[si][:, jj, :], pg_[:, :], AF.Silu, R=[r_pg_], W=[r_sg[si]])
                    wi = nst % 2
                    nst += 1
                    P.dma("pool", wst[wi][:, :, 0:nj * 128], wfiv[:, :, DFF + j0 * 128:DFF + (j0 + nj) * 128], W=[r_wst[wi]])
                    for jj in range(nj):
                        pu, r_pu = PS()
                        for k in range(8):
                            P.mm(pu[:, :], wst[wi][:, k, jj * 128:(jj + 1) * 128], h2T[gi][:, k, :], k == 0, k == 7, R=[r_wst[wi], r_h2[gi]], W=[r_pu])
                        P.tt("dve", aT[:, j0 + jj, :], pu[:, :], sg[si][:, jj, :], ALU.mult, R=[r_pu, r_sg[si]], W=[r_aT])
                        if grp == 0:
                            wfo_chunk(j0 + jj)
                for bb in range(4):
                    blk = grp * 4 + bb
                    oi = blk % 2
                    for hf in range(2):
                        pp, r_pp = PS()
                        for j in range(22):
                            P.mm(pp[:, :], aT[:, j, bb * 128:(bb + 1) * 128], wfo[:, j, hf * 512:(hf + 1) * 512], j == 0, j == 21,
                                 R=[r_aT, r_wfo], W=[r_pp])
                        P.tt("dve", x1[gi][:, bb, hf * 512:(hf + 1) * 512], pp[:, :], x1[gi][:, bb, hf * 512:(hf + 1) * 512], ALU.add,
                             R=[r_pp, r_x1[gi]], W=[r_x1[gi]])
                    sq, ss, xn, r_t = ntmp[oi]
                    P.act(sq[:], x1[gi][:, bb, :], AF.Square, accum=ss[:, 0:1], R=[r_x1[gi]], W=[r_t])
                    P.act(ss[:, 1:2], ss[:, 0:1], AF.Ln, bias=EPS, scale=1.0 / DM, R=[r_t], W=[r_t])
                    P.act(ss[:, 2:3], ss[:, 1:2], AF.Exp, scale=-0.5, R=[r_t], W=[r_t])
                    P.stt("dve", ob[oi][:], x1[gi][:, bb, :], ss[:, 2:3], fgrow[:], ALU.mult, ALU.mult,
                          R=[r_x1[gi], r_t, r_fg], W=[r_ob[oi]])
                    P.out_dma("sp", out[blk * 128:(blk + 1) * 128, :], ob[oi][:], R=[r_ob[oi]])
            MK('end')
            P.finish()
    return nc, dbg_out


def _flipseq(a):
    return np.concatenate([a[:CTX][::-1], a[CTX:][::-1]], axis=0)


def _rope_tables():
    rows = LAT // 64
    row = np.repeat(np.arange(rows, dtype=np.float32), 64)
    col = np.tile(np.arange(64, dtype=np.float32), rows)
    z = np.zeros(CTX, np.float32)
    p_seq = np.concatenate([np.arange(CTX, dtype=np.float32), np.full(LAT, float(CTX), np.float32)])
    p_row = np.concatenate([z, row]); p_col = np.concatenate([z, col])

    def aa(pos, n):
        inv = (np.float32(10000.0) ** (-np.arange(n, dtype=np.float32) / np.float32(n))).astype(np.float32)
        return pos[:, None] * inv[None, :]
    ang = np.concatenate([aa(p_seq, 16), aa(p_row, 24), aa(p_col, 24)], axis=-1).astype(np.float32)
    return np.cos(ang).astype(np.float32), np.sin(ang).astype(np.float32)


def _consts():
    i = np.arange(C)
    blocks = [np.eye(C, dtype=np.float32)]
    MIs, MSs, LMs = [], [], []
    for d in range(2):
        incl = (i[:, None] <= i[None, :]) if d == 0 else (i[:, None] >= i[None, :])
        MIs.append(incl.astype(np.float32))
    for d in range(2):
        st = (i[None, :] < i[:, None]) if d == 0 else (i[None, :] > i[:, None])
        MSs.append(st.astype(np.float32))
    for d in range(2):
        for l in range(1, 8):
            b = 2 ** (l - 1)
            blk = (i[:, None] // (2 * b)) == (i[None, :] // (2 * b))
            ih = (i // b) % 2
            m = blk & (ih[:, None] == 1) & (ih[None, :] == 0)
            if d == 1:
                m = m.T
            LMs.append(m.astype(np.float32))
    dist = np.abs(i[:, None] - i[None, :]).astype(np.float32)
    blocks += MIs + MSs + LMs + [dist, dist, np.ones((C, C), np.float32)]
    cst = np.concatenate(blocks, axis=1)
    pos = i.astype(np.float32)
    csm = np.zeros((128, 24), np.float32)
    csm[:, 0] = pos + 1.0
    csm[:, 1] = (C - 1 - pos) + 1.0
    csm[:, 2] = C - 1.0 - pos
    csm[:, 3] = pos
    csm[:, 4] = float(C)
    csm[:, 5] = float(C * 16)
    for m in range(16):
        csm[:, 8 + m] = pos + float(C * m)
    return np.ascontiguousarray(cst), csm


def _core_inputs(inp, b, p, shared):
    f = np.float32
    xs = np.concatenate([inp["ctx"][b], inp["x"][b]], axis=0)
    if p == 1:
        xs = _flipseq(xs)
    xs = np.ascontiguousarray(xs, dtype=f)
    pi = [p, 1 - p]
    w_in = inp["w_in"][0]
    ab0 = 2048
    a_cols = [w_in[:, ab0 + 4 * k: ab0 + 4 * k + 4] for k in range(4)]
    w_ab = np.concatenate([a_cols[pi[0]], a_cols[pi[1]], a_cols[2 + pi[0]], a_cols[2 + pi[1]]], axis=1)
    cw = inp["conv_w"][0]
    if p == 1:
        cw = cw[::-1]
    convT = np.ascontiguousarray(cw.reshape(5, 12, 128).transpose(2, 1, 0), dtype=f)
    cvec = np.concatenate([inp["c"][b].reshape(8, 128).T, inp["c_ctx"].reshape(8, 128).T], axis=1)
    rowc = np.zeros((NRO,), f)
    rowc[RO_GN:RO_GN + 128] = inp["gdn_norm_g"][0]
    rowc[RO_RN:RO_RN + 128] = inp["ret_norm_g"][0]
    rowc[RO_FG:RO_FG + DM] = inp["final_g"]
    for d in range(2):
        rowc[RO_AL + d * 4:RO_AL + d * 4 + 4] = inp["gdn_a_log"][0][pi[d]]
        rowc[RO_DTB + d * 4:RO_DTB + d * 4 + 4] = inp["gdn_dt_bias"][0][pi[d]]
        rowc[RO_RL + d * 4:RO_RL + d * 4 + 4] = inp["ret_decay_logit"][0][pi[d]]
    rowc[RO_B1:RO_B1 + DM] = inp["ada_b"][0][2 * DM:3 * DM]
    rowc[RO_B2:RO_B2 + DM] = inp["ada_b"][0][5 * DM:6 * DM]
    rowc = np.ascontiguousarray(np.broadcast_to(rowc[None, :], (128, NRO)))
    cos, sin = shared["rope"]
    if p == 1:
        cos, sin = _flipseq(cos), _flipseq(sin)
    cs = np.concatenate([cos, sin], axis=1).reshape(NCH, 128, 128).transpose(1, 0, 2)
    return {
        "xs": xs, "xo": np.ascontiguousarray(xs[CTX:CTX + 2048]),
        "cvec": np.ascontiguousarray(cvec, dtype=f), "ada_w": shared["ada_w"], "adabT": shared["adabT"],
        "gmixT": shared["gmixT"], "gffnT": shared["gffnT"], "w_in": shared["w_in"],
        "w_ab": np.ascontiguousarray(w_ab, dtype=f), "convT": convT, "rowc": rowc,
        "cs": np.ascontiguousarray(cs, dtype=f), "cst": shared["cst"], "csm": shared["csm"],
        "w_out": shared["w_out"], "w_fi": shared["w_fi"], "w_fo": shared["w_fo"],
    }


def _shared(inp):
    f = np.float32
    cst, csm = _consts()
    return {
        "rope": _rope_tables(), "cst": cst, "csm": csm,
        "ada_w": np.ascontiguousarray(inp["ada_w"][0], dtype=f),
        "adabT": np.ascontiguousarray(inp["ada_b"][0].reshape(48, 128).T, dtype=f),
        "gmixT": np.ascontiguousarray(inp["norm_mix_g"][0].reshape(8, 128).T, dtype=f),
        "gffnT": np.ascontiguousarray(inp["norm_ffn_g"][0].reshape(8, 128).T, dtype=f),
        "w_in": np.ascontiguousarray(inp["w_in"][0], dtype=f), "w_out": np.ascontiguousarray(inp["w_out"][0], dtype=f),
        "w_fi": np.ascontiguousarray(inp["w_ffn_in"][0], dtype=f), "w_fo": np.ascontiguousarray(inp["w_ffn_out"][0], dtype=f),
    }


def kernel(**inputs):
    inp = {k: np.asarray(v) for k, v in inputs.items()}
    sh = _shared(inp)
    in_maps = [_core_inputs(inp, core // 2, core % 2, sh) for core in range(8)]
    nc, _ = build(False)
    res = run_bass_kernel_spmd(nc, in_maps, core_ids=list(range(8)))
    outp = np.zeros((4, LAT, DM), np.float32)
    for core in range(8):
        b, p = core // 2, core % 2
        o = np.asarray(res.results[core]["out"], dtype=np.float32)
        if p == 0:
            outp[b, :2048] = o
        else:
            outp[b, 2048:] = o[::-1]
    return outp
```

```python
import contextlib
import numpy as np
import concourse.bass as bass
import concourse.mybir as mybir
from concourse.bass_utils import run_bass_kernel_spmd

F32 = mybir.dt.float32
BF16 = mybir.dt.bfloat16
AF = mybir.ActivationFunctionType
ALU = mybir.AluOpType
AX = mybir.AxisListType

NDMA_SEM = 8
ALLSIG = False
STRICT = True


class Res:
    __slots__ = ("name", "w", "r", "rd")

    def __init__(self, name=""):
        self.name = name
        self.w = None
        self.r = {}
        self.rd = []


class Op:
    __slots__ = ("eng", "fn", "deps", "sig", "sidx", "dma", "didx", "pos", "pseudo")

    def __init__(self, eng, fn, dma):
        self.eng = eng
        self.fn = fn
        self.dma = dma
        self.deps = []
        self.sig = False
        self.sidx = 0
        self.didx = 0
        self.pos = 0
        self.pseudo = False


class Prog:
    ENGS = ("pe", "act", "dve", "pool", "sp")
    BLK = {"pe": "tensor", "act": "scalar", "dve": "vector", "pool": "gpsimd", "sp": "sync"}

    def __init__(self, nc):
        self.nc = nc
        self.ops = {e: [] for e in self.ENGS}
        self.stack = contextlib.ExitStack()
        self.nres = 0
        self.out_res = []
        self._bar_pos = {}

    def __enter__(self):
        self.stack.__enter__()
        self.esem = {e: self.stack.enter_context(self.nc.semaphore("s_" + e)) for e in self.ENGS}
        self.dsem = {e: [self.stack.enter_context(self.nc.semaphore("d_%s%d" % (e, i))) for i in range(NDMA_SEM)]
                     for e in ("sp", "pool", "act")}
        return self

    def __exit__(self, *a):
        return self.stack.__exit__(*a)

    def sb(self, name, shape, dt):
        return self.stack.enter_context(self.nc.sbuf_tensor(name, list(shape), dt))

    def ps(self, name, shape=(128, 512), dt=F32):
        return self.stack.enter_context(self.nc.psum_tensor(name, list(shape), dt))

    def res(self, name=""):
        self.nres += 1
        return Res(name)

    def op(self, eng, fn, R=(), W=(), dma=False):
        o = Op(eng, fn, dma)
        lst = self.ops[eng]
        o.pos = len(lst)
        deps = {}

        def add(d, raw):
            if d is None or d is o:
                return
            if d.dma or o.dma or d.eng != eng:
                deps[id(d)] = d
            elif eng != "pe" and (raw or STRICT):
                deps[id(d)] = d

        for r in R:
            add(r.w, True)
        for w in W:
            add(w.w, False)
            for rr in w.r.values():
                add(rr, False)
            for rr in w.rd:
                add(rr, False)
        best = {}
        out = []
        for d in deps.values():
            if d.dma:
                out.append(d)
            else:
                b = best.get(d.eng)
                if b is None or d.pos > b.pos:
                    best[d.eng] = d
        out.extend(best.values())
        o.deps = out
        for d in out:
            d.sig = True
        for r in R:
            if dma:
                r.rd.append(o)
            else:
                r.r[eng] = o
        for w in W:
            w.w = o
            w.r = {}
            w.rd = []
        lst.append(o)
        return o

    def dma(self, eng, out, in_, R=(), W=()):
        return self.op(eng, lambda e: e.dma_start(out=out, in_=in_), R, W, dma=True)

    def mm(self, out, lhsT, rhs, start, stop, R=(), W=()):
        return self.op("pe", lambda e: e.matmul(out, lhsT, rhs, start=start, stop=stop), R, W)

    def tr(self, out, in_, ident, R=(), W=()):
        return self.op("pe", lambda e: e.transpose(out, in_, ident), R, W)

    def act(self, out, in_, func, bias=None, scale=None, accum=None, R=(), W=()):
        kw = {}
        if bias is not None:
            kw["bias"] = bias
        if scale is not None:
            kw["scale"] = scale
        if accum is not None:
            kw["accum_out"] = accum
        return self.op("act", lambda e: e.activation(out, in_, func, **kw), R, W)

    def tt(self, eng, out, a, b, op, R=(), W=()):
        return self.op(eng, lambda e: e.tensor_tensor(out, a, b, op), R, W)

    def ts(self, eng, out, a, s1, s2, op0, op1=None, R=(), W=()):
        if op1 is None:
            return self.op(eng, lambda e: e.tensor_scalar(out, a, s1, None, op0), R, W)
        return self.op(eng, lambda e: e.tensor_scalar(out, a, s1, s2, op0, op1), R, W)

    def stt(self, eng, out, in0, scalar, in1, op0, op1, R=(), W=()):
        return self.op(eng, lambda e: e.scalar_tensor_tensor(out, in0, scalar, in1, op0, op1), R, W)

    def cp(self, eng, out, in_, R=(), W=()):
        if eng == "act":
            return self.op(eng, lambda e: e.copy(out, in_), R, W)
        return self.op(eng, lambda e: e.tensor_copy(out, in_), R, W)

    def out_dma(self, eng, out, in_, R=()):
        r = self.res("out")
        self.out_res.append(r)
        return self.dma(eng, out, in_, R=R, W=[r])

    def barrier(self):
        lasts = []
        for e in self.ENGS:
            for o_ in reversed(self.ops[e]):
                if not o_.pseudo and not o_.dma:
                    lasts.append(o_)
                    break
        dmas = [o for e in self.ENGS for o in self.ops[e][self._bar_pos.get(e, 0):] if o.dma]
        self._bar_pos = {e: len(self.ops[e]) for e in self.ENGS}
        for e in self.ENGS:
            o = Op(e, lambda eng: None, False)
            o.pseudo = True
            o.pos = len(self.ops[e])
            deps = [d for d in lasts if d.eng != e and not d.dma] + dmas
            o.deps = deps
            for d in deps:
                d.sig = True
            self.ops[e].append(o)

    def finish(self):
        fo = self.op("sp", lambda e: None, R=self.out_res)
        fo.pseudo = True
        for e in self.ENGS:
            cnt = 0
            nd = 0
            for o in self.ops[e]:
                if ALLSIG and e in ("dve", "act", "pool") and not o.pseudo and not o.dma:
                    o.sig = True
                if o.dma:
                    o.didx = nd
                    nd += 1
                elif o.sig:
                    assert not o.pseudo
                    cnt += 1
                    o.sidx = cnt
        with self.nc.Block() as block:
            for e in self.ENGS:
                getattr(block, self.BLK[e])(lambda eng, e=e: self._emit(e, eng))

    def _emit(self, e, eng):
        known = {}
        K = NDMA_SEM
        for o in self.ops[e]:
            waits = {}
            for d in o.deps:
                if d.dma:
                    sem = self.dsem[d.eng][d.didx % K]
                    val = 16 * (d.didx // K + 1)
                else:
                    sem = self.esem[d.eng]
                    val = d.sidx
                k = sem.num
                if waits.get(k, (None, 0))[1] < val:
                    waits[k] = (sem, val)
            if o.dma and o.didx >= K:
                sem = self.dsem[e][o.didx % K]
                val = 16 * (o.didx // K)
                if waits.get(sem.num, (None, 0))[1] < val:
                    waits[sem.num] = (sem, val)
            for k, (sem, val) in waits.items():
                if known.get(k, 0) < val:
                    eng.wait_ge(sem, val)
                    known[k] = val
            inst = o.fn(eng)
            if inst is None:
                continue
            if o.dma:
                inst.then_inc(self.dsem[e][o.didx % K], 16)
            elif o.sig:
                inst.then_inc(self.esem[e], 1)


CTX, LAT, DM, HD, C = 256, 4096, 1024, 128, 128
T = CTX + LAT
NCH = T // C
OWN0, OWN1 = 2, 18
NOWN = 16
DFF = 2816
EPS = 1e-6
U8 = mybir.dt.uint8

CB_I, CB_MI, CB_MS, CB_LM, CB_DIST, CB_ONES, NCB = 0, 1, 3, 5, 19, 21, 22
RO_GN, RO_RN, RO_AL, RO_DTB, RO_RL, NRS = 0, 128, 256, 264, 272, 280
RO_FG, RO_B1, RO_B2, NRO = 280, 1304, 2328, 3352


def interleave(gens):
    gens = list(gens)
    while gens:
        nxt = []
        for g in gens:
            try:
                next(g)
                nxt.append(g)
            except StopIteration:
                pass
        gens = nxt


def chain(gs):
    for g in gs:
        yield from g


def delayed(g, n):
    for _ in range(n):
        yield
    yield from g


MARKS = []


def build(dbg=False):
    nc = bass.Bass("TRN2", target_bir_lowering=False)

    def IN(name, shape):
        return nc.dram_tensor(name, list(shape), F32, kind="ExternalInput").ap()

    xs = IN("xs", [T, DM]); xo = IN("xo", [2048, DM])
    cvec = IN("cvec", [128, 16]); ada_w = IN("ada_w", [DM, 6 * DM]); adabT = IN("adabT", [128, 48])
    gmixT = IN("gmixT", [128, 8]); gffnT = IN("gffnT", [128, 8])
    w_in = IN("w_in", [DM, 4112]); w_ab = IN("w_ab", [DM, 16]); convT = IN("convT", [128, 12, 5])
    rowc = IN("rowc", [128, NRO]); cs = IN("cs", [128, NCH, 128]); cst = IN("cst", [128, NCB * 128]); csm = IN("csm", [128, 24])
    w_out = IN("w_out", [DM, DM]); w_fi = IN("w_fi", [DM, 2 * DFF]); w_fo = IN("w_fo", [DFF, DM])
    out = nc.dram_tensor("out", [2048, DM], F32, kind="ExternalOutput").ap()
    dbg_out = {}

    P = Prog(nc)
    MARKS.clear()
    def MK(name):
        MARKS.append((name, len(P.ops['pe'])))
    with P:
        R_ = P.res
        cstf = P.sb("cstf", [128, 8 * 128], F32); r_cst = R_()
        P.dma("sp", cstf[:, 0:640], cst[:, 0:640], W=[r_cst])
        P.dma("sp", cstf[:, 640:1024], cst[:, CB_DIST * 128:NCB * 128], W=[r_cst])
        csmf = P.sb("csmf", [128, 24], F32)
        P.dma("sp", csmf[:], csm[:, :], W=[r_cst])
        rowf = P.sb("rowf", [128, NRS], F32)
        P.dma("sp", rowf[:], rowc[:, 0:NRS], W=[r_cst])
        convf = P.sb("convf", [128, 12, 5], F32)
        P.dma("sp", convf[:], convT[:, :, :], W=[r_cst])
        lm8 = P.sb("lm8", [128, 14, 2, 128], U8); idb = P.sb("idb", [128, 128], BF16); r_c2 = R_()
        with nc.sbuf_tensor("lmf", [128, 14 * 128], F32) as lmf:
            r_lmf = R_()
            P.dma("sp", lmf[:], cst[:, CB_LM * 128:(CB_LM + 14) * 128], W=[r_lmf])
            for g_ in range(2):
                P.cp("dve", lm8[:, :, g_, :], lmf[:].rearrange("p (a x) -> p a x", x=128), R=[r_lmf], W=[r_c2])
            P.barrier()
        P.cp("dve", idb[:], cstf[:, 0:128], R=[r_cst], W=[r_c2])
        RC = [r_cst, r_c2]

        def cb(i):
            if i >= CB_DIST:
                i -= 14
            return cstf[:, i * 128:(i + 1) * 128]
        I_f = cb(CB_I); ones_f = cb(CB_ONES)
        MI = [cb(CB_MI), cb(CB_MI + 1)]; MS = [cb(CB_MS), cb(CB_MS + 1)]; DIST = [cb(CB_DIST), cb(CB_DIST + 1)]

        def LM8(d, l):
            return lm8[:, d * 7 + (l - 1), :, :]

        banks = [P.ps("bank%d" % i) for i in range(8)]
        bres = [R_() for _ in range(8)]
        bctr = [0]

        def PS():
            i = bctr[0] % 8
            bctr[0] += 1
            return banks[i], bres[i]

        modT = P.sb("modT", [128, 48, 2], F32); r_mod = R_()
        g1row = P.sb("g1row", [128, DM], BF16); g2row = P.sb("g2row", [128, DM], BF16); r_grow = R_()
        sc1 = P.sb("sc1", [128, 8, 4], F32)
        sc2 = P.sb("sc2", [128, 8, 2], F32); r_sc = R_()
        with contextlib.ExitStack() as es:
            adw = [es.enter_context(nc.sbuf_tensor("adw%d" % i, [128, 8, 512], F32)) for i in range(2)]
            r_adw = [R_(), R_()]
            cv = es.enter_context(nc.sbuf_tensor("cv", [128, 16], F32)); r_cv = R_()
            scv = es.enter_context(nc.sbuf_tensor("scv", [128, 8, 2], F32))
            mrow = es.enter_context(nc.sbuf_tensor("mrow", [2, 6 * DM], F32)); r_mrow = R_()
            P.dma("sp", cv[:], cvec[:, :], W=[r_cv])
            brow = es.enter_context(nc.sbuf_tensor("brow", [128, 2048], F32))
            P.dma("sp", brow[:], rowc[:, RO_B1:RO_B1 + 2048], W=[r_cv])
            P.act(scv[:].rearrange("p k t -> p t k"), cv[:].rearrange("p (t k) -> p t k", t=2), AF.Silu, R=[r_cv], W=[r_cv])
            adv = ada_w.rearrange("(k p) n -> p k n", p=128)
            for blk in range(12):
                bi = blk % 2
                P.dma("sp", adw[bi][:], adv[:, :, blk * 512:(blk + 1) * 512], W=[r_adw[bi]])
                pr, r_pr = banks[1 + blk % 2], bres[1 + blk % 2]
                for k in range(8):
                    P.mm(pr[0:2, :], scv[:, k, :], adw[bi][:, k, :], k == 0, k == 7, R=[r_adw[bi], r_cv], W=[r_pr])
                P.cp("act", mrow[0:2, blk * 512:(blk + 1) * 512], pr[0:2, :], R=[r_pr], W=[r_mrow])
            pm, r_pm = banks[0], bres[0]
            for n in range(48):
                P.tr(pm[:, n * 2:n * 2 + 2], mrow[0:2, n * 128:(n + 1) * 128], I_f[0:2, 0:2], R=[r_mrow] + RC, W=[r_pm])
            for gi_, (dstrow, col0) in enumerate(((g1row, 2 * DM), (g2row, 5 * DM))):
                for hf in range(2):
                    pg, r_pg = banks[3 + hf], bres[3 + hf]
                    P.mm(pg[:, :], ones_f[0:1, 0:128], mrow[0:1, col0 + hf * 512:col0 + (hf + 1) * 512], True, True, R=[r_mrow] + RC, W=[r_pg])
                    P.tt("dve", dstrow[:, hf * 512:(hf + 1) * 512], pg[:, :], brow[:, gi_ * 1024 + hf * 512:gi_ * 1024 + (hf + 1) * 512], ALU.add,
                         R=[r_pg, r_cv], W=[r_grow])
            abT = es.enter_context(nc.sbuf_tensor("abT", [128, 48], F32))
            gm = es.enter_context(nc.sbuf_tensor("gm", [128, 16], F32))
            P.dma("sp", abT[:], adabT[:, :], W=[r_cv])
            P.dma("sp", gm[:, 0:8], gmixT[:, :], W=[r_cv])
            P.dma("sp", gm[:, 8:16], gffnT[:, :], W=[r_cv])
            P.tt("dve", modT[:], pm[:, 0:96].rearrange("p (n t) -> p n t", t=2), abT[:].unsqueeze(2).to_broadcast([128, 48, 2]),
                 ALU.add, R=[r_pm, r_cv], W=[r_mod])
            P.stt("dve", sc1[:, :, 0], modT[:, 8:16, 0], 1.0, gm[:, 0:8], ALU.add, ALU.mult, R=[r_mod, r_cv], W=[r_sc])
            P.cp("dve", sc1[:, :, 1], modT[:, 0:8, 0], R=[r_mod], W=[r_sc])
            P.stt("dve", sc1[:, :, 2], modT[:, 8:16, 1], 1.0, gm[:, 0:8], ALU.add, ALU.mult, R=[r_mod, r_cv], W=[r_sc])
            P.cp("dve", sc1[:, :, 3], modT[:, 0:8, 1], R=[r_mod], W=[r_sc])
            P.stt("dve", sc2[:, :, 0], modT[:, 32:40, 0], 1.0, gm[:, 8:16], ALU.add, ALU.mult, R=[r_mod, r_cv], W=[r_sc])
            P.cp("dve", sc2[:, :, 1], modT[:, 24:32, 0], R=[r_mod], W=[r_sc])
            P.barrier()

        def norm_block(src, r_src, dstT, r_dst, col0, scl, shf, tmp):
            sq, ss, xn, r_t = tmp
            P.act(sq[:], src, AF.Square, accum=ss[:, 0:1], R=[r_src], W=[r_t])
            P.act(ss[:, 1:2], ss[:, 0:1], AF.Ln, bias=EPS, scale=1.0 / DM, R=[r_t], W=[r_t])
            P.act(ss[:, 2:3], ss[:, 1:2], AF.Exp, scale=-0.5, R=[r_t], W=[r_t])
            P.ts("dve", xn[:], src, ss[:, 2:3], None, ALU.mult, R=[r_src, r_t], W=[r_t])
            pt, r_pt = PS()
            ptb = pt[:, :].bitcast(BF16)
            for k in range(8):
                P.tr(ptb[:, k * 128:(k + 1) * 128], xn[:, k * 128:(k + 1) * 128], idb[:], R=[r_t, r_c2], W=[r_pt])
            for k in range(8):
                if k % 2 == 0:
                    P.act(dstT[:, k, col0:col0 + 128], ptb[:, k * 128:(k + 1) * 128], AF.Identity, bias=shf(k), scale=scl(k),
                          R=[r_pt, r_sc], W=[r_dst])
                else:
                    P.ts("dve", dstT[:, k, col0:col0 + 128], ptb[:, k * 128:(k + 1) * 128], scl(k), shf(k), ALU.mult, ALU.add,
                         R=[r_pt, r_sc], W=[r_dst])

        yT = P.sb("yT", [128, 8, 2048], BF16); r_yT = [R_() for _ in range(8)]
        wv = w_in.rearrange("(k p) n -> p k n", p=128)

        with contextlib.ExitStack() as es:
            def SB(name, shape, dt):
                return es.enter_context(nc.sbuf_tensor(name, list(shape), dt))
            hT = SB("hT", [128, 8, T], BF16); r_hT = [R_() for _ in range(NCH)]
            MK('B')
            esB = contextlib.ExitStack()
            def SBB(name, shape, dt):
                return esB.enter_context(nc.sbuf_tensor(name, list(shape), dt))
            xts = [SBB("xt%d" % i, [128, DM], F32) for i in range(2)]; r_xt = [R_(), R_()]
            ntmp = [(SBB("nsq%d" % i, [128, DM], BF16), SBB("nss%d" % i, [128, 4], F32), SBB("nxn%d" % i, [128, DM], BF16), R_()) for i in range(2)]
            for c in range(NCH):
                i = c % 2
                P.dma("sp", xts[i][:], xs[c * 128:(c + 1) * 128, :], W=[r_xt[i]])
                o_ = 2 if c < 2 else 0
                norm_block(xts[i][:], r_xt[i], hT, r_hT[c], c * 128,
                           lambda k, o_=o_: sc1[:, k, o_:o_ + 1], lambda k, o_=o_: sc1[:, k, o_ + 1:o_ + 2], ntmp[i])
            if dbg:
                dbg_out["hT"] = nc.dram_tensor("d_hT", [128, 8, T], BF16, kind="ExternalOutput").ap()
                P.out_dma("sp", dbg_out["hT"][:, :, :], hT[:], R=r_hT)

            P.barrier()
            esB.close()
            MK('C')
            wab = SB("wab", [128, 8, 16], BF16); r_wab = R_()
            P.dma("pool", wab[:], w_ab.rearrange("(k p) n -> p k n", p=128), W=[r_wab])
            abs_ = SB("abs", [128, 16, NCH], F32); r_ab = R_()
            pa, r_pa = PS(); pb, r_pb = PS()
            for c in range(NCH):
                tgt = pa[:, c * 16:(c + 1) * 16] if c < 32 else pb[:, (c - 32) * 16:(c - 31) * 16]
                for k in range(8):
                    P.mm(tgt, hT[:, k, c * 128:(c + 1) * 128], wab[:, k, :], k == 0, k == 7, R=[r_hT[c], r_wab],
                         W=[r_pa if c < 32 else r_pb])
            P.cp("dve", abs_[:, :, 0:32].rearrange("p j c -> p c j"), pa[:, 0:512].rearrange("p (c j) -> p c j", j=16), R=[r_pa], W=[r_ab])
            P.cp("dve", abs_[:, :, 32:34].rearrange("p j c -> p c j"), pb[:, 0:32].rearrange("p (c j) -> p c j", j=16), R=[r_pb], W=[r_ab])
            gal = SB("gal", [128, 8, NCH], F32); sal = SB("sal", [128, 8, NCH], F32); Gal = SB("Gal", [128, 8, NCH], F32)
            eG = SB("eG", [128, 8, NCH], F32); neG = SB("neG", [128, 8, NCH], F32); eGl = SB("eGl", [128, 8, NCH], F32)
            ekd = SB("ekd", [128, 8, NCH], F32); nA8 = SB("nA8", [128, 8], F32); r_g = R_()
            P.tt("dve", gal[:], abs_[:, 0:8, :], rowf[:, RO_DTB:RO_DTB + 8].unsqueeze(2).to_broadcast([128, 8, NCH]), ALU.add, R=[r_ab] + RC, W=[r_g])
            P.act(gal[:], gal[:], AF.Exp, R=[r_g], W=[r_g])
            P.act(gal[:], gal[:], AF.Ln, bias=1.0, R=[r_g], W=[r_g])
            P.act(nA8[:], rowf[:, RO_AL:RO_AL + 8], AF.Exp, R=RC, W=[r_g])
            P.stt("dve", gal[:], gal[:], -1.0, nA8[:].unsqueeze(2).to_broadcast([128, 8, NCH]), ALU.mult, ALU.mult, R=[r_g], W=[r_g])
            P.act(sal[:], abs_[:, 8:16, :], AF.Sigmoid, R=[r_ab], W=[r_g])
            P.act(sal[:], sal[:], AF.Sqrt, R=[r_g], W=[r_g])
            galf = gal[:].rearrange("p a c -> p (a c)")
            for d in range(2):
                pg, r_pg = PS()
                P.mm(pg[:, 0:4 * NCH], MI[d], galf[:, d * 4 * NCH:(d + 1) * 4 * NCH], True, True, R=[r_g] + RC, W=[r_pg])
                P.cp("dve", Gal[:, d * 4:(d + 1) * 4, :].rearrange("p a c -> p (a c)"), pg[:, 0:4 * NCH], R=[r_pg], W=[r_g])
            pg, r_pg = PS()
            P.mm(pg[:, 0:8 * NCH], ones_f, galf, True, True, R=[r_g] + RC, W=[r_pg])
            P.act(eGl[:].rearrange("p a c -> p (a c)"), pg[:, 0:8 * NCH], AF.Exp, R=[r_pg], W=[r_g])
            P.tt("dve", ekd[:].rearrange("p a c -> p (a c)"), pg[:, 0:8 * NCH], Gal[:].rearrange("p a c -> p (a c)"), ALU.subtract, R=[r_pg, r_g], W=[r_g])
            P.act(ekd[:], ekd[:], AF.Exp, R=[r_g], W=[r_g])
            P.act(eG[:], Gal[:], AF.Exp, R=[r_g], W=[r_g])
            P.ts("dve", neG[:], eG[:], -1.0, None, ALU.mult, R=[r_g], W=[r_g])

            junk = SB("junk", [128, 384], F32); r_junk = R_(); ysc = SB("ysc", [128, 128], BF16); r_ysc = R_()
            kbar = SB("kbar", [128, NCH, 128], BF16); r_kbar = [R_() for _ in range(NCH)]
            vtok = SB("vtok", [128, NCH, 128], BF16); r_vtok = [R_() for _ in range(NCH)]
            qT = SB("qT", [128, 2048], BF16); r_qT = R_()
            zs = SB("zs", [128, NOWN, 128], BF16); r_zs = R_()
            oacc = SB("oacc", [128, NOWN, 128], F32); r_oacc = [R_() for _ in range(NOWN)]
            sm = SB("sm", [128, 8], F32); r_sm = R_()
            smk = SB("smk", [128, 3, 4], F32); r_smk = [R_(), R_(), R_()]
            Sf = [SB("Sf%d" % d, [128, 128], F32) for d in range(2)]; Sb = [SB("Sb%d" % d, [128, 128], BF16) for d in range(2)]
            r_S = [R_(), R_()]
            esD = contextlib.ExitStack()
            def SBD(name, shape, dt):
                return esD.enter_context(nc.sbuf_tensor(name, list(shape), dt))
            u = SBD("u", [128, T], BF16); yc = SBD("yc", [128, T], BF16); kact = yc; r_u = R_(); r_yc = R_(); r_ka = r_yc
            wq = SBD("wq", [128, 8, 128], BF16); r_wq = R_()
            oss = SBD("oss", [128, 2, NOWN], F32); r_oss = R_(); ysc2 = SBD("ysc2", [128, 128], BF16); r_ysc2 = R_(); r_jk = [R_(), R_()]
            NR = 4
            GB = 2

            def bfv(base, off, n=GB * 128):
                return base[:, off:off + n].rearrange("p (g x) -> p g x", x=128)

            def f32v(base, off):
                return base[:, off:off + 2 * GB * 128].bitcast(F32).rearrange("p (g x) -> p g x", x=128)

            def mk_tset(i):
                return dict(kp=SBD("pkp%d" % i, [128, GB, 128], BF16), nA=SBD("pnA%d" % i, [128, GB, 128], BF16),
                            nAT=SBD("pnAT%d" % i, [128, GB, 128], BF16), D=SBD("pD%d" % i, [128, GB, 128], BF16),
                            X=SBD("pX%d" % i, [128, GB, 128], BF16), rg=SBD("prg%d" % i, [128, GB, 128], F32),
                            ex=SBD("pex%d" % i, [128, GB, 128], F32), exT=SBD("pexT%d" % i, [128, GB, 128], F32), res=R_())

            def mk_gslot(i):
                return dict(E=SBD("gE%d" % i, [128, GB, 128], BF16), exTI=SBD("gX%d" % i, [128, GB, 128], BF16),
                            kT=SBD("gkT%d" % i, [128, GB, 128], BF16), kpp=SBD("gkpp%d" % i, [128, GB, 128], BF16),
                            vp=SBD("gvp%d" % i, [128, GB, 128], BF16), res=R_())
            W_ = GB * 128
            tsets = [mk_tset(0), mk_tset(1),
                     dict(kp=bfv(u, 0), nA=bfv(u, W_), nAT=bfv(u, 2 * W_), D=bfv(u, 3 * W_), X=bfv(u, 4 * W_),
                          rg=f32v(u, 5 * W_), ex=f32v(u, 7 * W_), exT=f32v(u, 9 * W_), res=r_u),
                     dict(kp=bfv(yc, 5 * W_), nA=bfv(yc, 6 * W_), nAT=bfv(yc, 7 * W_), D=bfv(yc, 8 * W_), X=bfv(yc, 9 * W_),
                          rg=f32v(yc, 10 * W_), ex=f32v(yc, 12 * W_), exT=f32v(yc, 14 * W_), res=r_yc)]
            NGS = 8
            gring = [mk_gslot(i) for i in range(NGS - 2)]
            gring.append(dict(E=bfv(u, 11 * W_), exTI=bfv(u, 12 * W_), kT=bfv(u, 13 * W_), kpp=bfv(u, 14 * W_), vp=bfv(u, 15 * W_), res=r_u))
            gring.append(dict(E=bfv(yc, 0), exTI=bfv(yc, W_), kT=bfv(yc, 2 * W_), kpp=bfv(yc, 3 * W_), vp=bfv(yc, 4 * W_), res=r_yc))
            assert 16 * W_ <= T
            NSS = 2
            sring = [dict(Rr=SBD("sR%d" % i, [128, 128], BF16), vn=SBD("svn%d" % i, [128, 128], BF16), qkm=SBD("sqkm%d" % i, [128, 128], BF16),
                          o1=SBD("so1%d" % i, [128, 128], F32), res2=R_()) for i in range(NSS)]

            def conv_silu(ci, lo, hi, parts, dst, r_dst):
                first = True
                for tap in (2, 0, 1, 3, 4):
                    s = tap - 2
                    for (a, e) in parts:
                        l2, h2 = max(a, a - s, lo), min(e, e - s, hi)
                        if tap == 2:
                            P.ts("dve", yc[:, l2:h2], u[:, l2:h2], convf[:, ci, tap:tap + 1], None, ALU.mult, R=[r_u] + RC, W=[r_yc])
                        else:
                            eng = "dve"
                            P.stt(eng, yc[:, l2:h2], u[:, l2 + s:h2 + s], convf[:, ci, tap:tap + 1], yc[:, l2:h2], ALU.mult, ALU.add,
                                  R=[r_u, r_yc] + RC, W=[r_yc])
                P.act(dst, yc[:, lo:hi], AF.Silu, R=[r_yc], W=[r_dst])

            def inproj_fm(col, t0, t1, wt, r_wt):
                P.dma("pool", wt[:], wv[:, :, col:col + 128], W=[r_wt])
                t = t0
                i = 0
                while t < t1:
                    n = min(512, t1 - t)
                    pp, r_pp = PS()
                    for k in range(8):
                        P.mm(pp[:, 0:n], wt[:, k, :], hT[:, k, t:t + n], k == 0, k == 7, R=[r_wt] + r_hT, W=[r_pp])
                    if i % 2 == 0:
                        P.cp("act", u[:, t:t + n], pp[:, 0:n], R=[r_pp], W=[r_u])
                    else:
                        P.cp("dve", u[:, t:t + n], pp[:, 0:n], R=[r_pp], W=[r_u])
                    t += n
                    i += 1

            def fl(ap):
                return ap.rearrange("p g x -> p (g x)")

            def bc(ap2d, G):
                return ap2d.unsqueeze(1).to_broadcast([128, G, 128])

            def gdn_prep(h, d, c0, G, gs, full, ti):
                tm = tsets[ti]; r1 = tm["res"]; gr = gring[gs]; rg_ = gr["res"]
                j = d * 4 + h
                scb = sal[:, j, c0:c0 + G].unsqueeze(2).to_broadcast([128, G, 128])
                P.tt("pool", tm["kp"][:, 0:G, :], kbar[:, c0:c0 + G, :], scb, ALU.mult, R=r_kbar[c0:c0 + G] + [r_g], W=[r1])
                P.tt("pool", gr["kpp"][:, 0:G, :], tm["kp"][:, 0:G, :], ekd[:, j, c0:c0 + G].unsqueeze(2).to_broadcast([128, G, 128]), ALU.mult,
                     R=[r1, r_g], W=[rg_])
                P.tt("pool", gr["vp"][:, 0:G, :], vtok[:, c0:c0 + G, :], scb, ALU.mult, R=r_vtok[c0:c0 + G] + [r_g], W=[rg_])
                pt, r_pt = PS(); ptb = pt[:, 0:G * 64].bitcast(BF16)
                for g in range(G):
                    P.tr(ptb[:, g * 128:(g + 1) * 128], tm["kp"][:, g, :], idb[:], R=[r1, r_c2], W=[r_pt])
                P.cp("act", fl(gr["kT"][:, 0:G, :]), ptb, R=[r_pt], W=[rg_])
                yield
                P.tt("pool", tm["rg"][:, 0:G, :], bc(MS[d], G), gal[:, j, c0:c0 + G].unsqueeze(2).to_broadcast([128, G, 128]), ALU.mult,
                     R=[r_g] + RC, W=[r1])
                p1, r_p1 = PS(); p2, r_p2 = PS()
                for g in range(G):
                    P.mm(p1[:, g * 128:(g + 1) * 128], MI[d], tm["rg"][:, g, :], True, True, R=[r1] + RC, W=[r_p1])
                for g in range(G):
                    P.mm(p2[:, g * 128:(g + 1) * 128], tm["rg"][:, g, :], MI[d], True, True, R=[r1] + RC, W=[r_p2])
                P.act(fl(tm["ex"][:, 0:G, :]), p1[:, 0:G * 128], AF.Exp, R=[r_p1], W=[r1])
                P.act(fl(tm["exT"][:, 0:G, :]), p2[:, 0:G * 128], AF.Exp, R=[r_p2], W=[r1])
                yield
                P.tt("pool", tm["ex"][:, 0:G, :], tm["ex"][:, 0:G, :], bc(MS[d], G), ALU.mult, R=[r1] + RC, W=[r1])
                if full:
                    P.tt("pool", gr["exTI"][:, 0:G, :], tm["exT"][:, 0:G, :], bc(MI[d], G), ALU.mult, R=[r1] + RC, W=[rg_])
                P.tt("pool", tm["exT"][:, 0:G, :], tm["exT"][:, 0:G, :], bc(MS[1 - d], G), ALU.mult, R=[r1] + RC, W=[r1])
                pk, r_pk = PS()
                for g in range(G):
                    P.mm(pk[:, g * 128:(g + 1) * 128], gr["kT"][:, g, :], gr["kT"][:, g, :], True, True, R=[rg_], W=[r_pk])
                P.stt("dve", fl(tm["nA"][:, 0:G, :]), pk[:, 0:G * 128], -1.0, fl(tm["ex"][:, 0:G, :]), ALU.mult, ALU.mult, R=[r_pk, r1], W=[r1])
                P.stt("dve", fl(tm["nAT"][:, 0:G, :]), pk[:, 0:G * 128], -1.0, fl(tm["exT"][:, 0:G, :]), ALU.mult, ALU.mult, R=[r_pk, r1], W=[r1])
                yield
                P.cp("pool", tm["D"][:, 0:G, :], bc(idb[:], G), R=[r_c2], W=[r1])
                P.cp("pool", gr["E"][:, 0:G, :], bc(idb[:], G), R=[r_c2], W=[rg_])
                P.op("dve", lambda e: e.copy_predicated(tm["D"][:, 0:G, :], LM8(d, 1), tm["nA"][:, 0:G, :]), R=[r1, r_c2], W=[r1])
                P.op("dve", lambda e: e.copy_predicated(gr["E"][:, 0:G, :], LM8(1 - d, 1), tm["nAT"][:, 0:G, :]), R=[r1, rg_, r_c2], W=[rg_])
                yield
                for l in range(2, 8):
                    px, r_px = PS()
                    for g in range(G):
                        P.mm(px[:, g * 128:(g + 1) * 128], tm["nA"][:, g, :], gr["E"][:, g, :], True, True, R=[r1, rg_], W=[r_px])
                    P.cp("act", fl(tm["X"][:, 0:G, :]), px[:, 0:G * 128], R=[r_px], W=[r1])
                    yield
                    pe_, r_pe = PS()
                    for g in range(G):
                        P.mm(pe_[:, g * 128:(g + 1) * 128], tm["D"][:, g, :], tm["X"][:, g, :], True, True, R=[r1], W=[r_pe])
                    if l < 7:
                        pd, r_pd = PS()
                        for g in range(G):
                            P.mm(pd[:, g * 128:(g + 1) * 128], tm["X"][:, g, :], tm["D"][:, g, :], True, True, R=[r1], W=[r_pd])
                    P.op("dve", lambda e, pe_=pe_, l=l: e.copy_predicated(gr["E"][:, 0:G, :], LM8(1 - d, l),
                                                                        pe_[:, 0:G * 128].rearrange("p (g x) -> p g x", x=128)),
                         R=[r_pe, rg_, r_c2], W=[rg_])
                    if l < 7:
                        P.op("dve", lambda e, pd=pd, l=l: e.copy_predicated(tm["D"][:, 0:G, :], LM8(d, l),
                                                                          pd[:, 0:G * 128].rearrange("p (g x) -> p g x", x=128)),
                             R=[r_pd, r1, r_c2], W=[r1])
                    yield

            sctr = [0]

            def gdn_scan(h, d, c, gs, g, full):
                gr = gring[gs]; r1 = gr["res"]
                rs = sring[sctr[0] % NSS]; sctr[0] += 1
                r2 = rs["res2"]
                j = d * 4 + h
                kT_ = gr["kT"][:, g, :]
                oc = c - OWN0
                pk, r_pk = PS()
                P.mm(pk[:, 0:128], kT_, Sb[d][:], True, True, R=[r1, r_S[d]], W=[r_pk])
                if full:
                    po1, r_po1 = PS()
                    P.mm(po1[:, 0:128], qT[:, oc * 128:(oc + 1) * 128], Sb[d][:], True, True, R=[r_qT, r_S[d]], W=[r_po1])
                P.stt("dve", rs["Rr"][:], pk[:, 0:128], neG[:, j, c:c + 1], gr["vp"][:, g, :], ALU.mult, ALU.add, R=[r_pk, r1, r_g], W=[r2])
                if full:
                    P.act(rs["o1"][:], po1[:, 0:128], AF.Identity, scale=eG[:, j, c:c + 1], R=[r_po1, r_g], W=[r2])
                yield
                pv, r_pv = PS()
                P.mm(pv[:, 0:128], gr["E"][:, g, :], rs["Rr"][:], True, True, R=[r1, r2], W=[r_pv])
                P.cp("act", rs["vn"][:], pv[:, 0:128], R=[r_pv], W=[r2])
                yield
                ps_, r_ps = PS()
                P.mm(ps_[:, 0:128], gr["kpp"][:, g, :], rs["vn"][:], True, True, R=[r1, r2], W=[r_ps])
                P.stt("dve", Sf[d][:], Sf[d][:], eGl[:, j, c:c + 1], ps_[:, 0:128], ALU.mult, ALU.add, R=[r_S[d], r_ps, r_g], W=[r_S[d]])
                P.cp("act", Sb[d][:], Sf[d][:], R=[r_S[d]], W=[r_S[d]])
                yield
                if full:
                    pq, r_pq = PS()
                    P.mm(pq[:, 0:128], kT_, qT[:, oc * 128:(oc + 1) * 128], True, True, R=[r1, r_qT], W=[r_pq])
                    P.tt("dve", rs["qkm"][:], pq[:, 0:128], gr["exTI"][:, g, :], ALU.mult, R=[r_pq, r1], W=[r2])
                    yield
                    po2, r_po2 = PS()
                    P.mm(po2[:, 0:128], rs["qkm"][:], rs["vn"][:], True, True, R=[r2], W=[r_po2])
                    P.tt("dve", rs["o1"][:], rs["o1"][:], po2[:, 0:128], ALU.add, R=[r2, r_po2], W=[r2])
                    P.tt("pool", oacc[:, oc, :], oacc[:, oc, :], rs["o1"][:], ALU.add, R=[r2, r_oacc[oc]], W=[r_oacc[oc]])
                    yield

            chF = [0, 1] + list(range(OWN0, OWN1))
            chR = [1, 0] + list(range(NCH - 1, OWN0 - 1, -1))

            def sched_tasks():
                tasks = []
                iF = iR = 0
                while iF < len(chF) or iR < len(chR):
                    if iR < len(chR):
                        tasks.append((1, chR[iR])); iR += 1
                    if iF < len(chF) and iR * len(chF) >= iF * len(chR):
                        tasks.append((0, chF[iF])); iF += 1
                return tasks

            for h in range(4):
                MK('gdn%d_prep' % h)
                inproj_fm(512 + h * 128, 0, T, wq, r_wq)
                conv_silu(4 + h, 0, T, [(0, CTX), (CTX, T)], kact[:, :], r_ka)
                for c in range(NCH):
                    pt, r_pt = PS(); ptb = pt[:, 0:64].bitcast(BF16)
                    P.tr(ptb, kact[:, c * 128:(c + 1) * 128], idb[:], R=[r_ka, r_c2], W=[r_pt])
                    q_ = c % 3
                    P.act(junk[:, q_ * 128:(q_ + 1) * 128], ptb, AF.Square, accum=smk[:, q_, 0:1], R=[r_pt], W=[r_smk[q_]])
                    P.act(smk[:, q_, 1:2], smk[:, q_, 0:1], AF.Ln, bias=EPS, R=[r_smk[q_]], W=[r_smk[q_]])
                    P.act(smk[:, q_, 2:3], smk[:, q_, 1:2], AF.Exp, scale=-0.5, R=[r_smk[q_]], W=[r_smk[q_]])
                    P.ts("dve", kbar[:, c, :], ptb, smk[:, q_, 2:3], None, ALU.mult, R=[r_pt, r_smk[q_]], W=[r_kbar[c]])
                inproj_fm(1024 + h * 128, 0, T, wq, r_wq)
                conv_silu(8 + h, 0, T, [(0, CTX), (CTX, T)], kact[:, :], r_ka)
                for c in range(NCH):
                    pt, r_pt = PS(); ptb = pt[:, 0:64].bitcast(BF16)
                    P.tr(ptb, kact[:, c * 128:(c + 1) * 128], idb[:], R=[r_ka, r_c2], W=[r_pt])
                    P.cp("act" if c % 2 else "dve", vtok[:, c, :], ptb, R=[r_pt], W=[r_vtok[c]])
                inproj_fm(h * 128, CTX, CTX + 2048 + 128, wq, r_wq)
                conv_silu(h, CTX, CTX + 2048, [(CTX, T)], kact[:, CTX:CTX + 2048], r_ka)
                qsq = oacc[:].rearrange("p a b -> p (a b)"); r_qsq = r_oacc[0]
                P.act(qsq, kact[:, CTX:CTX + 2048], AF.Square, R=[r_ka] + r_oacc, W=r_oacc)
                for t in range(4):
                    pp, r_pp = PS()
                    P.mm(pp[:, :], ones_f, qsq[:, t * 512:(t + 1) * 512], True, True, R=[r_qsq] + RC, W=[r_pp])
                    P.act(qsq[:, t * 512:(t + 1) * 512], pp[:, :], AF.Ln, bias=EPS, R=[r_pp, r_qsq], W=[r_qsq])
                    P.act(qsq[:, t * 512:(t + 1) * 512], qsq[:, t * 512:(t + 1) * 512], AF.Exp, scale=-0.5, R=[r_qsq], W=[r_qsq])
                P.stt("dve", qT[:], kact[:, CTX:CTX + 2048], float(HD) ** -0.5, qsq, ALU.mult, ALU.mult, R=[r_ka, r_qsq], W=[r_qT])
                P.dma("pool", wq[:], wv[:, :, 1536 + h * 128:1536 + (h + 1) * 128], W=[r_wq])
                for oc in range(NOWN):
                    c = OWN0 + oc
                    pz, r_pz = PS()
                    for k in range(8):
                        P.mm(pz[:, 0:128], hT[:, k, c * 128:(c + 1) * 128], wq[:, k, :], k == 0, k == 7, R=[r_hT[c], r_wq], W=[r_pz])
                    P.act(zs[:, oc, :], pz[:, 0:128], AF.Silu, R=[r_pz], W=[r_zs])
                MK('gdn%d_scan' % h)
                for d in range(2):
                    P.op("pool", lambda e, d=d: e.memset(Sf[d][:], 0.0), W=[r_S[d]])
                    P.op("pool", lambda e, d=d: e.memset(Sb[d][:], 0.0), W=[r_S[d]])
                P.op("pool", lambda e: e.memset(oacc[:], 0.0), W=r_oacc)
                Fg = [(0, c0, 2, [0, 1]) for c0 in [0] + list(range(OWN0, OWN1, 2))]
                Rg = [(1, c0, 2, [1, 0]) for c0 in [0] + list(range(NCH - 2, OWN0 - 1, -2))]
                groups = []
                iF = iR = 0
                while iF < len(Fg) or iR < len(Rg):
                    if iR < len(Rg):
                        groups.append(Rg[iR]); iR += 1
                    if iF < len(Fg) and iR * len(Fg) >= iF * len(Rg):
                        groups.append(Fg[iF]); iF += 1
                WV = 4
                prevw = []
                for w0 in range(0, len(groups) + WV, WV):
                    wave = groups[w0:w0 + WV]
                    gens = []
                    cur = []
                    for i, (d, c0, G, order) in enumerate(wave):
                        gsl = (w0 + i) % NGS
                        full = OWN0 <= c0 < OWN1
                        gens.append(delayed(gdn_prep(h, d, c0, G, gsl, full, i), i % 2))
                        cur.append((d, c0, G, order, gsl, full))
                    for dd in range(2):
                        sc_ = []
                        for (pd_, pc0, pG, porder, pgs, pfull) in prevw:
                            if pd_ == dd:
                                sc_ += [gdn_scan(h, pd_, pc0 + g, pgs, g, pfull) for g in porder]
                        if sc_:
                            gens.append(delayed(chain(sc_), dd))
                    interleave(gens)
                    prevw = cur
                MK('gdn%d_out' % h)
                for oc in range(NOWN):
                    P.act(junk[:, 0:128], oacc[:, oc, :], AF.Square, accum=oss[:, 0, oc:oc + 1], R=[r_oacc[oc]], W=[r_junk, r_oss])
                P.act(oss[:, 0, :], oss[:, 0, :], AF.Ln, bias=EPS, scale=1.0 / HD, R=[r_oss], W=[r_oss])
                P.act(oss[:, 0, :], oss[:, 0, :], AF.Exp, scale=-0.5, R=[r_oss], W=[r_oss])
                for oc in range(NOWN):
                    i_ = oc % 2
                    jb = junk[:, 128 + i_ * 128:256 + i_ * 128]
                    ys_, r_ys_ = (ysc, r_ysc) if i_ == 0 else (ysc2, r_ysc2)
                    P.stt("dve", jb, oacc[:, oc, :], oss[:, 0, oc:oc + 1], rowf[:, RO_GN:RO_GN + 128], ALU.mult, ALU.mult,
                          R=[r_oacc[oc], r_oss] + RC, W=[r_jk[i_]])
                    P.tt("pool", ys_[:], jb, zs[:, oc, :], ALU.mult, R=[r_jk[i_], r_zs], W=[r_ys_])
                    pt, r_pt = PS(); ptb = pt[:, 0:64].bitcast(BF16)
                    P.tr(ptb, ys_[:], idb[:], R=[r_ys_, r_c2], W=[r_pt])
                    P.cp("act", yT[:, h, oc * 128:(oc + 1) * 128], ptb, R=[r_pt], W=[r_yT[h]])

            P.barrier()
            esD.close()
            cst_ = SB("cs_sb", [128, NCH, 128], BF16)
            r_csg = {}
            for (c0_, G_) in [(0, 2)] + [(c0_, 4) for c0_ in range(2, NCH, 4)]:
                r_csg[c0_] = R_()
                P.dma("pool", cst_[:, c0_:c0_ + G_, :], cs[:, c0_:c0_ + G_, :], W=[r_csg[c0_]])
            ktok = kbar; r_ktok = r_kbar
            kTr = SB("kTr", [128, NCH, 128], BF16); r_kTr = [R_() for _ in range(NCH)]
            lgc = SB("lgc", [128, 8], F32); r_lg = R_()
            ring = [dict(qkm=SB("eqkm%d" % i, [128, 128], BF16), o1=SB("eo1%d" % i, [128, 128], F32), kpp=SB("ekpp%d" % i, [128, 128], BF16), res2=R_()) for i in range(NR)]
            P.act(lgc[:], rowf[:, RO_RL:RO_RL + 8], AF.Exp, scale=-1.0, R=RC, W=[r_lg])
            P.act(lgc[:], lgc[:], AF.Ln, bias=1.0, R=[r_lg], W=[r_lg])
            P.ts("dve", lgc[:], lgc[:], -1.0, None, ALU.mult, R=[r_lg], W=[r_lg])
            DrT = [SB("DrT%d" % d, [128, 128], F32) for d in range(2)]
            rsc = SB("rsc", [128, 8], F32); r_rc = R_()
            rw3 = [SB("rw%d" % i, [128, 8, 128], BF16) for i in range(4)]; r_rw3 = [R_() for _ in range(4)]

            rts = [SB("rts%d" % i, [128, 4, 4, 64], F32) for i in range(2)]; r_rts = [R_(), R_()]
            qtmp = SB("qtmp", [128, 4, 128], BF16); r_qtmp = R_()
            ross = SB("ross", [128, 4, NOWN], F32); r_ross = R_(); junkE = SB("junkE", [128, 4, 128], F32); r_jkE = [R_(), R_()]
            ysc2e = SB("ysc2e", [128, 128], BF16); r_ysc2e = R_()
            kzb = SB("kzb", [128, 16, 128], BF16); z2 = SB("z2", [128, 18], F32); r_kzb = R_()
            rctr = [0]

            def rope_group(ps, r_ps, c0, G, dst, r_dst):
                i = rctr[0] % 2
                rctr[0] += 1
                rt_, r_rt_ = rts[i], r_rts[i]
                pv3 = ps[:, 0:G * 128].rearrange("p (g x) -> p g x", x=128)
                t1 = pv3[:, :, 0:64]; t2 = pv3[:, :, 64:128]
                cosv = cst_[:, c0:c0 + G, 0:64]; sinv = cst_[:, c0:c0 + G, 64:128]
                r_cs = r_csg[c0]
                P.tt("dve", rt_[:, 0, 0:G, :], t1, cosv, ALU.mult, R=[r_ps, r_cs], W=[r_rt_])
                P.tt("dve", rt_[:, 1, 0:G, :], t2, sinv, ALU.mult, R=[r_ps, r_cs], W=[r_rt_])
                P.tt("dve", rt_[:, 2, 0:G, :], t1, sinv, ALU.mult, R=[r_ps, r_cs], W=[r_rt_])
                P.tt("dve", rt_[:, 3, 0:G, :], t2, cosv, ALU.mult, R=[r_ps, r_cs], W=[r_rt_])
                P.tt("pool", dst[:, :, 0:64], rt_[:, 0, 0:G, :], rt_[:, 1, 0:G, :], ALU.subtract, R=[r_rt_], W=r_dst)
                P.tt("pool", dst[:, :, 64:128], rt_[:, 2, 0:G, :], rt_[:, 3, 0:G, :], ALU.add, R=[r_rt_], W=r_dst)

            kgroups = [(0, 2)] + [(c0, 4) for c0 in range(2, NCH, 4)]
            for h in range(4):
                MK('ret%d_prep' % h)
                cols = [2064 + h * 128, 2576 + h * 128, 3088 + h * 128, 3600 + h * 128]
                for i in range(4):
                    P.dma("pool", rw3[i][:], wv[:, :, cols[i]:cols[i] + 128], W=[r_rw3[i]])
                for d in range(2):
                    j = d * 4 + h
                    lg = lgc[:, j:j + 1]
                    P.act(DrT[d][:], DIST[d], AF.Exp, scale=lg, R=RC + [r_lg], W=[r_rc])
                    P.stt("dve", DrT[d][:], DrT[d][:], float(HD) ** -0.5, MI[d], ALU.mult, ALU.mult, R=[r_rc] + RC, W=[r_rc])
                    P.act(rsc[:, d * 4:d * 4 + 1], csmf[:, d:d + 1], AF.Exp, scale=lg, R=RC + [r_lg], W=[r_rc])
                    P.ts("dve", rsc[:, d * 4:d * 4 + 1], rsc[:, d * 4:d * 4 + 1], float(HD) ** -0.5, None, ALU.mult, R=[r_rc], W=[r_rc])
                    P.act(rsc[:, d * 4 + 1:d * 4 + 2], csmf[:, 2 + d:3 + d], AF.Exp, scale=lg, R=RC + [r_lg], W=[r_rc])
                    P.act(rsc[:, d * 4 + 2:d * 4 + 3], csmf[:, 4:5], AF.Exp, scale=lg, R=RC + [r_lg], W=[r_rc])
                    P.op("pool", lambda e, d=d: e.memset(Sf[d][:], 0.0), W=[r_S[d]])
                    P.op("pool", lambda e, d=d: e.memset(Sb[d][:], 0.0), W=[r_S[d]])
                P.op("pool", lambda e: e.memset(oacc[:], 0.0), W=r_oacc)
                for (c0, G) in kgroups:
                    pk, r_pk = PS()
                    for g in range(G):
                        c = c0 + g
                        for k in range(8):
                            P.mm(pk[:, g * 128:(g + 1) * 128], hT[:, k, c * 128:(c + 1) * 128], rw3[1][:, k, :], k == 0, k == 7,
                                 R=[r_hT[c], r_rw3[1]], W=[r_pk])
                    rope_group(pk, r_pk, c0, G, ktok[:, c0:c0 + G, :], r_ktok[c0:c0 + G])
                    pt, r_pt = PS(); ptb = pt[:, 0:G * 64].bitcast(BF16)
                    for g in range(G):
                        P.tr(ptb[:, g * 128:(g + 1) * 128], ktok[:, c0 + g, :], idb[:], R=[r_ktok[c0 + g], r_c2], W=[r_pt])
                    P.cp("act", kTr[:, c0:c0 + G, :].rearrange("p g x -> p (g x)"), ptb, R=[r_pt], W=r_kTr[c0:c0 + G])
                    pv, r_pv = PS()
                    for g in range(G):
                        c = c0 + g
                        for k in range(8):
                            P.mm(pv[:, g * 128:(g + 1) * 128], hT[:, k, c * 128:(c + 1) * 128], rw3[2][:, k, :], k == 0, k == 7,
                                 R=[r_hT[c], r_rw3[2]], W=[r_pv])
                    P.cp("act", vtok[:, c0:c0 + G, :].rearrange("p g x -> p (g x)"), pv[:, 0:G * 128], R=[r_pv], W=r_vtok[c0:c0 + G])
                    if OWN0 <= c0 < OWN1:
                        oc0 = c0 - OWN0
                        pq, r_pq = PS()
                        for g in range(G):
                            c = c0 + g
                            for k in range(8):
                                P.mm(pq[:, g * 128:(g + 1) * 128], hT[:, k, c * 128:(c + 1) * 128], rw3[0][:, k, :], k == 0, k == 7,
                                     R=[r_hT[c], r_rw3[0]], W=[r_pq])
                        rope_group(pq, r_pq, c0, G, qtmp[:, 0:G, :], [r_qtmp])
                        pt, r_pt = PS(); ptb = pt[:, 0:G * 64].bitcast(BF16)
                        for g in range(G):
                            P.tr(ptb[:, g * 128:(g + 1) * 128], qtmp[:, g, :], idb[:], R=[r_qtmp, r_c2], W=[r_pt])
                        P.cp("act", qT[:, oc0 * 128:(oc0 + G) * 128], ptb, R=[r_pt], W=[r_qT])
                        pz, r_pz = PS()
                        for g in range(G):
                            c = c0 + g
                            for k in range(8):
                                P.mm(pz[:, g * 128:(g + 1) * 128], hT[:, k, c * 128:(c + 1) * 128], rw3[3][:, k, :], k == 0, k == 7,
                                     R=[r_hT[c], r_rw3[3]], W=[r_pz])
                        P.act(zs[:, oc0:oc0 + G, :].rearrange("p g x -> p (g x)"), pz[:, 0:G * 128], AF.Silu, R=[r_pz], W=[r_zs])

                def ret_step(d, c, slot):
                    rs = ring[slot]; r2 = rs["res2"]
                    full = OWN0 <= c < OWN1
                    oc = c - OWN0
                    if full:
                        po1, r_po1 = PS()
                        P.mm(po1[:, 0:128], qT[:, oc * 128:(oc + 1) * 128], Sb[d][:], True, True, R=[r_qT, r_S[d]], W=[r_po1])
                        P.act(rs["o1"][:], po1[:, 0:128], AF.Identity, scale=rsc[:, d * 4:d * 4 + 1], R=[r_po1, r_rc], W=[r2])
                    P.ts("dve", rs["kpp"][:], ktok[:, c, :], rsc[:, d * 4 + 1:d * 4 + 2], None, ALU.mult, R=[r_ktok[c], r_rc], W=[r2])
                    ps_, r_ps = PS()
                    P.mm(ps_[:, 0:128], rs["kpp"][:], vtok[:, c, :], True, True, R=[r2, r_vtok[c]], W=[r_ps])
                    P.stt("dve", Sf[d][:], Sf[d][:], rsc[:, d * 4 + 2:d * 4 + 3], ps_[:, 0:128], ALU.mult, ALU.add, R=[r_S[d], r_ps, r_rc], W=[r_S[d]])
                    P.cp("act", Sb[d][:], Sf[d][:], R=[r_S[d]], W=[r_S[d]])
                    yield
                    if full:
                        pp, r_pp = PS()
                        P.mm(pp[:, 0:128], kTr[:, c, :], qT[:, oc * 128:(oc + 1) * 128], True, True, R=[r_kTr[c], r_qT], W=[r_pp])
                        P.tt("dve", rs["qkm"][:], pp[:, 0:128], DrT[d][:], ALU.mult, R=[r_pp, r_rc], W=[r2])
                        po2, r_po2 = PS()
                        P.mm(po2[:, 0:128], rs["qkm"][:], vtok[:, c, :], True, True, R=[r2, r_vtok[c]], W=[r_po2])
                        P.tt("dve", rs["o1"][:], rs["o1"][:], po2[:, 0:128], ALU.add, R=[r2, r_po2], W=[r2])
                        P.tt("pool", oacc[:, oc, :], oacc[:, oc, :], rs["o1"][:], ALU.add, R=[r2, r_oacc[oc]], W=[r_oacc[oc]])
                        yield

                def ret_batch_pre():
                    lgR = lgc[:, 4 + h:5 + h]
                    P.act(z2[:, 0:16], csmf[:, 8:24], AF.Exp, scale=lgR, R=RC + [r_lg], W=[r_kzb])
                    P.act(z2[:, 16:17], csmf[:, 5:6], AF.Exp, scale=lgR, R=RC + [r_lg], W=[r_kzb])
                    P.tt("pool", kzb[:], ktok[:, OWN1:NCH, :], z2[:, 0:16].unsqueeze(2).to_broadcast([128, 16, 128]), ALU.mult,
                         R=r_ktok[OWN1:NCH] + [r_kzb], W=[r_kzb])

                def ret_batch():
                    pb_, r_pb_ = PS()
                    for m in range(16):
                        P.mm(pb_[:, 0:128], kzb[:, m, :], vtok[:, OWN1 + m, :], m == 0, m == 15, R=[r_kzb, r_vtok[OWN1 + m]], W=[r_pb_])
                    P.stt("dve", Sf[1][:], Sf[1][:], z2[:, 16:17], pb_[:, 0:128], ALU.mult, ALU.add, R=[r_S[1], r_pb_, r_kzb], W=[r_S[1]])
                    P.cp("act", Sb[1][:], Sf[1][:], R=[r_S[1]], W=[r_S[1]])
                    yield

                ret_batch_pre()
                gF = chain([ret_step(0, c, (2 * i) % NR) for i, c in enumerate(chF)])
                gR = chain([ret_step(1, c, 1) for c in (1, 0)] + [ret_batch()]
                           + [ret_step(1, c, (2 * i + 1) % NR) for i, c in enumerate(range(OWN1 - 1, OWN0 - 1, -1))])
                interleave([gR, delayed(gF, 1)])
                for oc in range(NOWN):
                    P.act(junk[:, 0:128], oacc[:, oc, :], AF.Square, accum=ross[:, 1, oc:oc + 1], R=[r_oacc[oc]], W=[r_junk, r_ross])
                    P.act(junk[:, 128:256], oacc[:, oc, :], AF.Identity, accum=ross[:, 0, oc:oc + 1], R=[r_oacc[oc]], W=[r_junk, r_ross])
                P.ts("dve", ross[:, 0, :], ross[:, 0, :], 1.0 / HD, None, ALU.mult, R=[r_ross], W=[r_ross])
                P.tt("dve", ross[:, 2, :], ross[:, 0, :], ross[:, 0, :], ALU.mult, R=[r_ross], W=[r_ross])
                P.stt("dve", ross[:, 1, :], ross[:, 1, :], 1.0 / HD, ross[:, 2, :], ALU.mult, ALU.subtract, R=[r_ross], W=[r_ross])
                P.act(ross[:, 3, :], ross[:, 1, :], AF.Ln, bias=EPS, R=[r_ross], W=[r_ross])
                P.act(ross[:, 3, :], ross[:, 3, :], AF.Exp, scale=-0.5, R=[r_ross], W=[r_ross])
                for oc in range(NOWN):
                    i_ = oc % 2
                    ys_, r_ys_ = (ysc, r_ysc) if i_ == 0 else (ysc2e, r_ysc2e)
                    P.ts("dve", junkE[:, 2 * i_, :], oacc[:, oc, :], ross[:, 0, oc:oc + 1], ross[:, 3, oc:oc + 1], ALU.subtract, ALU.mult,
                         R=[r_oacc[oc], r_ross], W=[r_jkE[i_]])
                    P.tt("pool", junkE[:, 2 * i_ + 1, :], junkE[:, 2 * i_, :], rowf[:, RO_RN:RO_RN + 128], ALU.mult, R=[r_jkE[i_]] + RC, W=[r_jkE[i_]])
                    P.tt("pool", ys_[:], junkE[:, 2 * i_ + 1, :], zs[:, oc, :], ALU.mult, R=[r_jkE[i_], r_zs], W=[r_ys_])
                    pt, r_pt = PS(); ptb = pt[:, 0:64].bitcast(BF16)
                    P.tr(ptb, ys_[:], idb[:], R=[r_ys_, r_c2], W=[r_pt])
                    P.cp("act", yT[:, 4 + h, oc * 128:(oc + 1) * 128], ptb, R=[r_pt], W=[r_yT[4 + h]])
            if dbg:
                dbg_out["yT"] = nc.dram_tensor("d_yT", [128, 8, 2048], BF16, kind="ExternalOutput").ap()
                P.out_dma("sp", dbg_out["yT"][:, :, :], yT[:], R=r_yT)
            P.barrier()

        MK('F')
        with contextlib.ExitStack() as es:
            def SB(name, shape, dt):
                return es.enter_context(nc.sbuf_tensor(name, list(shape), dt))
            wo = SB("wo", [128, 8, DM], BF16); r_wo = R_()
            fgrow = SB("fgrow", [128, DM], F32); r_fg = R_()
            P.dma("sp", fgrow[:], rowc[:, RO_FG:RO_FG + DM], W=[r_fg])
            wfo = SB("wfo", [128, 22, DM], BF16); r_wfo = R_()
            wov = w_out.rearrange("(k p) n -> p k n", p=128); wfov = w_fo.rearrange("(k p) n -> p k n", p=128)
            wfiv = w_fi.rearrange("(k p) n -> p k n", p=128)
            wst = [SB("wst%d" % i, [128, 8, 512], BF16) for i in range(2)]; r_wst = [R_(), R_()]
            x1_ = SB("x1_", [128, 4, DM], F32); x1 = [x1_, x1_]; r_x1_ = R_(); r_x1 = [r_x1_, r_x1_]
            h2T_ = SB("h2T", [128, 8, 512], BF16); h2T = [h2T_, h2T_]; r_h2_ = R_(); r_h2 = [r_h2_, r_h2_]
            aT = SB("aT", [128, 22, 512], BF16); r_aT = R_()
            xob = [SB("xob%d" % i, [128, DM], F32) for i in range(2)]; r_xob = [R_(), R_()]
            ntmp = [(SB("fsq%d" % i, [128, DM], BF16), SB("fss%d" % i, [128, 4], F32), SB("fxn%d" % i, [128, DM], BF16), R_()) for i in range(2)]
            sg = [SB("sg%d" % i, [128, 4, 512], BF16) for i in range(2)]; r_sg = [R_(), R_()]
            ob = [SB("ob%d" % i, [128, DM], F32) for i in range(2)]; r_ob = [R_(), R_()]
            for k in range(8):
                i = k % 2
                P.dma("sp", ob[i][:], wov[:, k, :], W=[r_ob[i]])
                P.tt("dve" if i else "pool", wo[:, k, :], ob[i][:], g1row[:], ALU.mult, R=[r_ob[i], r_grow], W=[r_wo])
            nst = 0
            for grp in range(4):
                gi = grp % 2
                for bb in range(4):
                    blk = grp * 4 + bb
                    i = blk % 2
                    P.dma("sp", xob[i][:], xo[blk * 128:(blk + 1) * 128, :], W=[r_xob[i]])
                    for hf in range(2):
                        pp, r_pp = PS()
                        for k in range(8):
                            P.mm(pp[:, :], yT[:, k, blk * 128:(blk + 1) * 128], wo[:, k, hf * 512:(hf + 1) * 512], k == 0, k == 7,
                                 R=[r_yT[k], r_wo], W=[r_pp])
                        P.tt("dve", x1[gi][:, bb, hf * 512:(hf + 1) * 512], pp[:, :], xob[i][:, hf * 512:(hf + 1) * 512], ALU.add,
                             R=[r_pp, r_xob[i]], W=[r_x1[gi]])
                    norm_block(x1[gi][:, bb, :], r_x1[gi], h2T[gi], r_h2[gi], bb * 128,
                               lambda k: sc2[:, k, 0:1], lambda k: sc2[:, k, 1:2], ntmp[i])
                def wfo_chunk(k):
                    i = k % 2
                    P.dma("sp", ob[i][:], wfov[:, k, :], W=[r_ob[i]])
                    P.tt("dve", wfo[:, k, :], ob[i][:], g2row[:], ALU.mult, R=[r_ob[i], r_grow], W=[r_wfo])
                for j0 in range(0, 22, 4):
                    nj = min(4, 22 - j0)
                    si = (j0 // 4) % 2
                    wi = nst % 2
                    nst += 1
                    P.dma("pool", wst[wi][:, :, 0:nj * 128], wfiv[:, :, j0 * 128:(j0 + nj) * 128], W=[r_wst[wi]])
                    for jj in range(nj):
                        pg_, r_pg_ = PS()
                        for k in range(8):
                            P.mm(pg_[:, :], wst[wi][:, k, jj * 128:(jj + 1) * 128], h2T[gi][:, k, :], k == 0, k == 7, R=[r_wst[wi], r_h2[gi]], W=[r_pg_])
                        P.act(sg[si][:, jj, :], pg_[:, :], AF.Silu, R=[r_pg_], W=[r_sg[si]])
                    wi = nst % 2
                    nst += 1
                    P.dma("pool", wst[wi][:, :, 0:nj * 128], wfiv[:, :, DFF + j0 * 128:DFF + (j0 + nj) * 128], W=[r_wst[wi]])
                    for jj in range(nj):
                        pu, r_pu = PS()
                        for k in range(8):
                            P.mm(pu[:, :], wst[wi][:, k, jj * 128:(jj + 1) * 128], h2T[gi][:, k, :], k == 0, k == 7, R=[r_wst[wi], r_h2[gi]], W=[r_pu])
                        P.tt("dve", aT[:, j0 + jj, :], pu[:, :], sg[si][:, jj, :], ALU.mult, R=[r_pu, r_sg[si]], W=[r_aT])
                        if grp == 0:
                            wfo_chunk(j0 + jj)
                for bb in range(4):
                    blk = grp * 4 + bb
                    oi = blk % 2
                    for hf in range(2):
                        pp, r_pp = PS()
                        for j in range(22):
                            P.mm(pp[:, :], aT[:, j, bb * 128:(bb + 1) * 128], wfo[:, j, hf * 512:(hf + 1) * 512], j == 0, j == 21,
                                 R=[r_aT, r_wfo], W=[r_pp])
                        P.tt("dve", x1[gi][:, bb, hf * 512:(hf + 1) * 512], pp[:, :], x1[gi][:, bb, hf * 512:(hf + 1) * 512], ALU.add,
                             R=[r_pp, r_x1[gi]], W=[r_x1[gi]])
                    sq, ss, xn, r_t = ntmp[oi]
                    P.act(sq[:], x1[gi][:, bb, :], AF.Square, accum=ss[:, 0:1], R=[r_x1[gi]], W=[r_t])
                    P.act(ss[:, 1:2], ss[:, 0:1], AF.Ln, bias=EPS, scale=1.0 / DM, R=[r_t], W=[r_t])
                    P.act(ss[:, 2:3], ss[:, 1:2], AF.Exp, scale=-0.5, R=[r_t], W=[r_t])
                    P.stt("dve", ob[oi][:], x1[gi][:, bb, :], ss[:, 2:3], fgrow[:], ALU.mult, ALU.mult,
                          R=[r_x1[gi], r_t, r_fg], W=[r_ob[oi]])
                    P.out_dma("sp", out[blk * 128:(blk + 1) * 128, :], ob[oi][:], R=[r_ob[oi]])
            MK('end')
            P.finish()
    return nc, dbg_out


def _flipseq(a):
    return np.concatenate([a[:CTX][::-1], a[CTX:][::-1]], axis=0)


def _rope_tables():
    rows = LAT // 64
    row = np.repeat(np.arange(rows, dtype=np.float32), 64)
    col = np.tile(np.arange(64, dtype=np.float32), rows)
    z = np.zeros(CTX, np.float32)
    p_seq = np.concatenate([np.arange(CTX, dtype=np.float32), np.full(LAT, float(CTX), np.float32)])
    p_row = np.concatenate([z, row]); p_col = np.concatenate([z, col])

    def aa(pos, n):
        inv = (np.float32(10000.0) ** (-np.arange(n, dtype=np.float32) / np.float32(n))).astype(np.float32)
        return pos[:, None] * inv[None, :]
    ang = np.concatenate([aa(p_seq, 16), aa(p_row, 24), aa(p_col, 24)], axis=-1).astype(np.float32)
    return np.cos(ang).astype(np.float32), np.sin(ang).astype(np.float32)


def _consts():
    i = np.arange(C)
    blocks = [np.eye(C, dtype=np.float32)]
    MIs, MSs, LMs = [], [], []
    for d in range(2):
        incl = (i[:, None] <= i[None, :]) if d == 0 else (i[:, None] >= i[None, :])
        MIs.append(incl.astype(np.float32))
    for d in range(2):
        st = (i[None, :] < i[:, None]) if d == 0 else (i[None, :] > i[:, None])
        MSs.append(st.astype(np.float32))
    for d in range(2):
        for l in range(1, 8):
            b = 2 ** (l - 1)
            blk = (i[:, None] // (2 * b)) == (i[None, :] // (2 * b))
            ih = (i // b) % 2
            m = blk & (ih[:, None] == 1) & (ih[None, :] == 0)
            if d == 1:
                m = m.T
            LMs.append(m.astype(np.float32))
    dist = np.abs(i[:, None] - i[None, :]).astype(np.float32)
    blocks += MIs + MSs + LMs + [dist, dist, np.ones((C, C), np.float32)]
    cst = np.concatenate(blocks, axis=1)
    pos = i.astype(np.float32)
    csm = np.zeros((128, 24), np.float32)
    csm[:, 0] = pos + 1.0
    csm[:, 1] = (C - 1 - pos) + 1.0
    csm[:, 2] = C - 1.0 - pos
    csm[:, 3] = pos
    csm[:, 4] = float(C)
    csm[:, 5] = float(C * 16)
    for m in range(16):
        csm[:, 8 + m] = pos + float(C * m)
    return np.ascontiguousarray(cst), csm


def _core_inputs(inp, b, p, shared):
    f = np.float32
    xs = np.concatenate([inp["ctx"][b], inp["x"][b]], axis=0)
    if p == 1:
        xs = _flipseq(xs)
    xs = np.ascontiguousarray(xs, dtype=f)
    pi = [p, 1 - p]
    w_in = inp["w_in"][0]
    ab0 = 2048
    a_cols = [w_in[:, ab0 + 4 * k: ab0 + 4 * k + 4] for k in range(4)]
    w_ab = np.concatenate([a_cols[pi[0]], a_cols[pi[1]], a_cols[2 + pi[0]], a_cols[2 + pi[1]]], axis=1)
    cw = inp["conv_w"][0]
    if p == 1:
        cw = cw[::-1]
    convT = np.ascontiguousarray(cw.reshape(5, 12, 128).transpose(2, 1, 0), dtype=f)
    cvec = np.concatenate([inp["c"][b].reshape(8, 128).T, inp["c_ctx"].reshape(8, 128).T], axis=1)
    rowc = np.zeros((NRO,), f)
    rowc[RO_GN:RO_GN + 128] = inp["gdn_norm_g"][0]
    rowc[RO_RN:RO_RN + 128] = inp["ret_norm_g"][0]
    rowc[RO_FG:RO_FG + DM] = inp["final_g"]
    for d in range(2):
        rowc[RO_AL + d * 4:RO_AL + d * 4 + 4] = inp["gdn_a_log"][0][pi[d]]
        rowc[RO_DTB + d * 4:RO_DTB + d * 4 + 4] = inp["gdn_dt_bias"][0][pi[d]]
        rowc[RO_RL + d * 4:RO_RL + d * 4 + 4] = inp["ret_decay_logit"][0][pi[d]]
    rowc[RO_B1:RO_B1 + DM] = inp["ada_b"][0][2 * DM:3 * DM]
    rowc[RO_B2:RO_B2 + DM] = inp["ada_b"][0][5 * DM:6 * DM]
    rowc = np.ascontiguousarray(np.broadcast_to(rowc[None, :], (128, NRO)))
    cos, sin = shared["rope"]
    if p == 1:
        cos, sin = _flipseq(cos), _flipseq(sin)
    cs = np.concatenate([cos, sin], axis=1).reshape(NCH, 128, 128).transpose(1, 0, 2)
    return {
        "xs": xs, "xo": np.ascontiguousarray(xs[CTX:CTX + 2048]),
        "cvec": np.ascontiguousarray(cvec, dtype=f), "ada_w": shared["ada_w"], "adabT": shared["adabT"],
        "gmixT": shared["gmixT"], "gffnT": shared["gffnT"], "w_in": shared["w_in"],
        "w_ab": np.ascontiguousarray(w_ab, dtype=f), "convT": convT, "rowc": rowc,
        "cs": np.ascontiguousarray(cs, dtype=f), "cst": shared["cst"], "csm": shared["csm"],
        "w_out": shared["w_out"], "w_fi": shared["w_fi"], "w_fo": shared["w_fo"],
    }


def _shared(inp):
    f = np.float32
    cst, csm = _consts()
    return {
        "rope": _rope_tables(), "cst": cst, "csm": csm,
        "ada_w": np.ascontiguousarray(inp["ada_w"][0], dtype=f),
        "adabT": np.ascontiguousarray(inp["ada_b"][0].reshape(48, 128).T, dtype=f),
        "gmixT": np.ascontiguousarray(inp["norm_mix_g"][0].reshape(8, 128).T, dtype=f),
        "gffnT": np.ascontiguousarray(inp["norm_ffn_g"][0].reshape(8, 128).T, dtype=f),
        "w_in": np.ascontiguousarray(inp["w_in"][0], dtype=f), "w_out": np.ascontiguousarray(inp["w_out"][0], dtype=f),
        "w_fi": np.ascontiguousarray(inp["w_ffn_in"][0], dtype=f), "w_fo": np.ascontiguousarray(inp["w_ffn_out"][0], dtype=f),
    }


def kernel(**inputs):
    inp = {k: np.asarray(v) for k, v in inputs.items()}
    sh = _shared(inp)
    in_maps = [_core_inputs(inp, core // 2, core % 2, sh) for core in range(8)]
    nc, _ = build(False)
    res = run_bass_kernel_spmd(nc, in_maps, core_ids=list(range(8)))
    outp = np.zeros((4, LAT, DM), np.float32)
    for core in range(8):
        b, p = core // 2, core % 2
        o = np.asarray(res.results[core]["out"], dtype=np.float32)
        if p == 0:
            outp[b, :2048] = o
        else:
            outp[b, 2048:] = o[::-1]
    return outp
```
